# Optimizing a Trainium2 kernel written in Bass

```python
import jax, jax.numpy as jnp
from jax import lax
import numpy as np

D_MODEL = 2048
BATCH = 4
SEQ = 2048
DEPTH = 1
DEC_BATCH = 128
DEC_SEQ = 1
PAST_LEN = 16384
PAGE_SIZE = 128

D_RNN = D_MODEL // 2
N_RNN_BLOCKS = 8
RNN_BLOCK = D_RNN // N_RNN_BLOCKS
CONV_W = 4
LRU_C = 8.0
D_POOL = D_MODEL // 2
POOL_WINDOWS = (2, 4, 8, 16)
N_POOL_GROUPS = len(POOL_WINDOWS)
POOL_GROUP = D_POOL // N_POOL_GROUPS
POOL_HIST = max(POOL_WINDOWS) - 1
N_MEM = 256
N_XHEADS = 4
XHEAD_DIM = D_MODEL // 8
D_X = N_XHEADS * XHEAD_DIM
N_BRANCH = 3
D_MIX = D_RNN + D_POOL + D_X
D_IN = 2 * D_MIX + N_BRANCH * D_MODEL
EPS = 1e-6

kernel_name = "hybrid_rglru_pool_memxattn_step"


def rmsnorm(x, g):
    xf = x.astype(jnp.float32)
    y = xf * lax.rsqrt(jnp.mean(xf * xf, axis=-1, keepdims=True) + EPS)
    return (y * g.astype(jnp.float32)).astype(x.dtype)


def causal_conv(x, buf, w, b):
    L = x.shape[1]
    ext = jnp.concatenate([buf.astype(x.dtype), x], axis=1)
    out = b + sum(w[k] * ext[:, k:k + L] for k in range(CONV_W))
    return out, ext[:, -(CONV_W - 1):]


def rg_lru(x, h0, w_a, b_a, w_x, b_x, lam):
    B, L, _ = x.shape
    xb = x.reshape(B, L, N_RNN_BLOCKS, RNN_BLOCK)
    r = jax.nn.sigmoid((jnp.einsum('blnd,nde->blne', xb, w_a).reshape(B, L, D_RNN) + b_a).astype(jnp.float32))
    i = jax.nn.sigmoid((jnp.einsum('blnd,nde->blne', xb, w_x).reshape(B, L, D_RNN) + b_x).astype(jnp.float32))
    log_a = -LRU_C * r * jax.nn.softplus(-lam.astype(jnp.float32))
    a = jnp.exp(log_a)
    mult = jnp.sqrt(-jnp.expm1(2.0 * log_a))
    bterm = mult * i * x.astype(jnp.float32)
    bterm = bterm.at[:, 0].add(a[:, 0] * h0.astype(jnp.float32))

    def combine(lhs, rhs):
        a1, b1 = lhs
        a2, b2 = rhs
        return a1 * a2, a2 * b1 + b2

    _, h = lax.associative_scan(combine, (a, bterm), axis=1)
    return h.astype(x.dtype), h[:, -1].astype(x.dtype)


def pool_mix(x, hist, pos, w_pool, pool_scale):
    L = x.shape[1]
    ext = jnp.concatenate([hist.astype(jnp.float32), x.astype(jnp.float32)], axis=1)
    cs = jnp.concatenate([jnp.zeros_like(ext[:, :1]), jnp.cumsum(ext, axis=1)], axis=1)
    end = cs[:, POOL_HIST + 1:POOL_HIST + 1 + L]
    xf = x.astype(jnp.float32)
    outs = []
    for g, w in enumerate(POOL_WINDOWS):
        c0, c1 = g * POOL_GROUP, (g + 1) * POOL_GROUP
        start = cs[:, POOL_HIST + 1 - w:POOL_HIST + 1 - w + L, c0:c1]
        cnt = jnp.minimum(pos + 1, w).astype(jnp.float32)[None, :, None]
        d = (end[..., c0:c1] - start) / cnt - xf[..., c0:c1]
        outs.append(jnp.einsum('bld,de->ble', d, w_pool[g].astype(jnp.float32)))
    out = jnp.concatenate(outs, axis=-1) * pool_scale.astype(jnp.float32)
    return out.astype(x.dtype), ext[:, -POOL_HIST:].astype(x.dtype)


def mem_kv(mem, g_mem, w_kv):
    B, M, _ = mem.shape
    kv = rmsnorm(mem, g_mem) @ w_kv
    k, v = jnp.split(kv, 2, axis=-1)
    return k.reshape(B, M, N_XHEADS, XHEAD_DIM), v.reshape(B, M, N_XHEADS, XHEAD_DIM)


def cross_attn(q, k, v):
    B, L = q.shape[:2]
    s = jnp.einsum('blhd,bmhd->bhlm', q, k).astype(jnp.float32) * (XHEAD_DIM ** -0.5)
    p = jax.nn.softmax(s, axis=-1)
    o = jnp.einsum('bhlm,bmhd->blhd', p.astype(v.dtype), v)
    return o.reshape(B, L, D_X)


def layer(x, pos, conv_buf, h0, pool_hist, mem_k, mem_v, g_pre, w_in, conv_w, conv_b,
          w_rg_a, b_rg_a, w_rg_x, b_rg_x, lru_lambda, w_pool, pool_scale, w_branch, w_out, g_post):
    B, L, _ = x.shape
    u = rmsnorm(x, g_pre)
    z = u @ w_in
    cuts = [D_RNN, 2 * D_RNN, 2 * D_RNN + D_POOL, 2 * D_RNN + 2 * D_POOL,
            2 * D_RNN + 2 * D_POOL + D_X, 2 * D_MIX]
    xr, gr, xp, gp, q, gx, gates = jnp.split(z, cuts, axis=-1)
    xr_c, new_conv = causal_conv(xr, conv_buf, conv_w, conv_b)
    h, h_last = rg_lru(xr_c, h0, w_rg_a, b_rg_a, w_rg_x, b_rg_x, lru_lambda)
    o_r = h * jax.nn.silu(gr)
    o_p, new_hist = pool_mix(xp, pool_hist, pos, w_pool, pool_scale)
    o_p = o_p * jax.nn.silu(gp)
    o_x = cross_attn(q.reshape(B, L, N_XHEADS, XHEAD_DIM), mem_k, mem_v) * jax.nn.silu(gx)
    y_r = o_r @ w_branch[:D_RNN]
    y_p = o_p @ w_branch[D_RNN:D_RNN + D_POOL]
    y_x = o_x @ w_branch[D_RNN + D_POOL:]
    gs = jax.nn.sigmoid(gates.astype(jnp.float32)).reshape(B, L, N_BRANCH, D_MODEL).astype(x.dtype)
    merged = gs[:, :, 0] * y_r + gs[:, :, 1] * y_p + gs[:, :, 2] * y_x
    out = merged @ w_out
    return x + rmsnorm(out, g_post), new_conv, h_last, new_hist


def setup_inputs(seed: int = 0) -> dict:
    key = jax.random.key(seed)
    ks = jax.random.split(key, 32)
    f32 = jnp.float32

    def nrm(k, shape, scale):
        return jax.random.normal(k, shape, f32) * scale

    a0 = jax.random.uniform(ks[12], (DEPTH, D_RNN), f32, 0.9, 0.999) ** (1.0 / LRU_C)
    return {
        'x_prompt': nrm(ks[0], (BATCH, SEQ, D_MODEL), 1.0),
        'x_sample': nrm(ks[1], (DEC_BATCH, DEC_SEQ, D_MODEL), 1.0),
        'mem_prompt': nrm(ks[2], (BATCH, N_MEM, D_MODEL), 1.0),
        'state_rglru_h': nrm(ks[3], (DEPTH, DEC_BATCH, D_RNN), 0.5),
        'state_conv': nrm(ks[4], (DEPTH, DEC_BATCH, CONV_W - 1, D_RNN), 1.0),
        'state_pool': nrm(ks[5], (DEPTH, DEC_BATCH, POOL_HIST, D_POOL), 1.0),
        'cache_mem_k': nrm(ks[6], (DEPTH, DEC_BATCH, N_MEM, N_XHEADS, XHEAD_DIM), 1.0),
        'cache_mem_v': nrm(ks[7], (DEPTH, DEC_BATCH, N_MEM, N_XHEADS, XHEAD_DIM), 1.0),
        'g_pre': 1.0 + nrm(ks[8], (DEPTH, D_MODEL), 0.05),
        'w_in': nrm(ks[9], (DEPTH, D_MODEL, D_IN), D_MODEL ** -0.5),
        'conv_w': nrm(ks[10], (DEPTH, CONV_W, D_RNN), CONV_W ** -0.5),
        'conv_b': nrm(ks[11], (DEPTH, D_RNN), 0.02),
        'w_rg_a': nrm(ks[13], (DEPTH, N_RNN_BLOCKS, RNN_BLOCK, RNN_BLOCK), RNN_BLOCK ** -0.5),
        'b_rg_a': nrm(ks[14], (DEPTH, D_RNN), 0.02),
        'w_rg_x': nrm(ks[15], (DEPTH, N_RNN_BLOCKS, RNN_BLOCK, RNN_BLOCK), RNN_BLOCK ** -0.5),
        'b_rg_x': nrm(ks[16], (DEPTH, D_RNN), 0.02),
        'lru_lambda': jnp.log(a0) - jnp.log1p(-a0),
        'w_pool': nrm(ks[17], (DEPTH, N_POOL_GROUPS, POOL_GROUP, POOL_GROUP), POOL_GROUP ** -0.5),
        'pool_scale': 1.0 + nrm(ks[18], (DEPTH, D_POOL), 0.05),
        'g_mem': 1.0 + nrm(ks[19], (DEPTH, D_MODEL), 0.05),
        'w_kv': nrm(ks[20], (DEPTH, D_MODEL, 2 * D_X), D_MODEL ** -0.5),
        'w_branch': nrm(ks[21], (DEPTH, D_MIX, D_MODEL), (D_MIX // N_BRANCH) ** -0.5),
        'w_out': nrm(ks[22], (DEPTH, D_MODEL, D_MODEL), D_MODEL ** -0.5),
        'g_post': 1.0 + nrm(ks[23], (DEPTH, D_MODEL), 0.05),
    }


def reference(x_prompt, x_sample, mem_prompt, state_rglru_h, state_conv, state_pool,
              cache_mem_k, cache_mem_v, g_pre, w_in, conv_w, conv_b, w_rg_a, b_rg_a,
              w_rg_x, b_rg_x, lru_lambda, w_pool, pool_scale, g_mem, w_kv, w_branch,
              w_out, g_post):
    B, S, _ = x_prompt.shape
    pos_p = jnp.arange(S, dtype=jnp.int32)
    pos_s = PAST_LEN + jnp.arange(x_sample.shape[1], dtype=jnp.int32)
    yp, ys = x_prompt, x_sample
    hp_l, cp_l, pp_l, mk_l, mv_l, hs_l, cs_l, ps_l = [], [], [], [], [], [], [], []
    for l in range(DEPTH):
        lw = (g_pre[l], w_in[l], conv_w[l], conv_b[l], w_rg_a[l], b_rg_a[l], w_rg_x[l],
              b_rg_x[l], lru_lambda[l], w_pool[l], pool_scale[l], w_branch[l], w_out[l], g_post[l])
        mk, mv = mem_kv(mem_prompt, g_mem[l], w_kv[l])
        zc = jnp.zeros((B, CONV_W - 1, D_RNN), yp.dtype)
        zh = jnp.zeros((B, D_RNN), yp.dtype)
        zp = jnp.zeros((B, POOL_HIST, D_POOL), yp.dtype)
        yp, c_p, h_p, p_p = layer(yp, pos_p, zc, zh, zp, mk, mv, *lw)
        ys, c_s, h_s, p_s = layer(ys, pos_s, state_conv[l], state_rglru_h[l], state_pool[l],
                                  cache_mem_k[l], cache_mem_v[l], *lw)
        hp_l.append(h_p); cp_l.append(c_p); pp_l.append(p_p); mk_l.append(mk); mv_l.append(mv)
        hs_l.append(h_s); cs_l.append(c_s); ps_l.append(p_s)
    new_h_prompt = jnp.stack(hp_l)
    new_conv_prompt = jnp.stack(cp_l)
    new_pool_prompt = jnp.stack(pp_l)
    mem_k_prompt = jnp.stack(mk_l)
    mem_v_prompt = jnp.stack(mv_l)
    new_h_sample = jnp.stack(hs_l)
    new_conv_sample = jnp.stack(cs_l)
    new_pool_sample = jnp.stack(ps_l)
    return (yp, ys, new_h_prompt, new_conv_prompt, new_pool_prompt, mem_k_prompt, mem_v_prompt,
            new_h_sample, new_conv_sample, new_pool_sample)
```

```python
import numpy as np
from contextlib import ExitStack
import concourse.bass as bass
import concourse.mybir as mybir
from concourse.bass_utils import run_bass_kernel_spmd

F32 = mybir.dt.float32
BF16 = mybir.dt.bfloat16
AF = mybir.ActivationFunctionType
ALU = mybir.AluOpType
AX = mybir.AxisListType

ENGS = ("pe", "act", "dve", "pool", "sp")


class Buf:
    __slots__ = ("w", "r", "name")

    def __init__(self, name=""):
        self.w = None
        self.r = []
        self.name = name


class Prog:
    def __init__(self, nc, n_dma_sems=32):
        self.nc = nc
        self.ops = {e: [] for e in ENGS}
        self.cnt = {e: 0 for e in ENGS}
        self.waited = {e: {} for e in ENGS}
        self.rings = {"sp": 24, "pool": 24, "act": 8}
        self.dma_val = {q: [0] * n for q, n in self.rings.items()}
        self.dma_next = {q: 0 for q in self.rings}
        self.sems = {}

    @staticmethod
    def _flat(x):
        out = []
        for b in x:
            if isinstance(b, (list, tuple)):
                out.extend(Prog._flat(b))
            else:
                out.append(b)
        return out

    def _collect(self, eng, reads, writes, after, safe):
        reads, writes, after = self._flat(reads), self._flat(writes), self._flat(after)
        deps = set()
        for b in reads:
            if b.w is not None:
                k, v = b.w
                if k == eng:
                    if eng == "pe" or safe or self.cnt[eng] - v >= 3:
                        continue
                deps.add(b.w)
        for b in list(writes) + list(after):
            if b.w is not None and b.w[0] != eng:
                deps.add(b.w)
            for c in b.r:
                if c[0] != eng:
                    deps.add(c)
        wd = self.waited[eng]
        best = {}
        for (k, v) in deps:
            if wd.get(k, 0) >= v:
                continue
            if best.get(k, 0) < v:
                best[k] = v
        out = []
        for k, v in best.items():
            wd[k] = v
            out.append((k, v))
        return out

    def _finish(self, comp, reads, writes):
        reads, writes = self._flat(reads), self._flat(writes)
        for b in reads:
            b.r.append(comp)
        for b in writes:
            b.w = comp
            b.r = []

    def op(self, eng, fn, reads=(), writes=(), after=(), safe=False):
        waits = self._collect(eng, reads, writes, after, safe)
        self.cnt[eng] += 1
        comp = (eng, self.cnt[eng])
        self.ops[eng].append((fn, waits, (eng, 1)))
        self._finish(comp, reads, writes)
        return comp

    def dma(self, eng, fn, reads=(), writes=(), after=()):
        i = self.dma_next[eng]
        self.dma_next[eng] = (i + 1) % self.rings[eng]
        key = "d%s%d" % (eng, i)
        waits = self._collect(eng, reads, writes, after, False)
        prev = self.dma_val[eng][i]
        if prev > 0 and self.waited[eng].get(key, 0) < prev:
            self.waited[eng][key] = prev
            waits.append((key, prev))
        self.dma_val[eng][i] += 16
        comp = (key, self.dma_val[eng][i])
        self.ops[eng].append((fn, waits, (key, 16)))
        self._finish(comp, reads, writes)
        return comp

    def wait_all(self, eng, bufs):
        waits = self._collect(eng, bufs, (), (), False)
        self.ops[eng].append((None, waits, None))

    def emit(self):
        nc = self.nc
        with ExitStack() as es:
            for e in ENGS:
                self.sems[e] = es.enter_context(nc.semaphore("s_" + e))
            for q, n in self.rings.items():
                for i in range(n):
                    self.sems["d%s%d" % (q, i)] = es.enter_context(nc.semaphore("s_d%s%d" % (q, i)))
            block = es.enter_context(nc.Block())

            def run(h, name):
                for (fn, waits, inc) in self.ops[name]:
                    for (k, v) in waits:
                        h.wait_ge(self.sems[k], v)
                    if fn is None:
                        continue
                    ins = fn(h)
                    if inc is not None:
                        ins.then_inc(self.sems[inc[0]], inc[1])

            @block.tensor
            def _(e):
                run(e, "pe")

            @block.scalar
            def _(e):
                run(e, "act")

            @block.vector
            def _(e):
                run(e, "dve")

            @block.gpsimd
            def _(e):
                run(e, "pool")

            @block.sync
            def _(e):
                run(e, "sp")


D = 2048
TP, TS, T = 1024, 16, 1040
TILES = [(0, 347), (347, 347), (694, 346)]
QT = [(0, 512), (512, 512)]
SW = 1056
NSLOT = 6
EPS = 1e-6
POOLW = (2, 4, 8, 16)


def build_program(dbg=False):
    nc = bass.Bass("TRN2", target_bir_lowering=False)
    dbg_t = nc.dram_tensor("dbg", [128, 12, 1040], F32, kind="ExternalOutput").ap() if dbg else None

    def din(name, shape):
        return nc.dram_tensor(name, shape, F32, kind="ExternalInput").ap()

    def dout(name, shape):
        return nc.dram_tensor(name, shape, F32, kind="ExternalOutput").ap()

    xp = din("xp", [TP, D]); xq = din("xq", [TP, D]); xs = din("xs", [TS, D]); mem = din("mem", [256, D])
    st_h = din("st_h", [TS, 1024]); st_conv = din("st_conv", [TS, 3, 1024]); st_pool = din("st_pool", [TS, 15, 1024])
    ck = din("ck", [TS, 256, 1024]); cv = din("cv", [TS, 256, 1024])
    g_pre = din("g_pre", [D]); w_in = din("w_in", [D, 12288]); conv_w = din("conv_w", [4, 1024]); conv_b = din("conv_b", [1024])
    w_rg_a = din("w_rg_a", [8, 128, 128]); b_rg_a = din("b_rg_a", [1024]); w_rg_x = din("w_rg_x", [8, 128, 128]); b_rg_x = din("b_rg_x", [1024])
    lam = din("lam", [1024]); w_pool = din("w_pool", [4, 256, 256]); pool_scale = din("pool_scale", [1024]); g_mem = din("g_mem", [D])
    w_kv = din("w_kv", [D, 2048]); w_branch = din("w_branch", [3072, D]); w_out = din("w_out", [D, D]); g_post = din("g_post", [D])
    flag = din("flag", [128, 1]); icnt = din("icnt", [128, 4, 16])

    y_p = dout("y_p", [TP, D]); y_s = dout("y_s", [TS, D]); h_p = dout("h_p", [1024]); conv_p = dout("conv_p", [3, 1024])
    pool_p = dout("pool_p", [15, 1024]); mk = dout("mk", [256, 1024]); mv = dout("mv", [256, 1024])
    h_s = dout("h_s", [TS, 1024]); conv_s = dout("conv_s", [TS, 3, 1024]); pool_s = dout("pool_s", [TS, 15, 1024])

    es = ExitStack()
    with es:
        NW = 51100
        arena = es.enter_context(nc.sbuf_tensor("arena", [128, NW], F32))
        off = [0]

        def carve(nwords):
            a = arena[:, off[0]:off[0] + nwords]
            off[0] += nwords
            assert off[0] <= NW, off[0]
            return a

        uT_w = carve(8320); uT = uT_w.bitcast(BF16).rearrange("p (c t) -> p c t", c=16)
        regB_w = carve(8320); regB = regB_w.bitcast(BF16)
        merged = regB.rearrange("p (c t) -> p c t", c=16)
        qT = regB[:, 0:8 * T].rearrange("p (c t) -> p c t", c=8)
        pT_all = regB[:, 8 * T:8 * T + 8192].rearrange("p (mc h t) -> p mc h t", mc=2, h=4)
        umT = regB[:, 0:4096].rearrange("p (c t) -> p c t", c=16)
        regC_w = carve(12480); regC = regC_w.bitcast(BF16)
        o_all = regC.rearrange("p (c t) -> p c t", c=24)
        uqT = regC[:, 8 * T:8 * T + 16384].rearrange("p (c t) -> p c t", c=16)
        arena_D = carve(8192)
        wsl_w = [arena_D[:, 0:4096], arena_D[:, 4096:8192]]
        wsl = [w.bitcast(BF16) for w in wsl_w]
        Fw = carve(NSLOT * SW)
        Fs = [Fw[:, i * SW:(i + 1) * SW] for i in range(NSLOT)]
        xstage = [Fw[:, 0:2048], Fw[:, 2 * SW:2 * SW + 2048]]
        kvf = [regB_w[:, 2048:4096], regB_w[:, 4096:6144]]
        kst = xstage[0].rearrange("p (m f) -> p m f", m=2)
        vst = xstage[1].rearrange("p (m f) -> p m f", m=2)
        hist = regC_w[:, 768:2688].rearrange("p (r k) -> p r k", r=15)
        osb = Fw[:, 4 * SW:4 * SW + 2048]
        KT_w = carve(1024)
        KT = KT_w.bitcast(BF16).rearrange("p (c m) -> p c m", c=8)
        Vb = carve(1024).bitcast(BF16).rearrange("p (m f) -> p m f", m=2)
        gpost_bc = regC_w[:, 0:2048]
        ident = carve(128)
        consts = carve(104)
        gpre_fm = consts[:, 0:16]; gmem_fm = consts[:, 16:32]
        convw_fm = consts[:, 32:64].rearrange("p (k c) -> p k c", k=4)
        convb_fm = consts[:, 64:72]; ba_fm = consts[:, 72:80]; bx_fm = consts[:, 80:88]; cch = consts[:, 88:96]; pscale_fm = consts[:, 96:104]
        cstage = regC_w[:, 640:768]
        dg = [carve(128) for _ in range(3)]
        ssq3 = carve(12)
        flag_t = carve(1); icnt_t = carve(64).rearrange("p (g t) -> p g t", g=4)
        wrga = carve(512).bitcast(BF16).rearrange("p (n e) -> p n e", n=8)
        wrgx = carve(512).bitcast(BF16).rearrange("p (n e) -> p n e", n=8)
        wpl = carve(1024).bitcast(BF16).rearrange("p (g dc e) -> p g dc e", g=4, dc=2)
        uqtail = carve(128).bitcast(BF16).rearrange("p (c t) -> p c t", c=16)
        sm = carve(16)
        ssq = carve(8)
        cst = regC_w[:, 0:384].rearrange("p (r k) -> p r k", r=3)
        cst_fm = carve(384).rearrange("p (r c b) -> p r c b", r=3, c=8)
        h0st = regC_w[:, 384:512]; h0_fm = carve(128).rearrange("p (c b) -> p c b", c=8)
        hsum = regC_w[:, 512:640]; hsum_fm = carve(128).rearrange("p (c b) -> p c b", c=8)
        h0p = carve(8); qtail = carve(24).rearrange("p (c r) -> p c r", c=8)
        tailx = carve(24).rearrange("p (c r) -> p c r", c=8); tailp = carve(120).rearrange("p (c r) -> p c r", c=8)
        hlast = carve(8)
        xsT_in = carve(128).rearrange("p (c b) -> p c b", c=8); xpsT_in = carve(128).rearrange("p (c b) -> p c b", c=8)
        hsT_in = carve(128).rearrange("p (c b) -> p c b", c=8)
        outT, outT2, outT3, outT4, outT5, outT6 = [regC_w[:, 2048 + 128 * i:2176 + 128 * i] for i in range(6)]
        qexp = regC_w[:, 8320:9344].bitcast(BF16).rearrange("p (h dc b j) -> p h dc b j", h=4, dc=2, b=16)
        epj = carve(512)
        ep = [epj[:, 0:256], epj[:, 256:512]]
        ps_s = regC_w[:, 9344:10368].rearrange("p (h m) -> p h m", h=4)
        pTs = carve(128).rearrange("p (mc h b) -> p mc h b", mc=2, h=4)
        os_fm = carve(128).rearrange("p (c b) -> p c b", c=8)
        junk = epj
        small15 = carve(16)

        psum = es.enter_context(nc.psum_tensor("psum", [128, 8, 512], F32))

        P = Prog(nc)
        B_uT = Buf("uT"); B_regB = Buf("regB"); B_qT = [Buf() for _ in range(8)]; B_pT = Buf("pT")
        B_o = [Buf("o%d" % i) for i in range(24)]
        B_uq = Buf("uqT")
        B_q = [Buf("q%d" % i) for i in range(4)]
        B_ws = [[B_q[0], B_q[1]], [B_q[2], B_q[3]]]
        B_F = [Buf("F%d" % i) for i in range(NSLOT)]
        B_ps = [Buf("ps%d" % i) for i in range(8)]
        B_c = {}

        def bc(name):
            if name not in B_c:
                B_c[name] = Buf(name)
            return B_c[name]

        B_out = Buf("outs")
        out_bufs = []
        rot = [0]

        def bank():
            b = rot[0]
            rot[0] = (b + 1) % 6
            return b

        aux = [0]

        def auxbank():
            b = 6 + aux[0]
            aux[0] = 1 - aux[0]
            return b

        def out_dma(dst, src, reads):
            b = Buf()
            P.dma("sp", lambda e: e.dma_start(out=dst, in_=src), reads=reads, writes=[b])
            out_bufs.append(b)

        def dump(idx, src, bufs, ncol=1040):
            if dbg_t is not None:
                out_dma(dbg_t[:, idx, 0:ncol], src, bufs)

        def load(eng, dst, src, wb, slow=False, after=()):
            if slow:
                P.dma(eng, lambda e: e.dma_start(out=dst, in_=src, allow_slow_non_contiguous=True), writes=wb, after=after)
            else:
                P.dma(eng, lambda e: e.dma_start(out=dst, in_=src), writes=wb, after=after)

        wt_state = {"n": 0}

        def wtile(src_fn, after=()):
            s = wt_state["n"] % 2
            wt_state["n"] += 1
            for (dst, src) in src_fn(wsl[s]):
                P.dma("pool", lambda e, dst=dst, src=src: e.dma_start(out=dst, in_=src), writes=[B_ws[s]], after=after)
            return s

        def win_tile(t):
            def f(slot):
                v = slot.rearrange("p (kc n) -> p kc n", kc=16)
                return [(v, w_in[:, t * 512:(t + 1) * 512].rearrange("(kc p) n -> p kc n", p=128))]
            return wtile(f)

        def wview(s):
            return wsl[s].rearrange("p (kc n) -> p kc n", kc=16)

        P.op("dve", lambda e: e.memset(ident, 0.0), writes=[bc("ident")])
        P.op("pool", lambda e: e.affine_select(out=ident, in_=ident, pattern=[[-1, 128]], compare_op=ALU.not_equal, fill=1.0, base=0, channel_multiplier=1),
             reads=[bc("ident")], writes=[bc("ident")])
        crow = [(0, 16, g_pre.rearrange("(c p) -> c p", p=128)), (16, 16, g_mem.rearrange("(c p) -> c p", p=128)),
                (32, 32, conv_w.rearrange("k (c p) -> (k c) p", p=128)), (64, 8, conv_b.rearrange("(c p) -> c p", p=128)),
                (72, 8, b_rg_a.rearrange("(c p) -> c p", p=128)), (80, 8, b_rg_x.rearrange("(c p) -> c p", p=128)),
                (88, 8, lam.rearrange("(c p) -> c p", p=128)), (96, 8, pool_scale.rearrange("(c p) -> c p", p=128))]
        for (r0, nr, src) in crow:
            load("act", cstage[r0:r0 + nr, :], src, [bc("cstage")])
        bkc = auxbank()
        P.op("pe", lambda e: e.transpose(out=psum[:, bkc, 0:104], in_=cstage[0:104, :], identity=ident[0:104, 0:104]), reads=[bc("cstage"), bc("ident")], writes=[B_ps[bkc]])
        P.op("dve", lambda e: e.tensor_copy(out=consts, in_=psum[:, bkc, 0:104]), reads=[B_ps[bkc]],
             writes=[bc(nm) for nm in ("gpre", "gmem", "convw", "convb", "ba", "bx", "cch", "pscale")])
        load("act", flag_t, flag, [bc("flag")])
        load("act", icnt_t, icnt, [bc("icnt")])
        P.dma("pool", lambda e: e.dma_start(out=wrga, in_=w_rg_a.rearrange("n d e -> d n e")), writes=[bc("wrga")])
        P.dma("pool", lambda e: e.dma_start(out=wrgx, in_=w_rg_x.rearrange("n d e -> d n e")), writes=[bc("wrgx")])
        P.dma("pool", lambda e: e.dma_start(out=wpl, in_=w_pool.rearrange("g (dc p) e -> p g dc e", p=128)), writes=[bc("wpl")])

        B_hist = [bc("hist")]
        for c in range(8):
            load("pool", cst[16 * c:16 * c + 16, :, :], st_conv[:, :, c * 128:(c + 1) * 128], [bc("cst")])
            load("pool", h0st[16 * c:16 * c + 16, :], st_h[:, c * 128:(c + 1) * 128], [bc("h0st")])
            load("pool", hist[16 * c:16 * c + 16, :, :], st_pool[:, :, c * 128:(c + 1) * 128], B_hist)
        nb = [0]

        xst3 = [xstage[0], xstage[1], Fw[:, 4 * SW:4 * SW + 2048]]
        BXS3 = [[B_F[0], B_F[1]], [B_F[2], B_F[3]], [B_F[4], B_F[5]]]

        def norm_parts(src_rows, n, dstT_fn, g_fm, gname, dstbuf, extra_after=()):
            i = nb[0] % 3
            nb[0] += 1
            xst = xst3[i]
            bx = BXS3[i]
            rstd = sm[:, i:i + 1]
            rb = bc("rstd%d" % i)
            sq = ssq3[:, 4 * i:4 * i + 4]
            sqb = bc("ssq%d" % i)
            dgi = dg[i]
            dgb = bc("dg%d" % i)

            def front():
                load("sp", xst[0:n, :], src_rows, bx)
                junk_bf = junk.bitcast(BF16)
                for j in range(2):
                    P.op("act", lambda e, j=j: e.activation(out=junk_bf[0:n, :], in_=xst[0:n, j * 1024:(j + 1) * 1024], func=AF.Square, accum_out=sq[0:n, j:j + 1]),
                         reads=bx, writes=[bc("junk"), sqb])
                P.op("dve", lambda e: e.tensor_reduce(out=rstd[0:n], in_=sq[0:n, 0:2], axis=AX.X, op=ALU.add), reads=[sqb], writes=[rb])
                P.op("dve", lambda e: e.tensor_scalar(out=rstd[0:n], in0=rstd[0:n], scalar1=1.0 / D, scalar2=EPS, op0=ALU.mult, op1=ALU.add), reads=[rb], writes=[rb])
                P.op("act", lambda e: e.activation(out=rstd[0:n], in_=rstd[0:n], func=AF.Sqrt), reads=[rb], writes=[rb])
                P.op("dve", lambda e: e.reciprocal(out=rstd[0:n], in_=rstd[0:n]), reads=[rb], writes=[rb])
                P.op("act", lambda e: e.activation(out=xst[0:n, :], in_=xst[0:n, :], func=AF.Copy, scale=rstd[0:n]), reads=bx + [rb], writes=bx)

            def back():
                for grp in range(4):
                    bk = auxbank()

                    def tr(e, grp=grp, bk=bk):
                        ins = None
                        for j in range(4):
                            c = grp * 4 + j
                            ins = e.transpose(out=psum[:, bk, j * 128:j * 128 + n], in_=xst[0:n, c * 128:(c + 1) * 128], identity=ident[0:n, 0:n])
                        return ins
                    P.op("pe", tr, reads=bx + [bc("ident")], writes=[B_ps[bk]])
                    c0 = grp * 4
                    P.op("dve", lambda e, c0=c0, bk=bk: e.tensor_tensor(out=dstT_fn(slice(c0, c0 + 4)), in0=psum[:, bk, :].rearrange("p (j m) -> p j m", j=4)[:, :, 0:n],
                                                                     in1=g_fm[:, c0:c0 + 4].unsqueeze(2).to_broadcast([128, 4, n]), op=ALU.mult),
                         reads=[B_ps[bk], bc(gname)], writes=[dstbuf], after=extra_after)
            return front, back

        def norm_many(arglist):
            parts = [norm_parts(*a) for a in arglist]
            for k in range(len(parts) + 1):
                if k < len(parts):
                    parts[k][0]()
                if k > 0:
                    parts[k - 1][1]()

        def norm_block(*a, **kw):
            norm_many([a])

        norm_many([(mem[mb * 128:(mb + 1) * 128, :], 128, (lambda c, mb=mb: umT[:, c, mb * 128:(mb + 1) * 128]), gmem_fm, "gmem", B_regB) for mb in range(2)])
        B_uqd = Buf("uq_done")
        nl = [(xq[tb * 128:(tb + 1) * 128, :], 128, (lambda c, tb=tb: uqT[:, c, tb * 128:(tb + 1) * 128]), gpre_fm, "gpre", [B_uq, B_uqd]) for tb in range(8)]
        nl += [(xp[tb * 128:(tb + 1) * 128, :], 128, (lambda c, tb=tb: uT[:, c, tb * 128:(tb + 1) * 128]), gpre_fm, "gpre", B_uT) for tb in range(8)]
        nl += [(xs, 16, (lambda c: uT[:, c, 1024:1040]), gpre_fm, "gpre", B_uT)]
        norm_many(nl)
        P.op("dve", lambda e: e.tensor_copy(out=uqtail, in_=uqT[:, :, 1008:1024]), reads=[B_uq], writes=[bc("uqtail")])

        bk = auxbank()

        def tr_c(e, bk=bk):
            ins = None
            for r in range(3):
                ins = e.transpose(out=psum[:, bk, r * 128:(r + 1) * 128], in_=cst[:, r, :], identity=ident)
            return ins
        P.op("pe", tr_c, reads=[bc("cst"), bc("ident")], writes=[B_ps[bk]])
        P.op("dve", lambda e, bk=bk: e.tensor_copy(out=cst_fm.rearrange("p r c b -> p (r c b)"), in_=psum[:, bk, 0:384]), reads=[B_ps[bk]], writes=[bc("cst_fm")])
        bk = auxbank()
        P.op("pe", lambda e, bk=bk: e.transpose(out=psum[:, bk, 0:128], in_=h0st, identity=ident), reads=[bc("h0st"), bc("ident")], writes=[B_ps[bk]])
        P.op("dve", lambda e, bk=bk: e.tensor_copy(out=h0_fm.rearrange("p c b -> p (c b)"), in_=psum[:, bk, 0:128]), reads=[B_ps[bk]], writes=[bc("h0_fm")])
        for g in range(4):
            w = POOLW[g]
            if w == 2:
                P.op("dve", lambda e, g=g: e.tensor_copy(out=hsum[32 * g:32 * g + 32, :], in_=hist[32 * g:32 * g + 32, 14, :]), reads=B_hist, writes=[bc("hsum")])
            else:
                P.op("dve", lambda e, g=g, w=w: e.tensor_reduce(out=hsum[32 * g:32 * g + 32, :], in_=hist[32 * g:32 * g + 32, 16 - w:15, :].rearrange("p r k -> p k r"), axis=AX.X, op=ALU.add),
                     reads=B_hist, writes=[bc("hsum")])
        bk = auxbank()
        P.op("pe", lambda e, bk=bk: e.transpose(out=psum[:, bk, 0:128], in_=hsum, identity=ident), reads=[bc("hsum"), bc("ident")], writes=[B_ps[bk]])
        P.op("dve", lambda e, bk=bk: e.tensor_copy(out=hsum_fm.rearrange("p c b -> p (c b)"), in_=psum[:, bk, 0:128]), reads=[B_ps[bk]], writes=[bc("hsum_fm")])

        B_kvf = [Buf("kvf0"), Buf("kvf1")]
        for ct in range(4):
            s = wtile(lambda slot, ct=ct: [(slot.rearrange("p (kc n) -> p kc n", kc=16), w_kv[:, ct * 512:(ct + 1) * 512].rearrange("(kc p) n -> p kc n", p=128))],
                      after=([B_uqd] if ct < 2 else []))
            wv = wview(s)
            for mb in range(2):
                bk = bank()

                def mm(e, mb=mb, bk=bk, wv=wv):
                    ins = None
                    for kc in range(16):
                        ins = e.matmul(psum[:, bk, :], lhsT=umT[:, kc, mb * 128:(mb + 1) * 128], rhs=wv[:, kc, :], start=(kc == 0), stop=(kc == 15))
                    return ins
                P.op("pe", mm, reads=[B_regB, B_ws[s]], writes=[B_ps[bk]])
                P.op("act", lambda e, mb=mb, bk=bk, ct=ct: e.activation(out=kvf[mb][:, ct * 512:(ct + 1) * 512], in_=psum[:, bk, :], func=AF.Copy),
                     reads=[B_ps[bk]], writes=B_kvf)
        for mb in range(2):
            out_dma(mk[mb * 128:(mb + 1) * 128, :], kvf[mb][:, 0:1024], B_kvf)
            out_dma(mv[mb * 128:(mb + 1) * 128, :], kvf[mb][:, 1024:2048], B_kvf)
            P.op("dve", lambda e, mb=mb: e.tensor_copy(out=Vb[:, mb, :], in_=kvf[mb][:, 1024:2048]), reads=B_kvf, writes=[bc("Vb")])
            for grp in range(2):
                bk = auxbank()

                def tr(e, mb=mb, grp=grp, bk=bk):
                    ins = None
                    for j in range(4):
                        c = grp * 4 + j
                        ins = e.transpose(out=psum[:, bk, j * 128:(j + 1) * 128], in_=kvf[mb][:, c * 128:(c + 1) * 128], identity=ident)
                    return ins
                P.op("pe", tr, reads=B_kvf + [bc("ident")], writes=[B_ps[bk]])
                P.op("act", lambda e, mb=mb, grp=grp, bk=bk: e.activation(out=KT[:, grp * 4:(grp + 1) * 4, mb * 128:(mb + 1) * 128],
                                                                            in_=psum[:, bk, :].rearrange("p (j m) -> p j m", j=4), func=AF.Copy),
                     reads=[B_ps[bk]], writes=[bc("KT")])

        Fs2 = [regB_w[:, i * SW:(i + 1) * SW] for i in range(NSLOT)]
        B_F2 = [Buf("G%d" % i) for i in range(NSLOT)]
        SETS = [(Fs, B_F), (Fs2, B_F2)]
        Dq_w = [arena_D[:, k * 2048:(k + 1) * 2048] for k in range(4)]
        Dq = [w.bitcast(BF16) for w in Dq_w]
        qfree = [0, 1, 2, 3]

        def qalloc():
            return qfree.pop(0)

        def qrelease(k):
            qfree.append(k)

        P.op("act", lambda e: e.activation(out=cch, in_=cch, func=AF.Exp, scale=-1.0), reads=[bc("cch")], writes=[bc("cch")])
        P.op("act", lambda e: e.activation(out=cch, in_=cch, func=AF.Ln, bias=1.0), reads=[bc("cch")], writes=[bc("cch")])
        P.op("dve", lambda e: e.tensor_scalar(out=cch, in0=cch, scalar1=-8.0, scalar2=None, op0=ALU.mult), reads=[bc("cch")], writes=[bc("cch")])
        hba = carve(8); hbx = carve(8); hcch = carve(8)
        P.op("dve", lambda e: e.tensor_scalar(out=hba, in0=ba_fm, scalar1=0.5, scalar2=None, op0=ALU.mult), reads=[bc("ba")], writes=[bc("hba")])
        P.op("dve", lambda e: e.tensor_scalar(out=hbx, in0=bx_fm, scalar1=0.5, scalar2=None, op0=ALU.mult), reads=[bc("bx")], writes=[bc("hbx")])
        P.op("dve", lambda e: e.tensor_scalar(out=hcch, in0=cch, scalar1=0.5, scalar2=None, op0=ALU.mult), reads=[bc("cch")], writes=[bc("hcch")])

        def ucols(kc, st, sz):
            return uT[:, kc, st:st + sz]

        pend_banks = set()

        def zmm(wv, wbufs, ucols_fn, tiles, ubuf, pend=None):
            res = []
            for (st, sz) in tiles:
                bk = bank()
                if pend is not None:
                    assert bk not in pend, "PSUM bank %d re-claimed before its evacuation was queued" % bk
                    pend.add(bk)

                def mm(e, bk=bk, st=st, sz=sz):
                    ins = None
                    for kc in range(16):
                        ins = e.matmul(psum[:, bk, 0:sz], lhsT=wv[:, kc, :], rhs=ucols_fn(kc, st, sz), start=(kc == 0), stop=(kc == 15))
                    return ins
                P.op("pe", mm, reads=wbufs + [ubuf], writes=[B_ps[bk]])
                res.append((bk, st, sz))
            return res

        def z_mm(s, j, ucols_fn, tiles, ubuf):
            wv = wview(s)
            return zmm(wv[:, :, j * 128:(j + 1) * 128], [B_ws[s]], ucols_fn, tiles, ubuf)

        BPQ = [[[Buf() for _ in range(2)] for _ in range(NSLOT)] for _ in range(2)]
        BPP = [[[Buf() for _ in range(3)] for _ in range(NSLOT)] for _ in range(2)]

        def make_rg(n, is_q, ci):
            si = ci % 2
            Sl, Bl = SETS[si]
            first_after = ([B_regB] + B_kvf) if si == 1 else []
            BP = BPQ[si] if is_q else BPP[si]
            pre = list(Bl) + first_after + ([] if is_q else [b for sl in BPQ[si] for b in sl])
            st8 = {}

            def load():
                if len(qfree) < 1:
                    return False
                k = qalloc()
                st8["k"] = k
                v = Dq[k].rearrange("p (a kc n) -> p a kc n", a=2, kc=16)
                P.dma("pool", lambda e: e.dma_start(out=v[:, 0], in_=w_in[:, n * 128:(n + 1) * 128].rearrange("(kc p) n -> p kc n", p=128)), writes=[B_q[k]])
                if not is_q:
                    P.dma("pool", lambda e: e.dma_start(out=v[:, 1], in_=w_in[:, 1024 + n * 128:1024 + (n + 1) * 128].rearrange("(kc p) n -> p kc n", p=128)), writes=[B_q[k]])
                return True

            def gen():
                k = st8["k"]
                v = Dq[k].rearrange("p (a kc n) -> p a kc n", a=2, kc=16)
                X, C, CB, R, I, M = Sl[0], Sl[1], Sl[2].bitcast(BF16), Sl[3], Sl[4], Sl[5]
                BX, BC, BCB, BR, BI, BM = BP
                parts = QT if is_q else TILES
                NP = len(parts)
                LP = 1024
                seq = [(st, min(st + sz, LP)) for (st, sz) in parts]
                if is_q:
                    xb = zmm(v[:, 0], [B_q[k]], lambda kc, st, sz: uqT[:, kc, st:st + sz], QT, B_uq, pend_banks)
                else:
                    xb = zmm(v[:, 0], [B_q[k]], ucols, TILES, B_uT, pend_banks)
                yield
                gb = [] if is_q else zmm(v[:, 1], [B_q[k]], ucols, TILES, B_uT, pend_banks)
                qrelease(k)
                yield
                if is_q:
                    P.op("dve", lambda e: e.memset(X[:, 0:3], 0.0), writes=[BX[0]], after=pre)
                else:
                    P.op("dve", lambda e: e.tensor_copy(out=X[:, 0:3], in_=qtail[:, n, :]), reads=[bc("qtail")], writes=[BX[0]], after=pre)
                for p, (bk, st, sz) in enumerate(xb):
                    P.op("act", lambda e, bk=bk, st=st, sz=sz: e.activation(out=X[:, 3 + st:3 + st + sz], in_=psum[:, bk, 0:sz], func=AF.Copy), reads=[B_ps[bk]], writes=[BX[p]], after=pre)
                    pend_banks.discard(bk)
                yield
                for p, (st, sz) in enumerate(parts):
                    P.op("dve", lambda e, st=st, sz=sz: e.tensor_scalar(out=C[:, st:st + sz], in0=X[:, 3 + st:3 + st + sz], scalar1=convw_fm[:, 3, n:n + 1], scalar2=convb_fm[:, n:n + 1], op0=ALU.mult, op1=ALU.add),
                         reads=[BX[p], bc("convw"), bc("convb")], writes=[BC[p]], after=pre)
                yield
                for kk in range(3):
                    for p, (st, en) in enumerate(seq):
                        rd = [BX[p], BC[p], bc("convw")] + ([BX[p - 1]] if p > 0 else [])
                        P.op("dve", lambda e, kk=kk, st=st, en=en: e.scalar_tensor_tensor(out=C[:, st:en], in0=X[:, st + kk:en + kk], scalar=convw_fm[:, kk, n:n + 1], in1=C[:, st:en], op0=ALU.mult, op1=ALU.add),
                             reads=rd, writes=[BC[p]], safe=True)
                    yield
                lastp = NP - 1
                if is_q:
                    P.op("dve", lambda e: e.tensor_copy(out=qtail[:, n, :], in_=X[:, 3 + 1021:3 + 1024]), reads=[BX[lastp]], writes=[bc("qtail")])
                else:
                    P.op("dve", lambda e: e.tensor_copy(out=tailx[:, n, :], in_=X[:, 3 + 1021:3 + 1024]), reads=[BX[lastp]], writes=[bc("tailx")])
                    P.op("dve", lambda e: e.tensor_copy(out=xsT_in[:, n, :], in_=X[:, 3 + 1024:3 + 1040]), reads=[BX[lastp]], writes=[bc("xsT_in")])
                    for kk in range(3):
                        P.op("dve", lambda e, kk=kk: e.scalar_tensor_tensor(out=C[:, 1024:1040], in0=cst_fm[:, kk, n, :], scalar=convw_fm[:, kk, n:n + 1], in1=C[:, 1024:1040], op0=ALU.mult, op1=ALU.add),
                             reads=[bc("cst_fm"), BC[lastp], bc("convw")], writes=[BC[lastp]])
                yield
                for p, (st, sz) in enumerate(parts):
                    P.op("act", lambda e, st=st, sz=sz: e.activation(out=CB[:, st:st + sz], in_=C[:, st:st + sz], func=AF.Copy), reads=[BC[p]], writes=[BCB[p]], after=pre)
                for p, (bk, st, sz) in enumerate(gb):
                    ex = [BX[p + 1]] if p + 1 < NP else []
                    P.op("act", lambda e, bk=bk, st=st, sz=sz: e.activation(out=X[:, 3 + st:3 + st + sz], in_=psum[:, bk, 0:sz], func=AF.Silu), reads=[B_ps[bk]], writes=[BX[p]], after=ex)
                    pend_banks.discard(bk)
                yield
                for (wg, hb, bname, dst, bdst, wname) in ((wrga, hba, "hba", R, BR, "wrga"), (wrgx, hbx, "hbx", I, BI, "wrgx")):
                    for p, (st, sz) in enumerate(parts):
                        bk = auxbank()
                        P.op("pe", lambda e, bk=bk, st=st, sz=sz, wg=wg: e.matmul(psum[:, bk, 0:sz], lhsT=wg[:, n, :], rhs=CB[:, st:st + sz], start=True, stop=True),
                             reads=[BCB[p], bc(wname)], writes=[B_ps[bk]])
                        P.op("act", lambda e, bk=bk, st=st, sz=sz, dst=dst, hb=hb: e.activation(out=dst[:, st:st + sz], in_=psum[:, bk, 0:sz], func=AF.Tanh, bias=hb[:, n:n + 1], scale=0.5),
                             reads=[B_ps[bk], bc(bname)], writes=[bdst[p]], after=pre)
                yield
                for p, (st, sz) in enumerate(parts):
                    P.op("act", lambda e, st=st, sz=sz: e.activation(out=M[:, st:st + sz], in_=R[:, st:st + sz], func=AF.Exp, scale=cch[:, n:n + 1], bias=cch[:, n:n + 1]), reads=[BR[p], bc("cch")], writes=[BM[p]], after=pre)
                for p, (st, sz) in enumerate(parts):
                    P.op("act", lambda e, st=st, sz=sz: e.activation(out=R[:, st:st + sz], in_=R[:, st:st + sz], func=AF.Exp, scale=hcch[:, n:n + 1], bias=hcch[:, n:n + 1]), reads=[BR[p], bc("hcch")], writes=[BR[p]])
                yield
                for p, (st, sz) in enumerate(parts):
                    P.op("act", lambda e, st=st, sz=sz: e.activation(out=M[:, st:st + sz], in_=M[:, st:st + sz], func=AF.Sqrt, scale=-1.0, bias=1.0), reads=[BM[p]], writes=[BM[p]])
                yield
                for p, (st, sz) in enumerate(parts):
                    P.op("dve", lambda e, st=st, sz=sz: e.scalar_tensor_tensor(out=M[:, st:st + sz], in0=I[:, st:st + sz], scalar=1.0, in1=M[:, st:st + sz], op0=ALU.add, op1=ALU.mult), reads=[BM[p], BI[p]], writes=[BM[p]])
                yield
                for p, (st, sz) in enumerate(parts):
                    P.op("dve", lambda e, st=st, sz=sz: e.scalar_tensor_tensor(out=M[:, st:st + sz], in0=M[:, st:st + sz], scalar=0.5, in1=C[:, st:st + sz], op0=ALU.mult, op1=ALU.mult), reads=[BM[p], BC[p]], writes=[BM[p]])
                yield
                for p, (st, en) in enumerate(seq):
                    if p == 0:
                        init = 0.0 if is_q else h0p[:, n:n + 1]
                        rd = [BR[p], BM[p]] + ([] if is_q else [bc("h0p")])
                    else:
                        init = C[:, st - 1:st]
                        rd = [BR[p], BM[p], BC[p - 1]]
                    P.op("dve", lambda e, st=st, en=en, init=init: e.tensor_tensor_scan(out=C[:, st:en], data0=R[:, st:en], data1=M[:, st:en], initial=init, op0=ALU.mult, op1=ALU.add),
                         reads=rd, writes=[BC[p]])
                if is_q:
                    P.op("dve", lambda e: e.tensor_scalar(out=h0p[:, n:n + 1], in0=C[:, LP - 1:LP], scalar1=flag_t[:, 0:1], scalar2=None, op0=ALU.mult), reads=[BC[lastp], bc("flag")], writes=[bc("h0p")])
                else:
                    P.op("dve", lambda e: e.tensor_tensor(out=C[:, 1024:1040], in0=R[:, 1024:1040], in1=h0_fm[:, n, :], op=ALU.mult), reads=[BR[lastp], bc("h0_fm")], writes=[BC[lastp]])
                    P.op("dve", lambda e: e.tensor_tensor(out=C[:, 1024:1040], in0=C[:, 1024:1040], in1=M[:, 1024:1040], op=ALU.add), reads=[BC[lastp], BM[lastp]], writes=[BC[lastp]])
                    P.op("dve", lambda e: e.tensor_copy(out=hlast[:, n:n + 1], in_=C[:, 1023:1024]), reads=[BC[lastp]], writes=[bc("hlast")])
                    P.op("dve", lambda e: e.tensor_copy(out=hsT_in[:, n, :], in_=C[:, 1024:1040]), reads=[BC[lastp]], writes=[bc("hsT_in")])
                    yield
                    if n == 0:
                        dump(0, C[:, 0:1040], list(BC)); dump(1, X[:, 3:3 + 1040], list(BX)); dump(2, R[:, 0:1040], list(BR)); dump(3, M[:, 0:1040], list(BM))
                    for p, (st, sz) in enumerate(parts):
                        P.op("dve", lambda e, st=st, sz=sz: e.tensor_tensor(out=o_all[:, n, st:st + sz], in0=C[:, st:st + sz], in1=X[:, 3 + st:3 + st + sz], op=ALU.mult), reads=[BX[p], BC[p]], writes=[B_o[n]],
                             after=[bc("cst"), bc("h0st"), bc("hsum"), bc("cstage"), bc("hist")])
            return (load, gen)

        def rg_fence(si):
            allp = [b for sl in BPQ[si] for b in sl] + [b for sl in BPP[si] for b in sl]
            P.op("dve", lambda e: e.memset(small15[:, 1:2], 0.0), reads=allp, writes=[bc("small15")] + list(SETS[si][1]))

        def make_pool(g, ci):
            Sl, Bl = SETS[ci % 2]
            first_after = ([B_regB] + B_kvf) if ci % 2 == 1 else []
            w = POOLW[g]
            st8 = {}

            def load():
                if len(qfree) < 2:
                    return False
                kx = qalloc(); kg = qalloc()
                st8["kx"], st8["kg"] = kx, kg
                vx = Dq[kx].rearrange("p (kc n) -> p kc n", kc=16)
                vg = Dq[kg].rearrange("p (kc n) -> p kc n", kc=16)
                P.dma("pool", lambda e: e.dma_start(out=vx, in_=w_in[:, 2048 + g * 256:2048 + (g + 1) * 256].rearrange("(kc p) n -> p kc n", p=128)), writes=[B_q[kx]])
                P.dma("pool", lambda e: e.dma_start(out=vg, in_=w_in[:, 3072 + g * 256:3072 + (g + 1) * 256].rearrange("(kc p) n -> p kc n", p=128)), writes=[B_q[kg]])
                return True

            def gen():
                kx, kg = st8["kx"], st8["kg"]
                vx = Dq[kx].rearrange("p (kc n) -> p kc n", kc=16)
                vg = Dq[kg].rearrange("p (kc n) -> p kc n", kc=16)
                X, SA, SBt, G = Sl[0], Sl[1], Sl[2], Sl[4]
                DB = Sl[3].bitcast(BF16)
                BX, BSA, BSB, BDB, BG = Bl[0], Bl[1], Bl[2], Bl[3], Bl[4]
                E = 15 + 1024
                for eo in range(2):
                    c = 2 * g + eo
                    wvx = vx[:, :, eo * 128:(eo + 1) * 128]
                    banks = zmm(wvx, [B_q[kx]], ucols, TILES, B_uT)
                    bkh = auxbank()

                    def mmh(e, bkh=bkh, wvx=wvx):
                        ins = None
                        for kc in range(16):
                            ins = e.matmul(psum[:, bkh, 0:16], lhsT=wvx[:, kc, :], rhs=uqtail[:, kc, :], start=(kc == 0), stop=(kc == 15))
                        return ins
                    P.op("pe", mmh, reads=[B_q[kx], bc("uqtail")], writes=[B_ps[bkh]])
                    if eo == 1:
                        qrelease(kx)
                    yield
                    P.op("act", lambda e, bkh=bkh: e.activation(out=X[:, 0:15], in_=psum[:, bkh, 1:16], func=AF.Copy), reads=[B_ps[bkh]], writes=[BX], after=first_after)
                    for (bk, st, sz) in banks:
                        P.op("act", lambda e, bk=bk, st=st, sz=sz: e.activation(out=X[:, 15 + st:15 + st + sz], in_=psum[:, bk, 0:sz], func=AF.Copy), reads=[B_ps[bk]], writes=[BX])
                    yield
                    P.op("dve", lambda e, c=c: e.tensor_copy(out=tailp[:, c, :], in_=X[:, 15 + 1009:15 + 1024]), reads=[BX], writes=[bc("tailp")])
                    P.op("dve", lambda e, c=c: e.tensor_copy(out=xpsT_in[:, c, :], in_=X[:, 15 + 1024:15 + 1040]), reads=[BX], writes=[bc("xpsT_in")])
                    P.op("dve", lambda e: e.tensor_tensor(out=SA[:, 1:E], in0=X[:, 1:E], in1=X[:, 0:E - 1], op=ALU.add), reads=[BX], writes=[BSA], after=first_after)
                    yield
                    cur, curb, oth, othb = SA, BSA, SBt, BSB
                    sh = 2
                    lo = 1
                    while sh < w:
                        lo2 = lo + sh
                        P.op("dve", lambda e, cur=cur, oth=oth, lo2=lo2, sh=sh: e.tensor_tensor(out=oth[:, lo2:E], in0=cur[:, lo2:E], in1=cur[:, lo2 - sh:E - sh], op=ALU.add),
                             reads=[curb], writes=[othb], after=first_after)
                        cur, curb, oth, othb = oth, othb, cur, curb
                        lo = lo2
                        sh *= 2
                        yield
                    dcol = eo * SW
                    P.op("dve", lambda e, cur=cur, dcol=dcol: e.scalar_tensor_tensor(out=DB[:, dcol:dcol + 1024], in0=cur[:, 15:E], scalar=1.0 / w, in1=X[:, 15:E], op0=ALU.mult, op1=ALU.subtract),
                         reads=[curb, BX], writes=[BDB], after=first_after)
                    P.op("dve", lambda e, cur=cur: e.tensor_tensor(out=small15[:, 0:15], in0=cur[:, 15:30], in1=icnt_t[:, g, 0:15], op=ALU.mult), reads=[curb, bc("icnt")], writes=[bc("small15")])
                    P.op("dve", lambda e, dcol=dcol: e.tensor_tensor(out=DB[:, dcol:dcol + 15], in0=small15[:, 0:15], in1=X[:, 15:30], op=ALU.subtract), reads=[bc("small15"), BX], writes=[BDB])
                    P.op("dve", lambda e, c=c: e.tensor_tensor(out=small15[:, 0:16], in0=hsum_fm[:, c, :], in1=X[:, E:E + 16], op=ALU.add), reads=[bc("hsum_fm"), BX], writes=[bc("small15")])
                    P.op("dve", lambda e, dcol=dcol: e.scalar_tensor_tensor(out=DB[:, dcol + 1024:dcol + 1040], in0=small15[:, 0:16], scalar=1.0 / w, in1=X[:, E:E + 16], op0=ALU.mult, op1=ALU.subtract),
                         reads=[bc("small15"), BX], writes=[BDB])
                    yield
                for eo in range(2):
                    c = 2 * g + eo
                    gbanks = zmm(vg[:, :, eo * 128:(eo + 1) * 128], [B_q[kg]], ucols, TILES, B_uT)
                    if eo == 1:
                        qrelease(kg)
                    yield
                    for (bk, st, sz) in gbanks:
                        P.op("act", lambda e, bk=bk, st=st, sz=sz: e.activation(out=G[:, st:st + sz], in_=psum[:, bk, 0:sz], func=AF.Silu), reads=[B_ps[bk]], writes=[BG], after=first_after)
                    yield
                    for (st, sz) in TILES:
                        bk = bank()

                        def mmp(e, bk=bk, st=st, sz=sz, eo=eo):
                            ins = None
                            for dc in range(2):
                                ins = e.matmul(psum[:, bk, 0:sz], lhsT=wpl[:, g, dc, eo * 128:(eo + 1) * 128], rhs=DB[:, dc * SW + st:dc * SW + st + sz], start=(dc == 0), stop=(dc == 1))
                            return ins
                        P.op("pe", mmp, reads=[BDB, bc("wpl")], writes=[B_ps[bk]])
                        P.op("dve", lambda e, bk=bk, st=st, sz=sz, c=c: e.scalar_tensor_tensor(out=o_all[:, 8 + c, st:st + sz], in0=psum[:, bk, 0:sz], scalar=pscale_fm[:, c:c + 1], in1=G[:, st:st + sz], op0=ALU.mult, op1=ALU.mult),
                             reads=[B_ps[bk], BG, bc("pscale")], writes=[B_o[8 + c]], after=[B_uq])
                    yield
            return (load, gen)

        def run_pipeline(chains, depth=2, skew=9, lookahead=2):
            loaded = 0
            active = []
            nxt = 0
            while nxt < len(chains) or active:
                while loaded < len(chains) and loaded <= nxt + lookahead:
                    if not chains[loaded][0]():
                        break
                    loaded += 1
                if nxt < len(chains) and nxt < loaded and len(active) < depth and (not active or active[-1][1] >= skew):
                    active.append([chains[nxt][1](), 0])
                    nxt += 1
                assert active, "pipeline stalled"
                for a_ in list(active):
                    try:
                        next(a_[0])
                        a_[1] += 1
                    except StopIteration:
                        active.remove(a_)

        chains = []
        ci = 0
        for n in range(8):
            chains.append(make_rg(n, True, ci)); ci += 1
        for n in range(8):
            chains.append(make_rg(n, False, ci)); ci += 1
        def run_seq(chains, skew):
            loaded = 0
            nxt = 0
            cur = None
            pre = None
            steps = 0
            while True:
                while loaded < len(chains) and loaded <= nxt + 1:
                    if not chains[loaded][0]():
                        break
                    loaded += 1
                if cur is None:
                    if pre is not None:
                        cur, pre, steps = pre, None, 1
                    elif nxt < len(chains):
                        assert nxt < loaded
                        cur = chains[nxt][1]()
                        nxt += 1
                        next(cur)
                        steps = 1
                    else:
                        break
                try:
                    next(cur)
                    steps += 1
                except StopIteration:
                    cur = None
                    continue
                if steps == skew and pre is None and nxt < len(chains) and nxt < loaded:
                    pre = chains[nxt][1]()
                    nxt += 1
                    next(pre)

        def run_rg_sched(chains):
            N = len(chains)
            gens = [None] * N
            loaded = [0]

            def ensure(j, must):
                while loaded[0] <= min(j, N - 1):
                    if not chains[loaded[0]][0]():
                        break
                    loaded[0] += 1
                if must:
                    assert loaded[0] > j, "weights for chain %d could not be queued" % j

            def G(i):
                if i < 0 or i >= N:
                    return None
                if gens[i] is None:
                    ensure(i, True)
                    gens[i] = chains[i][1]()
                return gens[i]

            def st(g, n=1):
                if g is None:
                    return
                for _ in range(n):
                    try:
                        next(g)
                    except StopIteration:
                        return

            st(G(0), 2)
            for i in range(N + 1):
                A = gens[i - 1] if i >= 1 else None
                B = G(i) if i < N else None
                C = G(i + 1) if i + 1 < N else None
                ensure(i + 2, False)
                st(B, 1)
                st(C, 1)
                st(A, 2)
                st(B, 5)
                st(A, 2)
                st(B, 2)
                st(A, 2)
                st(C, 1)

        run_rg_sched(chains)
        rg_fence(0)
        rg_fence(1)
        chains = []
        for g in range(4):
            chains.append(make_pool(g, ci)); ci += 1
        run_pipeline(chains)

        for t in range(2):
            s_q = win_tile(8 + t)
            for j in range(4):
                c = t * 4 + j
                banks = z_mm(s_q, j, ucols, TILES, B_uT)
                for (bk, st, sz) in banks:
                    P.op("act", lambda e, bk=bk, st=st, sz=sz, c=c: e.activation(out=qT[:, c, st:st + sz], in_=psum[:, bk, 0:sz], func=AF.Copy), reads=[B_ps[bk]], writes=[B_qT[c]], after=[B_regB] + B_F2 + B_kvf)
        s_gx = [win_tile(10), win_tile(11)]
        SC = 1.0 / 16.0

        ep3 = [ep[0], ep[1], carve(256)]
        smx = carve(8)
        ab8 = [0]

        def abank8():
            b_ = ab8[0]
            ab8[0] = (b_ + 1) % 8
            return b_

        def attn_stages(it, h, tb):
            i3 = it % 3
            e_t = ep3[i3]
            be = bc("ep3_%d" % i3)
            mx = smx[:, i3:i3 + 1]
            sume = smx[:, 3 + i3:4 + i3]
            bm = bc("mx3_%d" % i3)
            bs_ = bc("sume3_%d" % i3)
            stt = {}

            def s1():
                bk = abank8()
                stt["bk"] = bk

                def mms(e):
                    ins = None
                    for dc in range(2):
                        ins = e.matmul(psum[:, bk, 0:256], lhsT=qT[:, 2 * h + dc, tb * 128:(tb + 1) * 128], rhs=KT[:, 2 * h + dc, :], start=(dc == 0), stop=(dc == 1))
                    return ins
                P.op("pe", mms, reads=[B_qT[2 * h], B_qT[2 * h + 1], bc("KT")], writes=[B_ps[bk]])

            def s2():
                bk = stt["bk"]
                P.op("dve", lambda e: e.reduce_max(out=mx, in_=psum[:, bk, 0:256], axis=AX.X), reads=[B_ps[bk]], writes=[bm])
                P.op("dve", lambda e: e.tensor_scalar(out=mx, in0=mx, scalar1=-SC, scalar2=None, op0=ALU.mult), reads=[bm], writes=[bm])
                P.op("act", lambda e: e.activation(out=e_t, in_=psum[:, bk, 0:256], func=AF.Exp, scale=SC, bias=mx, accum_out=sume),
                     reads=[B_ps[bk], bm], writes=[be, bs_])

            def s3():
                P.op("dve", lambda e: e.reciprocal(out=sume, in_=sume), reads=[bs_], writes=[bs_])
                P.op("dve", lambda e: e.tensor_scalar(out=e_t, in0=e_t, scalar1=sume, scalar2=None, op0=ALU.mult), reads=[be, bs_], writes=[be])
                bk2 = abank8()
                stt["bk2"] = bk2

                def trp(e):
                    ins = None
                    for mc in range(2):
                        ins = e.transpose(out=psum[:, bk2, mc * 128:(mc + 1) * 128], in_=e_t[:, mc * 128:(mc + 1) * 128], identity=ident)
                    return ins
                P.op("pe", trp, reads=[be, bc("ident")], writes=[B_ps[bk2]])

            def s4():
                bk2 = stt["bk2"]
                P.op("act", lambda e: e.activation(out=pT_all[:, :, h, tb * 128:(tb + 1) * 128], in_=psum[:, bk2, 0:256].rearrange("p (mc t) -> p mc t", mc=2), func=AF.Copy),
                     reads=[B_ps[bk2]], writes=[B_pT], after=[B_regB] + B_F2 + B_kvf)
            return (s1, s2, s3, s4)

        astg = [attn_stages(h * 8 + tb, h, tb) for h in range(4) for tb in range(8)]
        NA = len(astg)
        for r in range(NA + 3):
            for d in range(4):
                k = r - d
                if 0 <= k < NA:
                    astg[k][d]()

        P.op("dve", lambda e: e.memset(qexp.rearrange("p h dc b j -> p (h dc b j)"), 0.0), writes=[bc("qexp")], after=[B_uq])
        for h in range(4):
            for dc in range(2):
                P.op("dve", lambda e, h=h, dc=dc: e.tensor_copy(out=qexp[:, h, dc, :, :].rearrange("p b j -> p (b j)")[:, 0:256:17], in_=qT[:, 2 * h + dc, 1024:1040]),
                     reads=[B_qT[2 * h + dc], bc("qexp")], writes=[bc("qexp")])
        kstb = [xstage[0][:, i * 1024:(i + 1) * 1024].bitcast(BF16).rearrange("p (m f) -> p m f", m=2) for i in range(2)]
        vstb = [xstage[1][:, i * 1024:(i + 1) * 1024].bitcast(BF16).rearrange("p (m f) -> p m f", m=2) for i in range(2)]
        B_kb = [Buf("kb0"), Buf("kb1")]
        B_vb = [Buf("vb0"), Buf("vb1")]
        ident_bf = carve(64).bitcast(BF16)
        P.op("dve", lambda e: e.tensor_copy(out=ident_bf, in_=ident), reads=[bc("ident")], writes=[bc("ident_bf")])
        pTs_bf = carve(64).bitcast(BF16).rearrange("p (mc h b) -> p mc h b", mc=2, h=4)
        KTs = Fs[4].bitcast(BF16)[:, 0:2048].rearrange("p (c m) -> p c m", c=8)
        psum_bf = [psum[:, 6, :].bitcast(BF16), psum[:, 7, :].bitcast(BF16)]
        KTs2 = [KTs, Fs[5].bitcast(BF16)[:, 0:2048].rearrange("p (c m) -> p c m", c=8)]
        BK2 = [B_F[4], B_F[5]]
        pbf = {bk_: psum[:, bk_, :].bitcast(BF16) for bk_ in (4, 5, 6, 7)}

        def s_T(b):
            kb = kstb[b % 2]
            kt = KTs2[b % 2]
            P.dma("pool", lambda e: e.dma_start(out=kb, in_=ck[b].rearrange("(mc p) f -> p mc f", p=128)), writes=[B_kb[b % 2]],
                  after=([B_F[0], B_F[1]] if b < 2 else []))
            for mc in range(2):
                bk = (4 + mc) if b % 2 == 0 else (6 + mc)
                pb = pbf[bk]

                def trk(e, mc=mc, pb=pb):
                    ins = None
                    for c in range(8):
                        ins = e.transpose(out=pb[:, c * 128:(c + 1) * 128], in_=kb[:, mc, c * 128:(c + 1) * 128], identity=ident_bf)
                    return ins
                P.op("pe", trk, reads=[B_kb[b % 2], bc("ident_bf")], writes=[B_ps[bk]])
                if mc == 0:
                    P.op("act", lambda e, mc=mc, pb=pb: e.activation(out=kt[:, :, mc * 128:(mc + 1) * 128], in_=pb.rearrange("p (j m) -> p j m", j=8), func=AF.Copy),
                         reads=[B_ps[bk]], writes=[BK2[b % 2]])
                else:
                    P.op("dve", lambda e, mc=mc, pb=pb: e.tensor_copy(out=kt[:, :, mc * 128:(mc + 1) * 128], in_=pb.rearrange("p (j m) -> p j m", j=8)),
                         reads=[B_ps[bk]], writes=[BK2[b % 2]])

        def s_M(b):
            kt = KTs2[b % 2]
            for h in range(4):
                def mmq(e, h=h):
                    ins = None
                    for dc in range(2):
                        ins = e.matmul(psum[0:16, h, 0:256], lhsT=qexp[:, h, dc, b, :], rhs=kt[:, 2 * h + dc, :], start=(b == 0 and dc == 0), stop=(b == TS - 1 and dc == 1))
                    return ins
                P.op("pe", mmq, reads=[BK2[b % 2], bc("qexp")], writes=[B_ps[h]])

        s_T(0)
        for b in range(TS):
            if b + 1 < TS:
                s_T(b + 1)
            s_M(b)
        for h in range(4):
            mx = sm[0:16, 8:9]
            sume = sm[0:16, 9:10]
            P.op("dve", lambda e, h=h: e.reduce_max(out=mx, in_=psum[0:16, h, 0:256], axis=AX.X), reads=[B_ps[h]], writes=[bc("mxs")])
            P.op("dve", lambda e: e.tensor_scalar(out=mx, in0=mx, scalar1=-SC, scalar2=None, op0=ALU.mult), reads=[bc("mxs")], writes=[bc("mxs")])
            P.op("act", lambda e, h=h: e.activation(out=ps_s[0:16, h, :], in_=psum[0:16, h, 0:256], func=AF.Exp, scale=SC, bias=mx, accum_out=sume), reads=[B_ps[h], bc("mxs")], writes=[bc("ps_s"), bc("sumes")], after=[B_uq])
            P.op("dve", lambda e: e.reciprocal(out=sume, in_=sume), reads=[bc("sumes")], writes=[bc("sumes")])
            P.op("dve", lambda e, h=h: e.tensor_scalar(out=ps_s[0:16, h, :], in0=ps_s[0:16, h, :], scalar1=sume, scalar2=None, op0=ALU.mult), reads=[bc("ps_s"), bc("sumes")], writes=[bc("ps_s")])
        bk = auxbank()

        def trps(e, bk=bk):
            ins = None
            for mc in range(2):
                for h in range(4):
                    ins = e.transpose(out=psum[:, bk, (mc * 4 + h) * 16:(mc * 4 + h) * 16 + 16], in_=ps_s[0:16, h, mc * 128:(mc + 1) * 128], identity=ident[0:16, 0:16])
            return ins
        P.op("pe", trps, reads=[bc("ps_s"), bc("ident")], writes=[B_ps[bk]])
        P.op("dve", lambda e, bk=bk: e.tensor_copy(out=pTs.rearrange("p mc h b -> p (mc h b)"), in_=psum[:, bk, 0:128]), reads=[B_ps[bk]], writes=[bc("pTs")])
        P.op("dve", lambda e: e.tensor_copy(out=pTs_bf.rearrange("p mc h b -> p (mc h b)"), in_=pTs.rearrange("p mc h b -> p (mc h b)")), reads=[bc("pTs")], writes=[bc("pTs_bf")])
        bko = auxbank()
        vst4 = [vstb[0], vstb[1], kstb[0], kstb[1]]
        B_v4 = [B_vb[0], B_vb[1], B_kb[0], B_kb[1]]
        for b in range(TS):
            vb = vst4[b % 4]
            P.dma("pool", lambda e, vb=vb, b=b: e.dma_start(out=vb, in_=cv[b].rearrange("(mc p) f -> p mc f", p=128)), writes=[B_v4[b % 4]],
                  after=([B_F[2], B_F[3]] if b < 2 else []))

            def mmv(e, b=b, bko=bko, vb=vb):
                ins = None
                for c in range(8):
                    for mc in range(2):
                        ins = e.matmul(psum[:, bko, c * 16 + b:c * 16 + b + 1], lhsT=vb[:, mc, c * 128:(c + 1) * 128], rhs=pTs_bf[:, mc, c // 2, b:b + 1], start=(mc == 0), stop=(mc == 1))
                return ins
            P.op("pe", mmv, reads=[B_v4[b % 4], bc("pTs_bf")], writes=[B_ps[bko]])
        P.op("dve", lambda e: e.memset(small15[:, 0:1], 0.0), reads=B_kb + B_vb, writes=[bc("small15")] + B_F[0:4])
        P.op("dve", lambda e, bko=bko: e.tensor_copy(out=os_fm.rearrange("p c b -> p (c b)"), in_=psum[:, bko, 0:128]), reads=[B_ps[bko]], writes=[bc("os_fm")])

        for t in range(2):
            s_g = s_gx[t]
            for j in range(4):
                c = t * 4 + j
                h = c // 2
                gbanks = z_mm(s_g, j, ucols, TILES, B_uT)
                G = Fs[5]
                for (bk, st, sz) in gbanks:
                    P.op("act", lambda e, bk=bk, st=st, sz=sz, G=G: e.activation(out=G[:, st:st + sz], in_=psum[:, bk, 0:sz], func=AF.Silu), reads=[B_ps[bk]], writes=[B_F[5]])
                for tt in range(2):
                    bk = bank()

                    def mmpv(e, bk=bk, tt=tt, c=c, h=h):
                        ins = None
                        for mc in range(2):
                            ins = e.matmul(psum[:, bk, :], lhsT=Vb[:, mc, c * 128:(c + 1) * 128], rhs=pT_all[:, mc, h, tt * 512:(tt + 1) * 512], start=(mc == 0), stop=(mc == 1))
                        return ins
                    P.op("pe", mmpv, reads=[bc("Vb"), B_pT], writes=[B_ps[bk]])
                    P.op("dve", lambda e, bk=bk, tt=tt, c=c, G=G: e.tensor_tensor(out=o_all[:, 16 + c, tt * 512:(tt + 1) * 512], in0=psum[:, bk, :], in1=G[:, tt * 512:(tt + 1) * 512], op=ALU.mult),
                         reads=[B_ps[bk], B_F[5]], writes=[B_o[16 + c]], after=[B_uq, bc("qexp"), bc("ps_s")])
                P.op("dve", lambda e, c=c, G=G: e.tensor_tensor(out=o_all[:, 16 + c, 1024:1040], in0=os_fm[:, c, :], in1=G[:, 1024:1040], op=ALU.mult), reads=[bc("os_fm"), B_F[5]], writes=[B_o[16 + c]], after=[B_uq, bc("qexp"), bc("ps_s")])

        outT, outT2, outT3, outT4, outT5, outT6 = [KT_w[:, 128 * i:128 * (i + 1)] for i in range(6)]

        def fm_out(src2d, ncols, stage, emit_dmas, rbufs):
            bk = auxbank()
            P.op("pe", lambda e, bk=bk: e.transpose(out=psum[0:ncols, bk, 0:128], in_=src2d, identity=ident), reads=rbufs + [bc("ident")], writes=[B_ps[bk]])
            sb_ = Buf()
            P.op("dve", lambda e, bk=bk: e.tensor_copy(out=stage[0:ncols, :], in_=psum[0:ncols, bk, 0:128]), reads=[B_ps[bk]], writes=[sb_], after=[bc("KT")])
            emit_dmas(sb_)

        out_dma(conv_s[:, 0:2, :], st_conv[:, 1:3, :], [])
        out_dma(pool_s[:, 0:14, :], st_pool[:, 1:15, :], [])
        fm_out(hlast, 8, outT, lambda sb_: out_dma(h_p.rearrange("(c k) -> c k", k=128), outT[0:8, :], [sb_]), [bc("hlast")])
        fm_out(tailx.rearrange("p c r -> p (c r)"), 24, outT2,
               lambda sb_: [out_dma(conv_p[:, c * 128:(c + 1) * 128], outT2[3 * c:3 * c + 3, :], [sb_]) for c in range(8)], [bc("tailx")])
        fm_out(tailp.rearrange("p c r -> p (c r)"), 120, outT3,
               lambda sb_: [out_dma(pool_p[:, c * 128:(c + 1) * 128], outT3[15 * c:15 * c + 15, :], [sb_]) for c in range(8)], [bc("tailp")])
        fm_out(hsT_in.rearrange("p c b -> p (c b)"), 128, outT4,
               lambda sb_: [out_dma(h_s[:, c * 128:(c + 1) * 128], outT4[16 * c:16 * c + 16, :], [sb_]) for c in range(8)], [bc("hsT_in")])
        fm_out(xsT_in.rearrange("p c b -> p (c b)"), 128, outT5,
               lambda sb_: [out_dma(conv_s[:, 2, c * 128:(c + 1) * 128], outT5[16 * c:16 * c + 16, :], [sb_]) for c in range(8)], [bc("xsT_in")])
        fm_out(xpsT_in.rearrange("p c b -> p (c b)"), 128, outT6,
               lambda sb_: [out_dma(pool_s[:, 14, c * 128:(c + 1) * 128], outT6[16 * c:16 * c + 16, :], [sb_]) for c in range(8)], [bc("xpsT_in")])

        B_merged = Buf("merged")
        Wsl = [Fw[:, 3 * SW + k * 1536:3 * SW + (k + 1) * 1536].bitcast(BF16).rearrange("p (kc n) -> p kc n", kc=24) for k in range(2)]
        B_W2 = [Buf("W2a"), Buf("W2b")]

        def load_phaseB(f):
            s_ = f % 2
            for i in range(3):
                dst = wsl[s_][:, i * 2048:(i + 1) * 2048].rearrange("p (kc n) -> p kc n", kc=16)
                src = w_in[:, 6144 + i * 2048 + f * 128:6144 + i * 2048 + (f + 1) * 128].rearrange("(kc p) n -> p kc n", p=128)
                P.dma("pool", lambda e, dst=dst, src=src: e.dma_start(out=dst, in_=src), writes=[B_ws[s_]])
            srcw = w_branch[:, f * 128:(f + 1) * 128].rearrange("(kc p) n -> p kc n", p=128)
            P.dma("pool", lambda e, s_=s_, srcw=srcw: e.dma_start(out=Wsl[s_], in_=srcw), writes=[B_W2[s_]], after=[B_F[3], B_F[4], B_F[5]])

        load_phaseB(0)
        for f in range(16):
            if f + 1 < 16:
                load_phaseB(f + 1)
            s_ = f % 2
            gv = wsl[s_][:, 0:6144].rearrange("p (i kc n) -> p i kc n", i=3, kc=16)
            bv = Wsl[s_]
            MACC, BMACC = Fs[2], B_F[2]
            for i in range(3):
                GS, BGS = (Fs[0], B_F[0]) if i != 1 else (Fs[1], B_F[1])
                for (st, sz) in TILES:
                    bk = bank()

                    def mmg(e, bk=bk, st=st, sz=sz, i=i, gv=gv):
                        ins = None
                        for kc in range(16):
                            ins = e.matmul(psum[:, bk, 0:sz], lhsT=gv[:, i, kc, :], rhs=uT[:, kc, st:st + sz], start=(kc == 0), stop=(kc == 15))
                        return ins
                    P.op("pe", mmg, reads=[B_ws[s_], B_uT], writes=[B_ps[bk]])
                    P.op("act", lambda e, bk=bk, st=st, sz=sz, GS=GS: e.activation(out=GS[:, st:st + sz], in_=psum[:, bk, 0:sz], func=AF.Sigmoid), reads=[B_ps[bk]], writes=[BGS])
                for (st, sz) in TILES:
                    bk = bank()

                    def mmy(e, bk=bk, st=st, sz=sz, i=i, bv=bv):
                        ins = None
                        for kc in range(8):
                            ins = e.matmul(psum[:, bk, 0:sz], lhsT=bv[:, i * 8 + kc, :], rhs=o_all[:, i * 8 + kc, st:st + sz], start=(kc == 0), stop=(kc == 7))
                        return ins
                    P.op("pe", mmy, reads=[B_W2[s_]] + B_o[i * 8:(i + 1) * 8], writes=[B_ps[bk]])
                    if i == 0:
                        P.op("dve", lambda e, bk=bk, st=st, sz=sz, GS=GS: e.tensor_tensor(out=MACC[:, st:st + sz], in0=psum[:, bk, 0:sz], in1=GS[:, st:st + sz], op=ALU.mult), reads=[B_ps[bk], BGS], writes=[BMACC])
                    elif i == 1:
                        P.op("dve", lambda e, bk=bk, st=st, sz=sz, GS=GS: e.tensor_tensor(out=GS[:, st:st + sz], in0=psum[:, bk, 0:sz], in1=GS[:, st:st + sz], op=ALU.mult), reads=[B_ps[bk], BGS], writes=[BGS])
                        P.op("dve", lambda e, st=st, sz=sz, GS=GS: e.tensor_tensor(out=MACC[:, st:st + sz], in0=MACC[:, st:st + sz], in1=GS[:, st:st + sz], op=ALU.add), reads=[BMACC, BGS], writes=[BMACC], safe=True)
                    else:
                        P.op("dve", lambda e, bk=bk, st=st, sz=sz, GS=GS: e.tensor_tensor(out=GS[:, st:st + sz], in0=psum[:, bk, 0:sz], in1=GS[:, st:st + sz], op=ALU.mult), reads=[B_ps[bk], BGS], writes=[BGS])
                        P.op("dve", lambda e, st=st, sz=sz, f=f, GS=GS: e.tensor_tensor(out=merged[:, f, st:st + sz], in0=MACC[:, st:st + sz], in1=GS[:, st:st + sz], op=ALU.add), reads=[BMACC, BGS], writes=[B_merged],
                             after=[B_pT] + B_qT, safe=True)

        wo = []
        for ct in range(4):
            src = w_out[:, ct * 512:(ct + 1) * 512].rearrange("(kc p) n -> p kc n", p=128)
            if ct < 2:
                s = wtile(lambda slot, src=src: [(slot.rearrange("p (kc n) -> p kc n", kc=16), src)])
                wo.append((wview(s), B_ws[s]))
            else:
                v = uT_w.bitcast(BF16)[:, (ct - 2) * 8192:(ct - 1) * 8192].rearrange("p (kc n) -> p kc n", kc=16)
                bwo = Buf("wo%d" % ct)
                P.dma("pool", lambda e, v=v, src=src: e.dma_start(out=v, in_=src), writes=[bwo], after=[B_uT])
                wo.append((v, bwo))
        load("sp", gpost_bc, g_post.partition_broadcast(128), [bc("gpost")], after=B_o)
        blocks = [(tb * 128, 128, xp[tb * 128:(tb + 1) * 128, :], y_p[tb * 128:(tb + 1) * 128, :]) for tb in range(8)] + [(1024, 16, xs, y_s)]
        osb_l = [osb, regC_w[:, 3072:5120]]
        B_osb_l = [[B_F[4], B_F[5]], [Buf("osb2")]]
        for bi, (t0, n, xsrc, ydst) in enumerate(blocks):
            osb = osb_l[bi % 2]
            B_osb = B_osb_l[bi % 2]
            ssqc = ssq[:, 4 * (bi % 2):4 * (bi % 2) + 4]
            ssqb = bc("ssqC%d" % (bi % 2))
            xi = bi % 2
            xst = xstage[xi]
            bxs = [B_F[2 * xi], B_F[2 * xi + 1]]
            load("sp", xst[0:n, :], xsrc, bxs, after=B_W2)
            for ct in range(4):
                bk = bank()
                wv, wb = wo[ct]

                def mmo(e, bk=bk, wv=wv, t0=t0, n=n):
                    ins = None
                    for kc in range(16):
                        ins = e.matmul(psum[0:n, bk, :], lhsT=merged[:, kc, t0:t0 + n], rhs=wv[:, kc, :], start=(kc == 0), stop=(kc == 15))
                    return ins
                P.op("pe", mmo, reads=[B_merged, wb], writes=[B_ps[bk]])
                P.op("act", lambda e, bk=bk, ct=ct, n=n, osb=osb: e.activation(out=osb[0:n, ct * 512:(ct + 1) * 512], in_=psum[0:n, bk, :], func=AF.Copy), reads=[B_ps[bk]], writes=B_osb, after=B_W2 + B_o)
                P.op("act", lambda e, bk=bk, ct=ct, n=n, ssqc=ssqc: e.activation(out=junk[0:n, :], in_=psum[0:n, bk, :], func=AF.Square, accum_out=ssqc[0:n, ct:ct + 1]), reads=[B_ps[bk]], writes=[bc("junk"), ssqb])
            rstd = sm[:, 10 + (bi % 2):11 + (bi % 2)]
            rb = bc("rstdC%d" % (bi % 2))
            P.op("dve", lambda e, n=n, rstd=rstd, ssqc=ssqc: e.tensor_reduce(out=rstd[0:n], in_=ssqc[0:n, 0:4], axis=AX.X, op=ALU.add), reads=[ssqb], writes=[rb])
            P.op("dve", lambda e, n=n, rstd=rstd: e.tensor_scalar(out=rstd[0:n], in0=rstd[0:n], scalar1=1.0 / D, scalar2=EPS, op0=ALU.mult, op1=ALU.add), reads=[rb], writes=[rb])
            P.op("act", lambda e, n=n, rstd=rstd: e.activation(out=rstd[0:n], in_=rstd[0:n], func=AF.Sqrt), reads=[rb], writes=[rb])
            P.op("dve", lambda e, n=n, rstd=rstd: e.reciprocal(out=rstd[0:n], in_=rstd[0:n]), reads=[rb], writes=[rb])
            P.op("dve", lambda e, n=n, osb=osb, rstd=rstd: e.scalar_tensor_tensor(out=osb[0:n, :], in0=osb[0:n, :], scalar=rstd[0:n], in1=gpost_bc[0:n, :], op0=ALU.mult, op1=ALU.mult), reads=B_osb + [rb, bc("gpost")], writes=B_osb)
            P.op("dve", lambda e, n=n, xst=xst, osb=osb: e.tensor_tensor(out=osb[0:n, :], in0=osb[0:n, :], in1=xst[0:n, :], op=ALU.add), reads=B_osb + bxs, writes=B_osb, safe=True)
            out_dma(ydst, osb[0:n, :], B_osb)

        dump(8, xsT_in.rearrange("p c b -> p (c b)"), [bc("xsT_in")], 128)
        dump(9, xpsT_in.rearrange("p c b -> p (c b)"), [bc("xpsT_in")], 128)
        dump(10, tailp.rearrange("p c r -> p (c r)"), [bc("tailp")], 120)
        P.wait_all("sp", out_bufs)
        print('arena used', off[0], 'of', NW)
        P.emit()
    return nc


_NC_CACHE = {}


def kernel(x_prompt, x_sample, mem_prompt, state_rglru_h, state_conv, state_pool, cache_mem_k, cache_mem_v,
           g_pre, w_in, conv_w, conv_b, w_rg_a, b_rg_a, w_rg_x, b_rg_x, lru_lambda, w_pool, pool_scale, g_mem,
           w_kv, w_branch, w_out, g_post):
    f = lambda a: np.ascontiguousarray(np.asarray(a, dtype=np.float32))
    x_prompt = f(x_prompt); x_sample = f(x_sample); mem_prompt = f(mem_prompt)
    state_rglru_h = f(state_rglru_h); state_conv = f(state_conv); state_pool = f(state_pool)
    cache_mem_k = f(cache_mem_k); cache_mem_v = f(cache_mem_v)
    shared = {
        "g_pre": f(g_pre)[0], "w_in": f(w_in)[0], "conv_w": f(conv_w)[0], "conv_b": f(conv_b)[0],
        "w_rg_a": f(w_rg_a)[0], "b_rg_a": f(b_rg_a)[0], "w_rg_x": f(w_rg_x)[0], "b_rg_x": f(b_rg_x)[0],
        "lam": f(lru_lambda)[0], "w_pool": f(w_pool)[0], "pool_scale": f(pool_scale)[0], "g_mem": f(g_mem)[0],
        "w_kv": f(w_kv)[0], "w_branch": f(w_branch)[0], "w_out": f(w_out)[0], "g_post": f(g_post)[0],
    }
    in_maps = []
    for c in range(8):
        b, half = c // 2, c % 2
        m = dict(shared)
        m["xp"] = x_prompt[b, half * 1024:(half + 1) * 1024]
        m["xq"] = x_prompt[b, 0:1024] if half == 1 else np.zeros((1024, 2048), np.float32)
        m["xs"] = x_sample[16 * c:16 * c + 16, 0]
        m["mem"] = mem_prompt[b]
        m["st_h"] = state_rglru_h[0, 16 * c:16 * c + 16]
        m["st_conv"] = state_conv[0, 16 * c:16 * c + 16]
        m["st_pool"] = state_pool[0, 16 * c:16 * c + 16]
        m["ck"] = cache_mem_k[0, 16 * c:16 * c + 16].reshape(16, 256, 1024)
        m["cv"] = cache_mem_v[0, 16 * c:16 * c + 16].reshape(16, 256, 1024)
        m["flag"] = np.full((128, 1), float(half), np.float32)
        ic = np.zeros((128, 4, 16), np.float32)
        for g, w in enumerate(POOLW):
            for t in range(16):
                pos = half * 1024 + t
                ic[:, g, t] = 1.0 / min(pos + 1, w)
        m["icnt"] = ic
        in_maps.append({k: np.ascontiguousarray(v) for k, v in m.items()})
    if "nc" not in _NC_CACHE:
        _NC_CACHE["nc"] = build_program()
    nc = _NC_CACHE["nc"]
    res = run_bass_kernel_spmd(nc, in_maps, core_ids=list(range(8)))
    R = res.results
    y_prompt = np.zeros((4, 2048, 2048), np.float32)
    y_sample = np.zeros((128, 1, 2048), np.float32)
    new_h_p = np.zeros((1, 4, 1024), np.float32); new_conv_p = np.zeros((1, 4, 3, 1024), np.float32)
    new_pool_p = np.zeros((1, 4, 15, 1024), np.float32)
    mk = np.zeros((1, 4, 256, 4, 256), np.float32); mv = np.zeros((1, 4, 256, 4, 256), np.float32)
    new_h_s = np.zeros((1, 128, 1024), np.float32); new_conv_s = np.zeros((1, 128, 3, 1024), np.float32)
    new_pool_s = np.zeros((1, 128, 15, 1024), np.float32)
    for c in range(8):
        b, half = c // 2, c % 2
        r = R[c]
        y_prompt[b, half * 1024:(half + 1) * 1024] = r["y_p"]
        y_sample[16 * c:16 * c + 16, 0] = r["y_s"]
        new_h_s[0, 16 * c:16 * c + 16] = r["h_s"]
        new_conv_s[0, 16 * c:16 * c + 16] = r["conv_s"]
        new_pool_s[0, 16 * c:16 * c + 16] = r["pool_s"]
        if half == 1:
            new_h_p[0, b] = r["h_p"]
            new_conv_p[0, b] = r["conv_p"]
            new_pool_p[0, b] = r["pool_p"]
        else:
            mk[0, b] = r["mk"].reshape(256, 4, 256)
            mv[0, b] = r["mv"].reshape(256, 4, 256)
    return (y_prompt, y_sample, new_h_p, new_conv_p, new_pool_p, mk, mv, new_h_s, new_conv_s, new_pool_s)
```

```python
import numpy as np
from contextlib import ExitStack
import concourse.bass as bass
import concourse.mybir as mybir
from concourse.bass_utils import run_bass_kernel_spmd

F32 = mybir.dt.float32
BF16 = mybir.dt.bfloat16
AF = mybir.ActivationFunctionType
ALU = mybir.AluOpType
AX = mybir.AxisListType

ENGS = ("pe", "act", "dve", "pool", "sp")


class Buf:
    __slots__ = ("w", "r", "name")

    def __init__(self, name=""):
        self.w = None
        self.r = []
        self.name = name


class Prog:
    def __init__(self, nc, n_dma_sems=32):
        self.nc = nc
        self.ops = {e: [] for e in ENGS}
        self.cnt = {e: 0 for e in ENGS}
        self.waited = {e: {} for e in ENGS}
        self.rings = {"sp": 24, "pool": 24, "act": 8}
        self.dma_val = {q: [0] * n for q, n in self.rings.items()}
        self.dma_next = {q: 0 for q in self.rings}
        self.sems = {}

    @staticmethod
    def _flat(x):
        out = []
        for b in x:
            if isinstance(b, (list, tuple)):
                out.extend(Prog._flat(b))
            else:
                out.append(b)
        return out

    def _collect(self, eng, reads, writes, after, safe):
        reads, writes, after = self._flat(reads), self._flat(writes), self._flat(after)
        deps = set()
        for b in reads:
            if b.w is not None:
                k, v = b.w
                if k == eng:
                    if eng == "pe" or safe or self.cnt[eng] - v >= 3:
                        continue
                deps.add(b.w)
        for b in list(writes) + list(after):
            if b.w is not None and b.w[0] != eng:
                deps.add(b.w)
            for c in b.r:
                if c[0] != eng:
                    deps.add(c)
        wd = self.waited[eng]
        best = {}
        for (k, v) in deps:
            if wd.get(k, 0) >= v:
                continue
            if best.get(k, 0) < v:
                best[k] = v
        out = []
        for k, v in best.items():
            wd[k] = v
            out.append((k, v))
        return out

    def _finish(self, comp, reads, writes):
        reads, writes = self._flat(reads), self._flat(writes)
        for b in reads:
            b.r.append(comp)
        for b in writes:
            b.w = comp
            b.r = []

    def op(self, eng, fn, reads=(), writes=(), after=(), safe=False):
        waits = self._collect(eng, reads, writes, after, safe)
        self.cnt[eng] += 1
        comp = (eng, self.cnt[eng])
        self.ops[eng].append((fn, waits, (eng, 1)))
        self._finish(comp, reads, writes)
        return comp

    def dma(self, eng, fn, reads=(), writes=(), after=()):
        i = self.dma_next[eng]
        self.dma_next[eng] = (i + 1) % self.rings[eng]
        key = "d%s%d" % (eng, i)
        waits = self._collect(eng, reads, writes, after, False)
        prev = self.dma_val[eng][i]
        if prev > 0 and self.waited[eng].get(key, 0) < prev:
            self.waited[eng][key] = prev
            waits.append((key, prev))
        self.dma_val[eng][i] += 16
        comp = (key, self.dma_val[eng][i])
        self.ops[eng].append((fn, waits, (key, 16)))
        self._finish(comp, reads, writes)
        return comp

    def wait_all(self, eng, bufs):
        waits = self._collect(eng, bufs, (), (), False)
        self.ops[eng].append((None, waits, None))

    def emit(self):
        nc = self.nc
        with ExitStack() as es:
            for e in ENGS:
                self.sems[e] = es.enter_context(nc.semaphore("s_" + e))
            for q, n in self.rings.items():
                for i in range(n):
                    self.sems["d%s%d" % (q, i)] = es.enter_context(nc.semaphore("s_d%s%d" % (q, i)))
            block = es.enter_context(nc.Block())

            def run(h, name):
                for (fn, waits, inc) in self.ops[name]:
                    for (k, v) in waits:
                        h.wait_ge(self.sems[k], v)
                    if fn is None:
                        continue
                    ins = fn(h)
                    if inc is not None:
                        ins.then_inc(self.sems[inc[0]], inc[1])

            @block.tensor
            def _(e):
                run(e, "pe")

            @block.scalar
            def _(e):
                run(e, "act")

            @block.vector
            def _(e):
                run(e, "dve")

            @block.gpsimd
            def _(e):
                run(e, "pool")

            @block.sync
            def _(e):
                run(e, "sp")


D = 2048
TP, TS, T = 1024, 16, 1040
TILES = [(0, 347), (347, 347), (694, 346)]
QT = [(0, 512), (512, 512)]
SW = 1056
NSLOT = 6
EPS = 1e-6
POOLW = (2, 4, 8, 16)


def build_program(dbg=False):
    nc = bass.Bass("TRN2", target_bir_lowering=False)
    dbg_t = nc.dram_tensor("dbg", [128, 12, 1040], F32, kind="ExternalOutput").ap() if dbg else None

    def din(name, shape):
        return nc.dram_tensor(name, shape, F32, kind="ExternalInput").ap()

    def dout(name, shape):
        return nc.dram_tensor(name, shape, F32, kind="ExternalOutput").ap()

    xp = din("xp", [TP, D]); xq = din("xq", [TP, D]); xs = din("xs", [TS, D]); mem = din("mem", [256, D])
    st_h = din("st_h", [TS, 1024]); st_conv = din("st_conv", [TS, 3, 1024]); st_pool = din("st_pool", [TS, 15, 1024])
    ck = din("ck", [TS, 256, 1024]); cv = din("cv", [TS, 256, 1024])
    g_pre = din("g_pre", [D]); w_in = din("w_in", [D, 12288]); conv_w = din("conv_w", [4, 1024]); conv_b = din("conv_b", [1024])
    w_rg_a = din("w_rg_a", [8, 128, 128]); b_rg_a = din("b_rg_a", [1024]); w_rg_x = din("w_rg_x", [8, 128, 128]); b_rg_x = din("b_rg_x", [1024])
    lam = din("lam", [1024]); w_pool = din("w_pool", [4, 256, 256]); pool_scale = din("pool_scale", [1024]); g_mem = din("g_mem", [D])
    w_kv = din("w_kv", [D, 2048]); w_branch = din("w_branch", [3072, D]); w_out = din("w_out", [D, D]); g_post = din("g_post", [D])
    flag = din("flag", [128, 1]); icnt = din("icnt", [128, 4, 16])

    y_p = dout("y_p", [TP, D]); y_s = dout("y_s", [TS, D]); h_p = dout("h_p", [1024]); conv_p = dout("conv_p", [3, 1024])
    pool_p = dout("pool_p", [15, 1024]); mk = dout("mk", [256, 1024]); mv = dout("mv", [256, 1024])
    h_s = dout("h_s", [TS, 1024]); conv_s = dout("conv_s", [TS, 3, 1024]); pool_s = dout("pool_s", [TS, 15, 1024])

    es = ExitStack()
    with es:
        NW = 51100
        arena = es.enter_context(nc.sbuf_tensor("arena", [128, NW], F32))
        off = [0]

        def carve(nwords):
            a = arena[:, off[0]:off[0] + nwords]
            off[0] += nwords
            assert off[0] <= NW, off[0]
            return a

        uT_w = carve(8320); uT = uT_w.bitcast(BF16).rearrange("p (c t) -> p c t", c=16)
        regB_w = carve(8320); regB = regB_w.bitcast(BF16)
        merged = regB.rearrange("p (c t) -> p c t", c=16)
        qT = regB[:, 0:8 * T].rearrange("p (c t) -> p c t", c=8)
        pT_all = regB[:, 8 * T:8 * T + 8192].rearrange("p (mc h t) -> p mc h t", mc=2, h=4)
        umT = regB[:, 0:4096].rearrange("p (c t) -> p c t", c=16)
        regC_w = carve(12480); regC = regC_w.bitcast(BF16)
        o_all = regC.rearrange("p (c t) -> p c t", c=24)
        uqT = regC[:, 8 * T:8 * T + 16384].rearrange("p (c t) -> p c t", c=16)
        arena_D = carve(8192)
        wsl_w = [arena_D[:, 0:4096], arena_D[:, 4096:8192]]
        wsl = [w.bitcast(BF16) for w in wsl_w]
        Fw = carve(NSLOT * SW)
        Fs = [Fw[:, i * SW:(i + 1) * SW] for i in range(NSLOT)]
        xstage = [Fw[:, 0:2048], Fw[:, 2 * SW:2 * SW + 2048]]
        kvf = [regB_w[:, 2048:4096], regB_w[:, 4096:6144]]
        kst = xstage[0].rearrange("p (m f) -> p m f", m=2)
        vst = xstage[1].rearrange("p (m f) -> p m f", m=2)
        hist = regC_w[:, 768:2688].rearrange("p (r k) -> p r k", r=15)
        osb = Fw[:, 4 * SW:4 * SW + 2048]
        KT_w = carve(1024)
        KT = KT_w.bitcast(BF16).rearrange("p (c m) -> p c m", c=8)
        Vb = carve(1024).bitcast(BF16).rearrange("p (m f) -> p m f", m=2)
        gpost_bc = regC_w[:, 0:2048]
        ident = carve(128)
        consts = carve(104)
        gpre_fm = consts[:, 0:16]; gmem_fm = consts[:, 16:32]
        convw_fm = consts[:, 32:64].rearrange("p (k c) -> p k c", k=4)
        convb_fm = consts[:, 64:72]; ba_fm = consts[:, 72:80]; bx_fm = consts[:, 80:88]; cch = consts[:, 88:96]; pscale_fm = consts[:, 96:104]
        cstage = regC_w[:, 640:768]
        dg = [carve(128) for _ in range(3)]
        ssq3 = carve(12)
        flag_t = carve(1); icnt_t = carve(64).rearrange("p (g t) -> p g t", g=4)
        wrga = carve(512).bitcast(BF16).rearrange("p (n e) -> p n e", n=8)
        wrgx = carve(512).bitcast(BF16).rearrange("p (n e) -> p n e", n=8)
        wpl = carve(1024).bitcast(BF16).rearrange("p (g dc e) -> p g dc e", g=4, dc=2)
        uqtail = carve(128).bitcast(BF16).rearrange("p (c t) -> p c t", c=16)
        sm = carve(16)
        ssq = carve(8)
        cst = regC_w[:, 0:384].rearrange("p (r k) -> p r k", r=3)
        cst_fm = carve(384).rearrange("p (r c b) -> p r c b", r=3, c=8)
        h0st = regC_w[:, 384:512]; h0_fm = carve(128).rearrange("p (c b) -> p c b", c=8)
        hsum = regC_w[:, 512:640]; hsum_fm = carve(128).rearrange("p (c b) -> p c b", c=8)
        h0p = carve(8); qtail = carve(24).rearrange("p (c r) -> p c r", c=8)
        tailx = carve(24).rearrange("p (c r) -> p c r", c=8); tailp = carve(120).rearrange("p (c r) -> p c r", c=8)
        hlast = carve(8)
        xsT_in = carve(128).rearrange("p (c b) -> p c b", c=8); xpsT_in = carve(128).rearrange("p (c b) -> p c b", c=8)
        hsT_in = carve(128).rearrange("p (c b) -> p c b", c=8)
        outT, outT2, outT3, outT4, outT5, outT6 = [regC_w[:, 2048 + 128 * i:2176 + 128 * i] for i in range(6)]
        qexp = regC_w[:, 8320:9344].bitcast(BF16).rearrange("p (h dc b j) -> p h dc b j", h=4, dc=2, b=16)
        epj = carve(512)
        ep = [epj[:, 0:256], epj[:, 256:512]]
        ps_s = regC_w[:, 9344:10368].rearrange("p (h m) -> p h m", h=4)
        pTs = carve(128).rearrange("p (mc h b) -> p mc h b", mc=2, h=4)
        os_fm = carve(128).rearrange("p (c b) -> p c b", c=8)
        junk = epj
        small15 = carve(16)

        psum = es.enter_context(nc.psum_tensor("psum", [128, 8, 512], F32))

        P = Prog(nc)
        B_uT = Buf("uT"); B_regB = Buf("regB"); B_qT = [Buf() for _ in range(8)]; B_pT = Buf("pT")
        B_o = [Buf("o%d" % i) for i in range(24)]
        B_uq = Buf("uqT")
        B_q = [Buf("q%d" % i) for i in range(4)]
        B_ws = [[B_q[0], B_q[1]], [B_q[2], B_q[3]]]
        B_F = [Buf("F%d" % i) for i in range(NSLOT)]
        B_ps = [Buf("ps%d" % i) for i in range(8)]
        B_c = {}

        def bc(name):
            if name not in B_c:
                B_c[name] = Buf(name)
            return B_c[name]

        B_out = Buf("outs")
        out_bufs = []
        rot = [0]

        def bank():
            b = rot[0]
            rot[0] = (b + 1) % 6
            return b

        aux = [0]

        def auxbank():
            b = 6 + aux[0]
            aux[0] = 1 - aux[0]
            return b

        def out_dma(dst, src, reads):
            b = Buf()
            P.dma("sp", lambda e: e.dma_start(out=dst, in_=src), reads=reads, writes=[b])
            out_bufs.append(b)

        def dump(idx, src, bufs, ncol=1040):
            if dbg_t is not None:
                out_dma(dbg_t[:, idx, 0:ncol], src, bufs)

        def load(eng, dst, src, wb, slow=False, after=()):
            if slow:
                P.dma(eng, lambda e: e.dma_start(out=dst, in_=src, allow_slow_non_contiguous=True), writes=wb, after=after)
            else:
                P.dma(eng, lambda e: e.dma_start(out=dst, in_=src), writes=wb, after=after)

        wt_state = {"n": 0}

        def wtile(src_fn):
            s = wt_state["n"] % 2
            wt_state["n"] += 1
            for (dst, src) in src_fn(wsl[s]):
                P.dma("pool", lambda e, dst=dst, src=src: e.dma_start(out=dst, in_=src), writes=[B_ws[s]])
            return s

        def win_tile(t):
            def f(slot):
                v = slot.rearrange("p (kc n) -> p kc n", kc=16)
                return [(v, w_in[:, t * 512:(t + 1) * 512].rearrange("(kc p) n -> p kc n", p=128))]
            return wtile(f)

        def wview(s):
            return wsl[s].rearrange("p (kc n) -> p kc n", kc=16)

        P.op("dve", lambda e: e.memset(ident, 0.0), writes=[bc("ident")])
        P.op("pool", lambda e: e.affine_select(out=ident, in_=ident, pattern=[[-1, 128]], compare_op=ALU.not_equal, fill=1.0, base=0, channel_multiplier=1),
             reads=[bc("ident")], writes=[bc("ident")])
        crow = [(0, 16, g_pre.rearrange("(c p) -> c p", p=128)), (16, 16, g_mem.rearrange("(c p) -> c p", p=128)),
                (32, 32, conv_w.rearrange("k (c p) -> (k c) p", p=128)), (64, 8, conv_b.rearrange("(c p) -> c p", p=128)),
                (72, 8, b_rg_a.rearrange("(c p) -> c p", p=128)), (80, 8, b_rg_x.rearrange("(c p) -> c p", p=128)),
                (88, 8, lam.rearrange("(c p) -> c p", p=128)), (96, 8, pool_scale.rearrange("(c p) -> c p", p=128))]
        for (r0, nr, src) in crow:
            load("act", cstage[r0:r0 + nr, :], src, [bc("cstage")])
        bkc = auxbank()
        P.op("pe", lambda e: e.transpose(out=psum[:, bkc, 0:104], in_=cstage[0:104, :], identity=ident[0:104, 0:104]), reads=[bc("cstage"), bc("ident")], writes=[B_ps[bkc]])
        P.op("dve", lambda e: e.tensor_copy(out=consts, in_=psum[:, bkc, 0:104]), reads=[B_ps[bkc]],
             writes=[bc(nm) for nm in ("gpre", "gmem", "convw", "convb", "ba", "bx", "cch", "pscale")])
        load("act", flag_t, flag, [bc("flag")])
        load("act", icnt_t, icnt, [bc("icnt")])
        P.dma("pool", lambda e: e.dma_start(out=wrga, in_=w_rg_a.rearrange("n d e -> d n e")), writes=[bc("wrga")])
        P.dma("pool", lambda e: e.dma_start(out=wrgx, in_=w_rg_x.rearrange("n d e -> d n e")), writes=[bc("wrgx")])
        P.dma("pool", lambda e: e.dma_start(out=wpl, in_=w_pool.rearrange("g (dc p) e -> p g dc e", p=128)), writes=[bc("wpl")])

        B_hist = [bc("hist")]
        for c in range(8):
            load("pool", cst[16 * c:16 * c + 16, :, :], st_conv[:, :, c * 128:(c + 1) * 128], [bc("cst")])
            load("pool", h0st[16 * c:16 * c + 16, :], st_h[:, c * 128:(c + 1) * 128], [bc("h0st")])
            load("pool", hist[16 * c:16 * c + 16, :, :], st_pool[:, :, c * 128:(c + 1) * 128], B_hist)
        nb = [0]

        xst3 = [xstage[0], xstage[1], Fw[:, 4 * SW:4 * SW + 2048]]
        BXS3 = [[B_F[0], B_F[1]], [B_F[2], B_F[3]], [B_F[4], B_F[5]]]

        def norm_parts(src_rows, n, dstT_fn, g_fm, gname, dstbuf, extra_after=()):
            i = nb[0] % 3
            nb[0] += 1
            xst = xst3[i]
            bx = BXS3[i]
            rstd = sm[:, i:i + 1]
            rb = bc("rstd%d" % i)
            sq = ssq3[:, 4 * i:4 * i + 4]
            sqb = bc("ssq%d" % i)
            dgi = dg[i]
            dgb = bc("dg%d" % i)

            def front():
                load("sp", xst[0:n, :], src_rows, bx)
                junk_bf = junk.bitcast(BF16)
                for j in range(2):
                    P.op("act", lambda e, j=j: e.activation(out=junk_bf[0:n, :], in_=xst[0:n, j * 1024:(j + 1) * 1024], func=AF.Square, accum_out=sq[0:n, j:j + 1]),
                         reads=bx, writes=[bc("junk"), sqb])
                P.op("dve", lambda e: e.tensor_reduce(out=rstd[0:n], in_=sq[0:n, 0:2], axis=AX.X, op=ALU.add), reads=[sqb], writes=[rb])
                P.op("dve", lambda e: e.tensor_scalar(out=rstd[0:n], in0=rstd[0:n], scalar1=1.0 / D, scalar2=EPS, op0=ALU.mult, op1=ALU.add), reads=[rb], writes=[rb])
                P.op("act", lambda e: e.activation(out=rstd[0:n], in_=rstd[0:n], func=AF.Sqrt), reads=[rb], writes=[rb])
                P.op("dve", lambda e: e.reciprocal(out=rstd[0:n], in_=rstd[0:n]), reads=[rb], writes=[rb])
                P.op("act", lambda e: e.activation(out=xst[0:n, :], in_=xst[0:n, :], func=AF.Copy, scale=rstd[0:n]), reads=bx + [rb], writes=bx)

            def back():
                for grp in range(4):
                    bk = auxbank()

                    def tr(e, grp=grp, bk=bk):
                        ins = None
                        for j in range(4):
                            c = grp * 4 + j
                            ins = e.transpose(out=psum[:, bk, j * 128:j * 128 + n], in_=xst[0:n, c * 128:(c + 1) * 128], identity=ident[0:n, 0:n])
                        return ins
                    P.op("pe", tr, reads=bx + [bc("ident")], writes=[B_ps[bk]])
                    c0 = grp * 4
                    P.op("dve", lambda e, c0=c0, bk=bk: e.tensor_tensor(out=dstT_fn(slice(c0, c0 + 4)), in0=psum[:, bk, :].rearrange("p (j m) -> p j m", j=4)[:, :, 0:n],
                                                                     in1=g_fm[:, c0:c0 + 4].unsqueeze(2).to_broadcast([128, 4, n]), op=ALU.mult),
                         reads=[B_ps[bk], bc(gname)], writes=[dstbuf], after=extra_after)
            return front, back

        def norm_many(arglist):
            parts = [norm_parts(*a) for a in arglist]
            for k in range(len(parts) + 1):
                if k < len(parts):
                    parts[k][0]()
                if k > 0:
                    parts[k - 1][1]()

        def norm_block(*a, **kw):
            norm_many([a])

        norm_many([(mem[mb * 128:(mb + 1) * 128, :], 128, (lambda c, mb=mb: umT[:, c, mb * 128:(mb + 1) * 128]), gmem_fm, "gmem", B_regB) for mb in range(2)])
        nl = [(xq[tb * 128:(tb + 1) * 128, :], 128, (lambda c, tb=tb: uqT[:, c, tb * 128:(tb + 1) * 128]), gpre_fm, "gpre", B_uq) for tb in range(8)]
        nl += [(xp[tb * 128:(tb + 1) * 128, :], 128, (lambda c, tb=tb: uT[:, c, tb * 128:(tb + 1) * 128]), gpre_fm, "gpre", B_uT) for tb in range(8)]
        nl += [(xs, 16, (lambda c: uT[:, c, 1024:1040]), gpre_fm, "gpre", B_uT)]
        norm_many(nl)
        P.op("dve", lambda e: e.tensor_copy(out=uqtail, in_=uqT[:, :, 1008:1024]), reads=[B_uq], writes=[bc("uqtail")])

        bk = auxbank()

        def tr_c(e, bk=bk):
            ins = None
            for r in range(3):
                ins = e.transpose(out=psum[:, bk, r * 128:(r + 1) * 128], in_=cst[:, r, :], identity=ident)
            return ins
        P.op("pe", tr_c, reads=[bc("cst"), bc("ident")], writes=[B_ps[bk]])
        P.op("dve", lambda e, bk=bk: e.tensor_copy(out=cst_fm.rearrange("p r c b -> p (r c b)"), in_=psum[:, bk, 0:384]), reads=[B_ps[bk]], writes=[bc("cst_fm")])
        bk = auxbank()
        P.op("pe", lambda e, bk=bk: e.transpose(out=psum[:, bk, 0:128], in_=h0st, identity=ident), reads=[bc("h0st"), bc("ident")], writes=[B_ps[bk]])
        P.op("dve", lambda e, bk=bk: e.tensor_copy(out=h0_fm.rearrange("p c b -> p (c b)"), in_=psum[:, bk, 0:128]), reads=[B_ps[bk]], writes=[bc("h0_fm")])
        for g in range(4):
            w = POOLW[g]
            if w == 2:
                P.op("dve", lambda e, g=g: e.tensor_copy(out=hsum[32 * g:32 * g + 32, :], in_=hist[32 * g:32 * g + 32, 14, :]), reads=B_hist, writes=[bc("hsum")])
            else:
                P.op("dve", lambda e, g=g, w=w: e.tensor_reduce(out=hsum[32 * g:32 * g + 32, :], in_=hist[32 * g:32 * g + 32, 16 - w:15, :].rearrange("p r k -> p k r"), axis=AX.X, op=ALU.add),
                     reads=B_hist, writes=[bc("hsum")])
        bk = auxbank()
        P.op("pe", lambda e, bk=bk: e.transpose(out=psum[:, bk, 0:128], in_=hsum, identity=ident), reads=[bc("hsum"), bc("ident")], writes=[B_ps[bk]])
        P.op("dve", lambda e, bk=bk: e.tensor_copy(out=hsum_fm.rearrange("p c b -> p (c b)"), in_=psum[:, bk, 0:128]), reads=[B_ps[bk]], writes=[bc("hsum_fm")])

        B_kvf = [Buf("kvf0"), Buf("kvf1")]
        for ct in range(4):
            s = wtile(lambda slot, ct=ct: [(slot.rearrange("p (kc n) -> p kc n", kc=16), w_kv[:, ct * 512:(ct + 1) * 512].rearrange("(kc p) n -> p kc n", p=128))])
            wv = wview(s)
            for mb in range(2):
                bk = bank()

                def mm(e, mb=mb, bk=bk, wv=wv):
                    ins = None
                    for kc in range(16):
                        ins = e.matmul(psum[:, bk, :], lhsT=umT[:, kc, mb * 128:(mb + 1) * 128], rhs=wv[:, kc, :], start=(kc == 0), stop=(kc == 15))
                    return ins
                P.op("pe", mm, reads=[B_regB, B_ws[s]], writes=[B_ps[bk]])
                P.op("act", lambda e, mb=mb, bk=bk, ct=ct: e.activation(out=kvf[mb][:, ct * 512:(ct + 1) * 512], in_=psum[:, bk, :], func=AF.Copy),
                     reads=[B_ps[bk]], writes=B_kvf)
        for mb in range(2):
            out_dma(mk[mb * 128:(mb + 1) * 128, :], kvf[mb][:, 0:1024], B_kvf)
            out_dma(mv[mb * 128:(mb + 1) * 128, :], kvf[mb][:, 1024:2048], B_kvf)
            P.op("dve", lambda e, mb=mb: e.tensor_copy(out=Vb[:, mb, :], in_=kvf[mb][:, 1024:2048]), reads=B_kvf, writes=[bc("Vb")])
            for grp in range(2):
                bk = auxbank()

                def tr(e, mb=mb, grp=grp, bk=bk):
                    ins = None
                    for j in range(4):
                        c = grp * 4 + j
                        ins = e.transpose(out=psum[:, bk, j * 128:(j + 1) * 128], in_=kvf[mb][:, c * 128:(c + 1) * 128], identity=ident)
                    return ins
                P.op("pe", tr, reads=B_kvf + [bc("ident")], writes=[B_ps[bk]])
                P.op("act", lambda e, mb=mb, grp=grp, bk=bk: e.activation(out=KT[:, grp * 4:(grp + 1) * 4, mb * 128:(mb + 1) * 128],
                                                                            in_=psum[:, bk, :].rearrange("p (j m) -> p j m", j=4), func=AF.Copy),
                     reads=[B_ps[bk]], writes=[bc("KT")])

        Fs2 = [regB_w[:, i * SW:(i + 1) * SW] for i in range(NSLOT)]
        B_F2 = [Buf("G%d" % i) for i in range(NSLOT)]
        SETS = [(Fs, B_F), (Fs2, B_F2)]
        Dq_w = [arena_D[:, k * 2048:(k + 1) * 2048] for k in range(4)]
        Dq = [w.bitcast(BF16) for w in Dq_w]
        qfree = [0, 1, 2, 3]

        def qalloc():
            return qfree.pop(0)

        def qrelease(k):
            qfree.append(k)

        P.op("act", lambda e: e.activation(out=cch, in_=cch, func=AF.Exp, scale=-1.0), reads=[bc("cch")], writes=[bc("cch")])
        P.op("act", lambda e: e.activation(out=cch, in_=cch, func=AF.Ln, bias=1.0), reads=[bc("cch")], writes=[bc("cch")])
        P.op("dve", lambda e: e.tensor_scalar(out=cch, in0=cch, scalar1=-8.0, scalar2=None, op0=ALU.mult), reads=[bc("cch")], writes=[bc("cch")])
        hba = carve(8); hbx = carve(8); hcch = carve(8)
        P.op("dve", lambda e: e.tensor_scalar(out=hba, in0=ba_fm, scalar1=0.5, scalar2=None, op0=ALU.mult), reads=[bc("ba")], writes=[bc("hba")])
        P.op("dve", lambda e: e.tensor_scalar(out=hbx, in0=bx_fm, scalar1=0.5, scalar2=None, op0=ALU.mult), reads=[bc("bx")], writes=[bc("hbx")])
        P.op("dve", lambda e: e.tensor_scalar(out=hcch, in0=cch, scalar1=0.5, scalar2=None, op0=ALU.mult), reads=[bc("cch")], writes=[bc("hcch")])

        def ucols(kc, st, sz):
            return uT[:, kc, st:st + sz]

        pend_banks = set()

        def zmm(wv, wbufs, ucols_fn, tiles, ubuf, pend=None):
            res = []
            for (st, sz) in tiles:
                bk = bank()
                if pend is not None:
                    assert bk not in pend, "PSUM bank %d re-claimed before its evacuation was queued" % bk
                    pend.add(bk)

                def mm(e, bk=bk, st=st, sz=sz):
                    ins = None
                    for kc in range(16):
                        ins = e.matmul(psum[:, bk, 0:sz], lhsT=wv[:, kc, :], rhs=ucols_fn(kc, st, sz), start=(kc == 0), stop=(kc == 15))
                    return ins
                P.op("pe", mm, reads=wbufs + [ubuf], writes=[B_ps[bk]])
                res.append((bk, st, sz))
            return res

        def z_mm(s, j, ucols_fn, tiles, ubuf):
            wv = wview(s)
            return zmm(wv[:, :, j * 128:(j + 1) * 128], [B_ws[s]], ucols_fn, tiles, ubuf)

        BPQ = [[[Buf() for _ in range(2)] for _ in range(NSLOT)] for _ in range(2)]
        BPP = [[[Buf() for _ in range(3)] for _ in range(NSLOT)] for _ in range(2)]

        def make_rg(n, is_q, ci):
            si = ci % 2
            Sl, Bl = SETS[si]
            first_after = ([B_regB] + B_kvf) if si == 1 else []
            BP = BPQ[si] if is_q else BPP[si]
            pre = list(Bl) + first_after + ([] if is_q else [b for sl in BPQ[si] for b in sl])
            st8 = {}

            def load():
                if len(qfree) < 1:
                    return False
                k = qalloc()
                st8["k"] = k
                v = Dq[k].rearrange("p (a kc n) -> p a kc n", a=2, kc=16)
                P.dma("pool", lambda e: e.dma_start(out=v[:, 0], in_=w_in[:, n * 128:(n + 1) * 128].rearrange("(kc p) n -> p kc n", p=128)), writes=[B_q[k]])
                if not is_q:
                    P.dma("pool", lambda e: e.dma_start(out=v[:, 1], in_=w_in[:, 1024 + n * 128:1024 + (n + 1) * 128].rearrange("(kc p) n -> p kc n", p=128)), writes=[B_q[k]])
                return True

            def gen():
                k = st8["k"]
                v = Dq[k].rearrange("p (a kc n) -> p a kc n", a=2, kc=16)
                X, C, CB, R, I, M = Sl[0], Sl[1], Sl[2].bitcast(BF16), Sl[3], Sl[4], Sl[5]
                BX, BC, BCB, BR, BI, BM = BP
                parts = QT if is_q else TILES
                NP = len(parts)
                LP = 1024
                seq = [(st, min(st + sz, LP)) for (st, sz) in parts]
                if is_q:
                    xb = zmm(v[:, 0], [B_q[k]], lambda kc, st, sz: uqT[:, kc, st:st + sz], QT, B_uq, pend_banks)
                else:
                    xb = zmm(v[:, 0], [B_q[k]], ucols, TILES, B_uT, pend_banks)
                yield
                gb = [] if is_q else zmm(v[:, 1], [B_q[k]], ucols, TILES, B_uT, pend_banks)
                qrelease(k)
                yield
                if is_q:
                    P.op("dve", lambda e: e.memset(X[:, 0:3], 0.0), writes=[BX[0]], after=pre)
                else:
                    P.op("dve", lambda e: e.tensor_copy(out=X[:, 0:3], in_=qtail[:, n, :]), reads=[bc("qtail")], writes=[BX[0]], after=pre)
                for p, (bk, st, sz) in enumerate(xb):
                    P.op("act", lambda e, bk=bk, st=st, sz=sz: e.activation(out=X[:, 3 + st:3 + st + sz], in_=psum[:, bk, 0:sz], func=AF.Copy), reads=[B_ps[bk]], writes=[BX[p]], after=pre)
                    pend_banks.discard(bk)
                yield
                for p, (st, sz) in enumerate(parts):
                    P.op("dve", lambda e, st=st, sz=sz: e.tensor_scalar(out=C[:, st:st + sz], in0=X[:, 3 + st:3 + st + sz], scalar1=convw_fm[:, 3, n:n + 1], scalar2=convb_fm[:, n:n + 1], op0=ALU.mult, op1=ALU.add),
                         reads=[BX[p], bc("convw"), bc("convb")], writes=[BC[p]], after=pre)
                yield
                for kk in range(3):
                    for p, (st, en) in enumerate(seq):
                        rd = [BX[p], BC[p], bc("convw")] + ([BX[p - 1]] if p > 0 else [])
                        P.op("dve", lambda e, kk=kk, st=st, en=en: e.scalar_tensor_tensor(out=C[:, st:en], in0=X[:, st + kk:en + kk], scalar=convw_fm[:, kk, n:n + 1], in1=C[:, st:en], op0=ALU.mult, op1=ALU.add),
                             reads=rd, writes=[BC[p]], safe=True)
                    yield
                lastp = NP - 1
                if is_q:
                    P.op("dve", lambda e: e.tensor_copy(out=qtail[:, n, :], in_=X[:, 3 + 1021:3 + 1024]), reads=[BX[lastp]], writes=[bc("qtail")])
                else:
                    P.op("dve", lambda e: e.tensor_copy(out=tailx[:, n, :], in_=X[:, 3 + 1021:3 + 1024]), reads=[BX[lastp]], writes=[bc("tailx")])
                    P.op("dve", lambda e: e.tensor_copy(out=xsT_in[:, n, :], in_=X[:, 3 + 1024:3 + 1040]), reads=[BX[lastp]], writes=[bc("xsT_in")])
                    for kk in range(3):
                        P.op("dve", lambda e, kk=kk: e.scalar_tensor_tensor(out=C[:, 1024:1040], in0=cst_fm[:, kk, n, :], scalar=convw_fm[:, kk, n:n + 1], in1=C[:, 1024:1040], op0=ALU.mult, op1=ALU.add),
                             reads=[bc("cst_fm"), BC[lastp], bc("convw")], writes=[BC[lastp]])
                yield
                for p, (st, sz) in enumerate(parts):
                    P.op("act", lambda e, st=st, sz=sz: e.activation(out=CB[:, st:st + sz], in_=C[:, st:st + sz], func=AF.Copy), reads=[BC[p]], writes=[BCB[p]], after=pre)
                for p, (bk, st, sz) in enumerate(gb):
                    ex = [BX[p + 1]] if p + 1 < NP else []
                    P.op("act", lambda e, bk=bk, st=st, sz=sz: e.activation(out=X[:, 3 + st:3 + st + sz], in_=psum[:, bk, 0:sz], func=AF.Silu), reads=[B_ps[bk]], writes=[BX[p]], after=ex)
                    pend_banks.discard(bk)
                yield
                for (wg, hb, bname, dst, bdst, wname) in ((wrga, hba, "hba", R, BR, "wrga"), (wrgx, hbx, "hbx", I, BI, "wrgx")):
                    for p, (st, sz) in enumerate(parts):
                        bk = auxbank()
                        P.op("pe", lambda e, bk=bk, st=st, sz=sz, wg=wg: e.matmul(psum[:, bk, 0:sz], lhsT=wg[:, n, :], rhs=CB[:, st:st + sz], start=True, stop=True),
                             reads=[BCB[p], bc(wname)], writes=[B_ps[bk]])
                        P.op("act", lambda e, bk=bk, st=st, sz=sz, dst=dst, hb=hb: e.activation(out=dst[:, st:st + sz], in_=psum[:, bk, 0:sz], func=AF.Tanh, bias=hb[:, n:n + 1], scale=0.5),
                             reads=[B_ps[bk], bc(bname)], writes=[bdst[p]], after=pre)
                yield
                for p, (st, sz) in enumerate(parts):
                    P.op("act", lambda e, st=st, sz=sz: e.activation(out=M[:, st:st + sz], in_=R[:, st:st + sz], func=AF.Exp, scale=cch[:, n:n + 1], bias=cch[:, n:n + 1]), reads=[BR[p], bc("cch")], writes=[BM[p]], after=pre)
                for p, (st, sz) in enumerate(parts):
                    P.op("act", lambda e, st=st, sz=sz: e.activation(out=R[:, st:st + sz], in_=R[:, st:st + sz], func=AF.Exp, scale=hcch[:, n:n + 1], bias=hcch[:, n:n + 1]), reads=[BR[p], bc("hcch")], writes=[BR[p]])
                yield
                for p, (st, sz) in enumerate(parts):
                    P.op("act", lambda e, st=st, sz=sz: e.activation(out=M[:, st:st + sz], in_=M[:, st:st + sz], func=AF.Sqrt, scale=-1.0, bias=1.0), reads=[BM[p]], writes=[BM[p]])
                yield
                for p, (st, sz) in enumerate(parts):
                    P.op("dve", lambda e, st=st, sz=sz: e.scalar_tensor_tensor(out=M[:, st:st + sz], in0=I[:, st:st + sz], scalar=1.0, in1=M[:, st:st + sz], op0=ALU.add, op1=ALU.mult), reads=[BM[p], BI[p]], writes=[BM[p]])
                yield
                for p, (st, sz) in enumerate(parts):
                    P.op("dve", lambda e, st=st, sz=sz: e.scalar_tensor_tensor(out=M[:, st:st + sz], in0=M[:, st:st + sz], scalar=0.5, in1=C[:, st:st + sz], op0=ALU.mult, op1=ALU.mult), reads=[BM[p], BC[p]], writes=[BM[p]])
                yield
                for p, (st, en) in enumerate(seq):
                    if p == 0:
                        init = 0.0 if is_q else h0p[:, n:n + 1]
                        rd = [BR[p], BM[p]] + ([] if is_q else [bc("h0p")])
                    else:
                        init = C[:, st - 1:st]
                        rd = [BR[p], BM[p], BC[p - 1]]
                    P.op("dve", lambda e, st=st, en=en, init=init: e.tensor_tensor_scan(out=C[:, st:en], data0=R[:, st:en], data1=M[:, st:en], initial=init, op0=ALU.mult, op1=ALU.add),
                         reads=rd, writes=[BC[p]])
                if is_q:
                    P.op("dve", lambda e: e.tensor_scalar(out=h0p[:, n:n + 1], in0=C[:, LP - 1:LP], scalar1=flag_t[:, 0:1], scalar2=None, op0=ALU.mult), reads=[BC[lastp], bc("flag")], writes=[bc("h0p")])
                else:
                    P.op("dve", lambda e: e.tensor_tensor(out=C[:, 1024:1040], in0=R[:, 1024:1040], in1=h0_fm[:, n, :], op=ALU.mult), reads=[BR[lastp], bc("h0_fm")], writes=[BC[lastp]])
                    P.op("dve", lambda e: e.tensor_tensor(out=C[:, 1024:1040], in0=C[:, 1024:1040], in1=M[:, 1024:1040], op=ALU.add), reads=[BC[lastp], BM[lastp]], writes=[BC[lastp]])
                    P.op("dve", lambda e: e.tensor_copy(out=hlast[:, n:n + 1], in_=C[:, 1023:1024]), reads=[BC[lastp]], writes=[bc("hlast")])
                    P.op("dve", lambda e: e.tensor_copy(out=hsT_in[:, n, :], in_=C[:, 1024:1040]), reads=[BC[lastp]], writes=[bc("hsT_in")])
                    yield
                    if n == 0:
                        dump(0, C[:, 0:1040], list(BC)); dump(1, X[:, 3:3 + 1040], list(BX)); dump(2, R[:, 0:1040], list(BR)); dump(3, M[:, 0:1040], list(BM))
                    for p, (st, sz) in enumerate(parts):
                        P.op("dve", lambda e, st=st, sz=sz: e.tensor_tensor(out=o_all[:, n, st:st + sz], in0=C[:, st:st + sz], in1=X[:, 3 + st:3 + st + sz], op=ALU.mult), reads=[BX[p], BC[p]], writes=[B_o[n]],
                             after=[bc("cst"), bc("h0st"), bc("hsum"), bc("cstage"), bc("hist")])
            return (load, gen)

        def rg_fence(si):
            allp = [b for sl in BPQ[si] for b in sl] + [b for sl in BPP[si] for b in sl]
            P.op("dve", lambda e: e.memset(small15[:, 1:2], 0.0), reads=allp, writes=[bc("small15")] + list(SETS[si][1]))

        def make_pool(g, ci):
            Sl, Bl = SETS[ci % 2]
            first_after = ([B_regB] + B_kvf) if ci % 2 == 1 else []
            w = POOLW[g]
            st8 = {}

            def load():
                if len(qfree) < 2:
                    return False
                kx = qalloc(); kg = qalloc()
                st8["kx"], st8["kg"] = kx, kg
                vx = Dq[kx].rearrange("p (kc n) -> p kc n", kc=16)
                vg = Dq[kg].rearrange("p (kc n) -> p kc n", kc=16)
                P.dma("pool", lambda e: e.dma_start(out=vx, in_=w_in[:, 2048 + g * 256:2048 + (g + 1) * 256].rearrange("(kc p) n -> p kc n", p=128)), writes=[B_q[kx]])
                P.dma("pool", lambda e: e.dma_start(out=vg, in_=w_in[:, 3072 + g * 256:3072 + (g + 1) * 256].rearrange("(kc p) n -> p kc n", p=128)), writes=[B_q[kg]])
                return True

            def gen():
                kx, kg = st8["kx"], st8["kg"]
                vx = Dq[kx].rearrange("p (kc n) -> p kc n", kc=16)
                vg = Dq[kg].rearrange("p (kc n) -> p kc n", kc=16)
                X, SA, SBt, G = Sl[0], Sl[1], Sl[2], Sl[4]
                DB = Sl[3].bitcast(BF16)
                BX, BSA, BSB, BDB, BG = Bl[0], Bl[1], Bl[2], Bl[3], Bl[4]
                E = 15 + 1024
                for eo in range(2):
                    c = 2 * g + eo
                    wvx = vx[:, :, eo * 128:(eo + 1) * 128]
                    banks = zmm(wvx, [B_q[kx]], ucols, TILES, B_uT)
                    bkh = auxbank()

                    def mmh(e, bkh=bkh, wvx=wvx):
                        ins = None
                        for kc in range(16):
                            ins = e.matmul(psum[:, bkh, 0:16], lhsT=wvx[:, kc, :], rhs=uqtail[:, kc, :], start=(kc == 0), stop=(kc == 15))
                        return ins
                    P.op("pe", mmh, reads=[B_q[kx], bc("uqtail")], writes=[B_ps[bkh]])
                    if eo == 1:
                        qrelease(kx)
                    yield
                    P.op("act", lambda e, bkh=bkh: e.activation(out=X[:, 0:15], in_=psum[:, bkh, 1:16], func=AF.Copy), reads=[B_ps[bkh]], writes=[BX], after=first_after)
                    for (bk, st, sz) in banks:
                        P.op("act", lambda e, bk=bk, st=st, sz=sz: e.activation(out=X[:, 15 + st:15 + st + sz], in_=psum[:, bk, 0:sz], func=AF.Copy), reads=[B_ps[bk]], writes=[BX])
                    yield
                    P.op("dve", lambda e, c=c: e.tensor_copy(out=tailp[:, c, :], in_=X[:, 15 + 1009:15 + 1024]), reads=[BX], writes=[bc("tailp")])
                    P.op("dve", lambda e, c=c: e.tensor_copy(out=xpsT_in[:, c, :], in_=X[:, 15 + 1024:15 + 1040]), reads=[BX], writes=[bc("xpsT_in")])
                    P.op("dve", lambda e: e.tensor_tensor(out=SA[:, 1:E], in0=X[:, 1:E], in1=X[:, 0:E - 1], op=ALU.add), reads=[BX], writes=[BSA], after=first_after)
                    yield
                    cur, curb, oth, othb = SA, BSA, SBt, BSB
                    sh = 2
                    lo = 1
                    while sh < w:
                        lo2 = lo + sh
                        P.op("dve", lambda e, cur=cur, oth=oth, lo2=lo2, sh=sh: e.tensor_tensor(out=oth[:, lo2:E], in0=cur[:, lo2:E], in1=cur[:, lo2 - sh:E - sh], op=ALU.add),
                             reads=[curb], writes=[othb], after=first_after)
                        cur, curb, oth, othb = oth, othb, cur, curb
                        lo = lo2
                        sh *= 2
                        yield
                    dcol = eo * SW
                    P.op("dve", lambda e, cur=cur, dcol=dcol: e.scalar_tensor_tensor(out=DB[:, dcol:dcol + 1024], in0=cur[:, 15:E], scalar=1.0 / w, in1=X[:, 15:E], op0=ALU.mult, op1=ALU.subtract),
                         reads=[curb, BX], writes=[BDB], after=first_after)
                    P.op("dve", lambda e, cur=cur: e.tensor_tensor(out=small15[:, 0:15], in0=cur[:, 15:30], in1=icnt_t[:, g, 0:15], op=ALU.mult), reads=[curb, bc("icnt")], writes=[bc("small15")])
                    P.op("dve", lambda e, dcol=dcol: e.tensor_tensor(out=DB[:, dcol:dcol + 15], in0=small15[:, 0:15], in1=X[:, 15:30], op=ALU.subtract), reads=[bc("small15"), BX], writes=[BDB])
                    P.op("dve", lambda e, c=c: e.tensor_tensor(out=small15[:, 0:16], in0=hsum_fm[:, c, :], in1=X[:, E:E + 16], op=ALU.add), reads=[bc("hsum_fm"), BX], writes=[bc("small15")])
                    P.op("dve", lambda e, dcol=dcol: e.scalar_tensor_tensor(out=DB[:, dcol + 1024:dcol + 1040], in0=small15[:, 0:16], scalar=1.0 / w, in1=X[:, E:E + 16], op0=ALU.mult, op1=ALU.subtract),
                         reads=[bc("small15"), BX], writes=[BDB])
                    yield
                for eo in range(2):
                    c = 2 * g + eo
                    gbanks = zmm(vg[:, :, eo * 128:(eo + 1) * 128], [B_q[kg]], ucols, TILES, B_uT)
                    if eo == 1:
                        qrelease(kg)
                    yield
                    for (bk, st, sz) in gbanks:
                        P.op("act", lambda e, bk=bk, st=st, sz=sz: e.activation(out=G[:, st:st + sz], in_=psum[:, bk, 0:sz], func=AF.Silu), reads=[B_ps[bk]], writes=[BG], after=first_after)
                    yield
                    for (st, sz) in TILES:
                        bk = bank()

                        def mmp(e, bk=bk, st=st, sz=sz, eo=eo):
                            ins = None
                            for dc in range(2):
                                ins = e.matmul(psum[:, bk, 0:sz], lhsT=wpl[:, g, dc, eo * 128:(eo + 1) * 128], rhs=DB[:, dc * SW + st:dc * SW + st + sz], start=(dc == 0), stop=(dc == 1))
                            return ins
                        P.op("pe", mmp, reads=[BDB, bc("wpl")], writes=[B_ps[bk]])
                        P.op("dve", lambda e, bk=bk, st=st, sz=sz, c=c: e.scalar_tensor_tensor(out=o_all[:, 8 + c, st:st + sz], in0=psum[:, bk, 0:sz], scalar=pscale_fm[:, c:c + 1], in1=G[:, st:st + sz], op0=ALU.mult, op1=ALU.mult),
                             reads=[B_ps[bk], BG, bc("pscale")], writes=[B_o[8 + c]], after=[B_uq])
                    yield
            return (load, gen)

        def run_pipeline(chains, depth=2, skew=9, lookahead=2):
            loaded = 0
            active = []
            nxt = 0
            while nxt < len(chains) or active:
                while loaded < len(chains) and loaded <= nxt + lookahead:
                    if not chains[loaded][0]():
                        break
                    loaded += 1
                if nxt < len(chains) and nxt < loaded and len(active) < depth and (not active or active[-1][1] >= skew):
                    active.append([chains[nxt][1](), 0])
                    nxt += 1
                assert active, "pipeline stalled"
                for a_ in list(active):
                    try:
                        next(a_[0])
                        a_[1] += 1
                    except StopIteration:
                        active.remove(a_)

        chains = []
        ci = 0
        for n in range(8):
            chains.append(make_rg(n, True, ci)); ci += 1
        for n in range(8):
            chains.append(make_rg(n, False, ci)); ci += 1
        def run_seq(chains, skew):
            loaded = 0
            nxt = 0
            cur = None
            pre = None
            steps = 0
            while True:
                while loaded < len(chains) and loaded <= nxt + 1:
                    if not chains[loaded][0]():
                        break
                    loaded += 1
                if cur is None:
                    if pre is not None:
                        cur, pre, steps = pre, None, 1
                    elif nxt < len(chains):
                        assert nxt < loaded
                        cur = chains[nxt][1]()
                        nxt += 1
                        next(cur)
                        steps = 1
                    else:
                        break
                try:
                    next(cur)
                    steps += 1
                except StopIteration:
                    cur = None
                    continue
                if steps == skew and pre is None and nxt < len(chains) and nxt < loaded:
                    pre = chains[nxt][1]()
                    nxt += 1
                    next(pre)

        def run_rg_sched(chains):
            N = len(chains)
            gens = [None] * N
            loaded = [0]

            def ensure(j, must):
                while loaded[0] <= min(j, N - 1):
                    if not chains[loaded[0]][0]():
                        break
                    loaded[0] += 1
                if must:
                    assert loaded[0] > j, "weights for chain %d could not be queued" % j

            def G(i):
                if i < 0 or i >= N:
                    return None
                if gens[i] is None:
                    ensure(i, True)
                    gens[i] = chains[i][1]()
                return gens[i]

            def st(g, n=1):
                if g is None:
                    return
                for _ in range(n):
                    try:
                        next(g)
                    except StopIteration:
                        return

            st(G(0), 2)
            for i in range(N + 1):
                A = gens[i - 1] if i >= 1 else None
                B = G(i) if i < N else None
                C = G(i + 1) if i + 1 < N else None
                ensure(i + 2, False)
                st(B, 1)
                st(C, 1)
                st(A, 2)
                st(B, 5)
                st(A, 2)
                st(B, 2)
                st(A, 2)
                st(C, 1)

        run_rg_sched(chains)
        rg_fence(0)
        rg_fence(1)
        chains = []
        for g in range(4):
            chains.append(make_pool(g, ci)); ci += 1
        run_pipeline(chains)

        for t in range(2):
            s_q = win_tile(8 + t)
            for j in range(4):
                c = t * 4 + j
                banks = z_mm(s_q, j, ucols, TILES, B_uT)
                for (bk, st, sz) in banks:
                    P.op("act", lambda e, bk=bk, st=st, sz=sz, c=c: e.activation(out=qT[:, c, st:st + sz], in_=psum[:, bk, 0:sz], func=AF.Copy), reads=[B_ps[bk]], writes=[B_qT[c]], after=[B_regB] + B_F2 + B_kvf)
        s_gx = [win_tile(10), win_tile(11)]
        SC = 1.0 / 16.0

        ep3 = [ep[0], ep[1], carve(256)]
        smx = carve(8)
        ab8 = [0]

        def abank8():
            b_ = ab8[0]
            ab8[0] = (b_ + 1) % 8
            return b_

        def attn_stages(it, h, tb):
            i3 = it % 3
            e_t = ep3[i3]
            be = bc("ep3_%d" % i3)
            mx = smx[:, i3:i3 + 1]
            sume = smx[:, 3 + i3:4 + i3]
            bm = bc("mx3_%d" % i3)
            bs_ = bc("sume3_%d" % i3)
            stt = {}

            def s1():
                bk = abank8()
                stt["bk"] = bk

                def mms(e):
                    ins = None
                    for dc in range(2):
                        ins = e.matmul(psum[:, bk, 0:256], lhsT=qT[:, 2 * h + dc, tb * 128:(tb + 1) * 128], rhs=KT[:, 2 * h + dc, :], start=(dc == 0), stop=(dc == 1))
                    return ins
                P.op("pe", mms, reads=[B_qT[2 * h], B_qT[2 * h + 1], bc("KT")], writes=[B_ps[bk]])

            def s2():
                bk = stt["bk"]
                P.op("dve", lambda e: e.reduce_max(out=mx, in_=psum[:, bk, 0:256], axis=AX.X), reads=[B_ps[bk]], writes=[bm])
                P.op("dve", lambda e: e.tensor_scalar(out=mx, in0=mx, scalar1=-SC, scalar2=None, op0=ALU.mult), reads=[bm], writes=[bm])
                P.op("act", lambda e: e.activation(out=e_t, in_=psum[:, bk, 0:256], func=AF.Exp, scale=SC, bias=mx, accum_out=sume),
                     reads=[B_ps[bk], bm], writes=[be, bs_])

            def s3():
                P.op("dve", lambda e: e.reciprocal(out=sume, in_=sume), reads=[bs_], writes=[bs_])
                P.op("dve", lambda e: e.tensor_scalar(out=e_t, in0=e_t, scalar1=sume, scalar2=None, op0=ALU.mult), reads=[be, bs_], writes=[be])
                bk2 = abank8()
                stt["bk2"] = bk2

                def trp(e):
                    ins = None
                    for mc in range(2):
                        ins = e.transpose(out=psum[:, bk2, mc * 128:(mc + 1) * 128], in_=e_t[:, mc * 128:(mc + 1) * 128], identity=ident)
                    return ins
                P.op("pe", trp, reads=[be, bc("ident")], writes=[B_ps[bk2]])

            def s4():
                bk2 = stt["bk2"]
                P.op("act", lambda e: e.activation(out=pT_all[:, :, h, tb * 128:(tb + 1) * 128], in_=psum[:, bk2, 0:256].rearrange("p (mc t) -> p mc t", mc=2), func=AF.Copy),
                     reads=[B_ps[bk2]], writes=[B_pT], after=[B_regB] + B_F2 + B_kvf)
            return (s1, s2, s3, s4)

        astg = [attn_stages(h * 8 + tb, h, tb) for h in range(4) for tb in range(8)]
        NA = len(astg)
        for r in range(NA + 3):
            for d in range(4):
                k = r - d
                if 0 <= k < NA:
                    astg[k][d]()

        P.op("dve", lambda e: e.memset(qexp.rearrange("p h dc b j -> p (h dc b j)"), 0.0), writes=[bc("qexp")], after=[B_uq])
        for h in range(4):
            for dc in range(2):
                P.op("dve", lambda e, h=h, dc=dc: e.tensor_copy(out=qexp[:, h, dc, :, :].rearrange("p b j -> p (b j)")[:, 0:256:17], in_=qT[:, 2 * h + dc, 1024:1040]),
                     reads=[B_qT[2 * h + dc], bc("qexp")], writes=[bc("qexp")])
        kstb = [xstage[0][:, i * 1024:(i + 1) * 1024].bitcast(BF16).rearrange("p (m f) -> p m f", m=2) for i in range(2)]
        vstb = [xstage[1][:, i * 1024:(i + 1) * 1024].bitcast(BF16).rearrange("p (m f) -> p m f", m=2) for i in range(2)]
        B_kb = [Buf("kb0"), Buf("kb1")]
        B_vb = [Buf("vb0"), Buf("vb1")]
        ident_bf = carve(64).bitcast(BF16)
        P.op("dve", lambda e: e.tensor_copy(out=ident_bf, in_=ident), reads=[bc("ident")], writes=[bc("ident_bf")])
        pTs_bf = carve(64).bitcast(BF16).rearrange("p (mc h b) -> p mc h b", mc=2, h=4)
        KTs = Fs[4].bitcast(BF16)[:, 0:2048].rearrange("p (c m) -> p c m", c=8)
        psum_bf = [psum[:, 6, :].bitcast(BF16), psum[:, 7, :].bitcast(BF16)]
        KTs2 = [KTs, Fs[5].bitcast(BF16)[:, 0:2048].rearrange("p (c m) -> p c m", c=8)]
        BK2 = [B_F[4], B_F[5]]
        pbf = {bk_: psum[:, bk_, :].bitcast(BF16) for bk_ in (4, 5, 6, 7)}

        def s_T(b):
            kb = kstb[b % 2]
            kt = KTs2[b % 2]
            P.dma("pool", lambda e: e.dma_start(out=kb, in_=ck[b].rearrange("(mc p) f -> p mc f", p=128)), writes=[B_kb[b % 2]],
                  after=([B_F[0], B_F[1]] if b < 2 else []))
            for mc in range(2):
                bk = (4 + mc) if b % 2 == 0 else (6 + mc)
                pb = pbf[bk]

                def trk(e, mc=mc, pb=pb):
                    ins = None
                    for c in range(8):
                        ins = e.transpose(out=pb[:, c * 128:(c + 1) * 128], in_=kb[:, mc, c * 128:(c + 1) * 128], identity=ident_bf)
                    return ins
                P.op("pe", trk, reads=[B_kb[b % 2], bc("ident_bf")], writes=[B_ps[bk]])
                if mc == 0:
                    P.op("act", lambda e, mc=mc, pb=pb: e.activation(out=kt[:, :, mc * 128:(mc + 1) * 128], in_=pb.rearrange("p (j m) -> p j m", j=8), func=AF.Copy),
                         reads=[B_ps[bk]], writes=[BK2[b % 2]])
                else:
                    P.op("dve", lambda e, mc=mc, pb=pb: e.tensor_copy(out=kt[:, :, mc * 128:(mc + 1) * 128], in_=pb.rearrange("p (j m) -> p j m", j=8)),
                         reads=[B_ps[bk]], writes=[BK2[b % 2]])

        def s_M(b):
            kt = KTs2[b % 2]
            for h in range(4):
                def mmq(e, h=h):
                    ins = None
                    for dc in range(2):
                        ins = e.matmul(psum[0:16, h, 0:256], lhsT=qexp[:, h, dc, b, :], rhs=kt[:, 2 * h + dc, :], start=(b == 0 and dc == 0), stop=(b == TS - 1 and dc == 1))
                    return ins
                P.op("pe", mmq, reads=[BK2[b % 2], bc("qexp")], writes=[B_ps[h]])

        s_T(0)
        for b in range(TS):
            if b + 1 < TS:
                s_T(b + 1)
            s_M(b)
        for h in range(4):
            mx = sm[0:16, 8:9]
            sume = sm[0:16, 9:10]
            P.op("dve", lambda e, h=h: e.reduce_max(out=mx, in_=psum[0:16, h, 0:256], axis=AX.X), reads=[B_ps[h]], writes=[bc("mxs")])
            P.op("dve", lambda e: e.tensor_scalar(out=mx, in0=mx, scalar1=-SC, scalar2=None, op0=ALU.mult), reads=[bc("mxs")], writes=[bc("mxs")])
            P.op("act", lambda e, h=h: e.activation(out=ps_s[0:16, h, :], in_=psum[0:16, h, 0:256], func=AF.Exp, scale=SC, bias=mx, accum_out=sume), reads=[B_ps[h], bc("mxs")], writes=[bc("ps_s"), bc("sumes")], after=[B_uq])
            P.op("dve", lambda e: e.reciprocal(out=sume, in_=sume), reads=[bc("sumes")], writes=[bc("sumes")])
            P.op("dve", lambda e, h=h: e.tensor_scalar(out=ps_s[0:16, h, :], in0=ps_s[0:16, h, :], scalar1=sume, scalar2=None, op0=ALU.mult), reads=[bc("ps_s"), bc("sumes")], writes=[bc("ps_s")])
        bk = auxbank()

        def trps(e, bk=bk):
            ins = None
            for mc in range(2):
                for h in range(4):
                    ins = e.transpose(out=psum[:, bk, (mc * 4 + h) * 16:(mc * 4 + h) * 16 + 16], in_=ps_s[0:16, h, mc * 128:(mc + 1) * 128], identity=ident[0:16, 0:16])
            return ins
        P.op("pe", trps, reads=[bc("ps_s"), bc("ident")], writes=[B_ps[bk]])
        P.op("dve", lambda e, bk=bk: e.tensor_copy(out=pTs.rearrange("p mc h b -> p (mc h b)"), in_=psum[:, bk, 0:128]), reads=[B_ps[bk]], writes=[bc("pTs")])
        P.op("dve", lambda e: e.tensor_copy(out=pTs_bf.rearrange("p mc h b -> p (mc h b)"), in_=pTs.rearrange("p mc h b -> p (mc h b)")), reads=[bc("pTs")], writes=[bc("pTs_bf")])
        G_s = dg[0].rearrange("p (c b) -> p c b", c=8)

        def gx_chunk(c):
            t, j = c // 4, c % 4
            h = c // 2
            gbanks = z_mm(s_gx[t], j, ucols, TILES, B_uT)
            G = Fs[5]
            for (bk, st, sz) in gbanks:
                P.op("act", lambda e, bk=bk, st=st, sz=sz: e.activation(out=G[:, st:st + sz], in_=psum[:, bk, 0:sz], func=AF.Silu), reads=[B_ps[bk]], writes=[B_F[5]])
            P.op("dve", lambda e: e.tensor_copy(out=G_s[:, c, :], in_=G[:, 1024:1040]), reads=[B_F[5]], writes=[bc("G_s")])
            for tt in range(2):
                bk = bank()

                def mmpv(e, bk=bk, tt=tt):
                    ins = None
                    for mc in range(2):
                        ins = e.matmul(psum[:, bk, :], lhsT=Vb[:, mc, c * 128:(c + 1) * 128], rhs=pT_all[:, mc, h, tt * 512:(tt + 1) * 512], start=(mc == 0), stop=(mc == 1))
                    return ins
                P.op("pe", mmpv, reads=[bc("Vb"), B_pT], writes=[B_ps[bk]])
                P.op("dve", lambda e, bk=bk, tt=tt: e.tensor_tensor(out=o_all[:, 16 + c, tt * 512:(tt + 1) * 512], in0=psum[:, bk, :], in1=G[:, tt * 512:(tt + 1) * 512], op=ALU.mult),
                     reads=[B_ps[bk], B_F[5]], writes=[B_o[16 + c]], after=[B_uq, bc("qexp"), bc("ps_s")])

        bko = auxbank()
        vst4 = [vstb[0], vstb[1], kstb[0], kstb[1]]
        B_v4 = [B_vb[0], B_vb[1], B_kb[0], B_kb[1]]
        for b in range(TS):
            vb = vst4[b % 4]
            P.dma("pool", lambda e, vb=vb, b=b: e.dma_start(out=vb, in_=cv[b].rearrange("(mc p) f -> p mc f", p=128)), writes=[B_v4[b % 4]],
                  after=([B_F[2], B_F[3]] if b < 2 else []))

            def mmv(e, b=b, bko=bko, vb=vb):
                ins = None
                for c in range(8):
                    for mc in range(2):
                        ins = e.matmul(psum[:, bko, c * 16 + b:c * 16 + b + 1], lhsT=vb[:, mc, c * 128:(c + 1) * 128], rhs=pTs_bf[:, mc, c // 2, b:b + 1], start=(mc == 0), stop=(mc == 1))
                return ins
            P.op("pe", mmv, reads=[B_v4[b % 4], bc("pTs_bf")], writes=[B_ps[bko]])
            if b % 2 == 1:
                gx_chunk(b // 2)
        P.op("dve", lambda e: e.memset(small15[:, 0:1], 0.0), reads=B_kb + B_vb, writes=[bc("small15")] + B_F[0:4])
        P.op("dve", lambda e, bko=bko: e.tensor_copy(out=os_fm.rearrange("p c b -> p (c b)"), in_=psum[:, bko, 0:128]), reads=[B_ps[bko]], writes=[bc("os_fm")])
        P.op("dve", lambda e: e.tensor_tensor(out=o_all[:, 16:24, 1024:1040], in0=os_fm, in1=G_s, op=ALU.mult), reads=[bc("os_fm"), bc("G_s")], writes=B_o[16:24],
             after=[B_uq, bc("qexp"), bc("ps_s")])

        outT, outT2, outT3, outT4, outT5, outT6 = [KT_w[:, 128 * i:128 * (i + 1)] for i in range(6)]

        def fm_out(src2d, ncols, stage, emit_dmas, rbufs):
            bk = auxbank()
            P.op("pe", lambda e, bk=bk: e.transpose(out=psum[0:ncols, bk, 0:128], in_=src2d, identity=ident), reads=rbufs + [bc("ident")], writes=[B_ps[bk]])
            sb_ = Buf()
            P.op("dve", lambda e, bk=bk: e.tensor_copy(out=stage[0:ncols, :], in_=psum[0:ncols, bk, 0:128]), reads=[B_ps[bk]], writes=[sb_], after=[bc("KT")])
            emit_dmas(sb_)

        out_dma(conv_s[:, 0:2, :], st_conv[:, 1:3, :], [])
        out_dma(pool_s[:, 0:14, :], st_pool[:, 1:15, :], [])
        fm_out(hlast, 8, outT, lambda sb_: out_dma(h_p.rearrange("(c k) -> c k", k=128), outT[0:8, :], [sb_]), [bc("hlast")])
        fm_out(tailx.rearrange("p c r -> p (c r)"), 24, outT2,
               lambda sb_: [out_dma(conv_p[:, c * 128:(c + 1) * 128], outT2[3 * c:3 * c + 3, :], [sb_]) for c in range(8)], [bc("tailx")])
        fm_out(tailp.rearrange("p c r -> p (c r)"), 120, outT3,
               lambda sb_: [out_dma(pool_p[:, c * 128:(c + 1) * 128], outT3[15 * c:15 * c + 15, :], [sb_]) for c in range(8)], [bc("tailp")])
        fm_out(hsT_in.rearrange("p c b -> p (c b)"), 128, outT4,
               lambda sb_: [out_dma(h_s[:, c * 128:(c + 1) * 128], outT4[16 * c:16 * c + 16, :], [sb_]) for c in range(8)], [bc("hsT_in")])
        fm_out(xsT_in.rearrange("p c b -> p (c b)"), 128, outT5,
               lambda sb_: [out_dma(conv_s[:, 2, c * 128:(c + 1) * 128], outT5[16 * c:16 * c + 16, :], [sb_]) for c in range(8)], [bc("xsT_in")])
        fm_out(xpsT_in.rearrange("p c b -> p (c b)"), 128, outT6,
               lambda sb_: [out_dma(pool_s[:, 14, c * 128:(c + 1) * 128], outT6[16 * c:16 * c + 16, :], [sb_]) for c in range(8)], [bc("xpsT_in")])

        B_merged = Buf("merged")
        Wsl = [Fw[:, 3 * SW + k * 1536:3 * SW + (k + 1) * 1536].bitcast(BF16).rearrange("p (kc n) -> p kc n", kc=24) for k in range(2)]
        B_W2 = [Buf("W2a"), Buf("W2b")]

        def load_phaseB(f):
            s_ = f % 2
            for i in range(3):
                dst = wsl[s_][:, i * 2048:(i + 1) * 2048].rearrange("p (kc n) -> p kc n", kc=16)
                src = w_in[:, 6144 + i * 2048 + f * 128:6144 + i * 2048 + (f + 1) * 128].rearrange("(kc p) n -> p kc n", p=128)
                P.dma("pool", lambda e, dst=dst, src=src: e.dma_start(out=dst, in_=src), writes=[B_ws[s_]])
            srcw = w_branch[:, f * 128:(f + 1) * 128].rearrange("(kc p) n -> p kc n", p=128)
            P.dma("pool", lambda e, s_=s_, srcw=srcw: e.dma_start(out=Wsl[s_], in_=srcw), writes=[B_W2[s_]], after=[B_F[3], B_F[4], B_F[5]])

        load_phaseB(0)
        for f in range(16):
            if f + 1 < 16:
                load_phaseB(f + 1)
            s_ = f % 2
            gv = wsl[s_][:, 0:6144].rearrange("p (i kc n) -> p i kc n", i=3, kc=16)
            bv = Wsl[s_]
            MACC, BMACC = Fs[2], B_F[2]
            for i in range(3):
                GS, BGS = (Fs[0], B_F[0]) if i != 1 else (Fs[1], B_F[1])
                for (st, sz) in TILES:
                    bk = bank()

                    def mmg(e, bk=bk, st=st, sz=sz, i=i, gv=gv):
                        ins = None
                        for kc in range(16):
                            ins = e.matmul(psum[:, bk, 0:sz], lhsT=gv[:, i, kc, :], rhs=uT[:, kc, st:st + sz], start=(kc == 0), stop=(kc == 15))
                        return ins
                    P.op("pe", mmg, reads=[B_ws[s_], B_uT], writes=[B_ps[bk]])
                    P.op("act", lambda e, bk=bk, st=st, sz=sz, GS=GS: e.activation(out=GS[:, st:st + sz], in_=psum[:, bk, 0:sz], func=AF.Sigmoid), reads=[B_ps[bk]], writes=[BGS])
                for (st, sz) in TILES:
                    bk = bank()

                    def mmy(e, bk=bk, st=st, sz=sz, i=i, bv=bv):
                        ins = None
                        for kc in range(8):
                            ins = e.matmul(psum[:, bk, 0:sz], lhsT=bv[:, i * 8 + kc, :], rhs=o_all[:, i * 8 + kc, st:st + sz], start=(kc == 0), stop=(kc == 7))
                        return ins
                    P.op("pe", mmy, reads=[B_W2[s_]] + B_o[i * 8:(i + 1) * 8], writes=[B_ps[bk]])
                    if i == 0:
                        P.op("dve", lambda e, bk=bk, st=st, sz=sz, GS=GS: e.tensor_tensor(out=MACC[:, st:st + sz], in0=psum[:, bk, 0:sz], in1=GS[:, st:st + sz], op=ALU.mult), reads=[B_ps[bk], BGS], writes=[BMACC])
                    elif i == 1:
                        P.op("dve", lambda e, bk=bk, st=st, sz=sz, GS=GS: e.tensor_tensor(out=GS[:, st:st + sz], in0=psum[:, bk, 0:sz], in1=GS[:, st:st + sz], op=ALU.mult), reads=[B_ps[bk], BGS], writes=[BGS])
                        P.op("dve", lambda e, st=st, sz=sz, GS=GS: e.tensor_tensor(out=MACC[:, st:st + sz], in0=MACC[:, st:st + sz], in1=GS[:, st:st + sz], op=ALU.add), reads=[BMACC, BGS], writes=[BMACC], safe=True)
                    else:
                        P.op("dve", lambda e, bk=bk, st=st, sz=sz, GS=GS: e.tensor_tensor(out=GS[:, st:st + sz], in0=psum[:, bk, 0:sz], in1=GS[:, st:st + sz], op=ALU.mult), reads=[B_ps[bk], BGS], writes=[BGS])
                        P.op("dve", lambda e, st=st, sz=sz, f=f, GS=GS: e.tensor_tensor(out=merged[:, f, st:st + sz], in0=MACC[:, st:st + sz], in1=GS[:, st:st + sz], op=ALU.add), reads=[BMACC, BGS], writes=[B_merged],
                             after=[B_pT] + B_qT, safe=True)

        wo = []
        for ct in range(4):
            src = w_out[:, ct * 512:(ct + 1) * 512].rearrange("(kc p) n -> p kc n", p=128)
            if ct < 2:
                s = wtile(lambda slot, src=src: [(slot.rearrange("p (kc n) -> p kc n", kc=16), src)])
                wo.append((wview(s), B_ws[s]))
            else:
                v = uT_w.bitcast(BF16)[:, (ct - 2) * 8192:(ct - 1) * 8192].rearrange("p (kc n) -> p kc n", kc=16)
                bwo = Buf("wo%d" % ct)
                P.dma("pool", lambda e, v=v, src=src: e.dma_start(out=v, in_=src), writes=[bwo], after=[B_uT])
                wo.append((v, bwo))
        load("sp", gpost_bc, g_post.partition_broadcast(128), [bc("gpost")], after=B_o)
        blocks = [(tb * 128, 128, xp[tb * 128:(tb + 1) * 128, :], y_p[tb * 128:(tb + 1) * 128, :]) for tb in range(8)] + [(1024, 16, xs, y_s)]
        osb_l = [osb, regC_w[:, 3072:5120]]
        B_osb_l = [[B_F[4], B_F[5]], [Buf("osb2")]]
        for bi, (t0, n, xsrc, ydst) in enumerate(blocks):
            osb = osb_l[bi % 2]
            B_osb = B_osb_l[bi % 2]
            ssqc = ssq[:, 4 * (bi % 2):4 * (bi % 2) + 4]
            ssqb = bc("ssqC%d" % (bi % 2))
            xi = bi % 2
            xst = xstage[xi]
            bxs = [B_F[2 * xi], B_F[2 * xi + 1]]
            load("sp", xst[0:n, :], xsrc, bxs, after=B_W2)
            for ct in range(4):
                bk = bank()
                wv, wb = wo[ct]

                def mmo(e, bk=bk, wv=wv, t0=t0, n=n):
                    ins = None
                    for kc in range(16):
                        ins = e.matmul(psum[0:n, bk, :], lhsT=merged[:, kc, t0:t0 + n], rhs=wv[:, kc, :], start=(kc == 0), stop=(kc == 15))
                    return ins
                P.op("pe", mmo, reads=[B_merged, wb], writes=[B_ps[bk]])
                P.op("act", lambda e, bk=bk, ct=ct, n=n, osb=osb: e.activation(out=osb[0:n, ct * 512:(ct + 1) * 512], in_=psum[0:n, bk, :], func=AF.Copy), reads=[B_ps[bk]], writes=B_osb, after=B_W2 + B_o)
                P.op("act", lambda e, bk=bk, ct=ct, n=n, ssqc=ssqc: e.activation(out=junk[0:n, :], in_=psum[0:n, bk, :], func=AF.Square, accum_out=ssqc[0:n, ct:ct + 1]), reads=[B_ps[bk]], writes=[bc("junk"), ssqb])
            rstd = sm[:, 10 + (bi % 2):11 + (bi % 2)]
            rb = bc("rstdC%d" % (bi % 2))
            P.op("dve", lambda e, n=n, rstd=rstd, ssqc=ssqc: e.tensor_reduce(out=rstd[0:n], in_=ssqc[0:n, 0:4], axis=AX.X, op=ALU.add), reads=[ssqb], writes=[rb])
            P.op("dve", lambda e, n=n, rstd=rstd: e.tensor_scalar(out=rstd[0:n], in0=rstd[0:n], scalar1=1.0 / D, scalar2=EPS, op0=ALU.mult, op1=ALU.add), reads=[rb], writes=[rb])
            P.op("act", lambda e, n=n, rstd=rstd: e.activation(out=rstd[0:n], in_=rstd[0:n], func=AF.Sqrt), reads=[rb], writes=[rb])
            P.op("dve", lambda e, n=n, rstd=rstd: e.reciprocal(out=rstd[0:n], in_=rstd[0:n]), reads=[rb], writes=[rb])
            P.op("dve", lambda e, n=n, osb=osb, rstd=rstd: e.scalar_tensor_tensor(out=osb[0:n, :], in0=osb[0:n, :], scalar=rstd[0:n], in1=gpost_bc[0:n, :], op0=ALU.mult, op1=ALU.mult), reads=B_osb + [rb, bc("gpost")], writes=B_osb)
            P.op("dve", lambda e, n=n, xst=xst, osb=osb: e.tensor_tensor(out=osb[0:n, :], in0=osb[0:n, :], in1=xst[0:n, :], op=ALU.add), reads=B_osb + bxs, writes=B_osb, safe=True)
            out_dma(ydst, osb[0:n, :], B_osb)

        dump(8, xsT_in.rearrange("p c b -> p (c b)"), [bc("xsT_in")], 128)
        dump(9, xpsT_in.rearrange("p c b -> p (c b)"), [bc("xpsT_in")], 128)
        dump(10, tailp.rearrange("p c r -> p (c r)"), [bc("tailp")], 120)
        P.wait_all("sp", out_bufs)
        print('arena used', off[0], 'of', NW)
        P.emit()
    return nc


_NC_CACHE = {}


def kernel(x_prompt, x_sample, mem_prompt, state_rglru_h, state_conv, state_pool, cache_mem_k, cache_mem_v,
           g_pre, w_in, conv_w, conv_b, w_rg_a, b_rg_a, w_rg_x, b_rg_x, lru_lambda, w_pool, pool_scale, g_mem,
           w_kv, w_branch, w_out, g_post):
    f = lambda a: np.ascontiguousarray(np.asarray(a, dtype=np.float32))
    x_prompt = f(x_prompt); x_sample = f(x_sample); mem_prompt = f(mem_prompt)
    state_rglru_h = f(state_rglru_h); state_conv = f(state_conv); state_pool = f(state_pool)
    cache_mem_k = f(cache_mem_k); cache_mem_v = f(cache_mem_v)
    shared = {
        "g_pre": f(g_pre)[0], "w_in": f(w_in)[0], "conv_w": f(conv_w)[0], "conv_b": f(conv_b)[0],
        "w_rg_a": f(w_rg_a)[0], "b_rg_a": f(b_rg_a)[0], "w_rg_x": f(w_rg_x)[0], "b_rg_x": f(b_rg_x)[0],
        "lam": f(lru_lambda)[0], "w_pool": f(w_pool)[0], "pool_scale": f(pool_scale)[0], "g_mem": f(g_mem)[0],
        "w_kv": f(w_kv)[0], "w_branch": f(w_branch)[0], "w_out": f(w_out)[0], "g_post": f(g_post)[0],
    }
    in_maps = []
    for c in range(8):
        b, half = c // 2, c % 2
        m = dict(shared)
        m["xp"] = x_prompt[b, half * 1024:(half + 1) * 1024]
        m["xq"] = x_prompt[b, 0:1024] if half == 1 else np.zeros((1024, 2048), np.float32)
        m["xs"] = x_sample[16 * c:16 * c + 16, 0]
        m["mem"] = mem_prompt[b]
        m["st_h"] = state_rglru_h[0, 16 * c:16 * c + 16]
        m["st_conv"] = state_conv[0, 16 * c:16 * c + 16]
        m["st_pool"] = state_pool[0, 16 * c:16 * c + 16]
        m["ck"] = cache_mem_k[0, 16 * c:16 * c + 16].reshape(16, 256, 1024)
        m["cv"] = cache_mem_v[0, 16 * c:16 * c + 16].reshape(16, 256, 1024)
        m["flag"] = np.full((128, 1), float(half), np.float32)
        ic = np.zeros((128, 4, 16), np.float32)
        for g, w in enumerate(POOLW):
            for t in range(16):
                pos = half * 1024 + t
                ic[:, g, t] = 1.0 / min(pos + 1, w)
        m["icnt"] = ic
        in_maps.append({k: np.ascontiguousarray(v) for k, v in m.items()})
    if "nc" not in _NC_CACHE:
        _NC_CACHE["nc"] = build_program()
    nc = _NC_CACHE["nc"]
    res = run_bass_kernel_spmd(nc, in_maps, core_ids=list(range(8)))
    R = res.results
    y_prompt = np.zeros((4, 2048, 2048), np.float32)
    y_sample = np.zeros((128, 1, 2048), np.float32)
    new_h_p = np.zeros((1, 4, 1024), np.float32); new_conv_p = np.zeros((1, 4, 3, 1024), np.float32)
    new_pool_p = np.zeros((1, 4, 15, 1024), np.float32)
    mk = np.zeros((1, 4, 256, 4, 256), np.float32); mv = np.zeros((1, 4, 256, 4, 256), np.float32)
    new_h_s = np.zeros((1, 128, 1024), np.float32); new_conv_s = np.zeros((1, 128, 3, 1024), np.float32)
    new_pool_s = np.zeros((1, 128, 15, 1024), np.float32)
    for c in range(8):
        b, half = c // 2, c % 2
        r = R[c]
        y_prompt[b, half * 1024:(half + 1) * 1024] = r["y_p"]
        y_sample[16 * c:16 * c + 16, 0] = r["y_s"]
        new_h_s[0, 16 * c:16 * c + 16] = r["h_s"]
        new_conv_s[0, 16 * c:16 * c + 16] = r["conv_s"]
        new_pool_s[0, 16 * c:16 * c + 16] = r["pool_s"]
        if half == 1:
            new_h_p[0, b] = r["h_p"]
            new_conv_p[0, b] = r["conv_p"]
            new_pool_p[0, b] = r["pool_p"]
        else:
            mk[0, b] = r["mk"].reshape(256, 4, 256)
            mv[0, b] = r["mv"].reshape(256, 4, 256)
    return (y_prompt, y_sample, new_h_p, new_conv_p, new_pool_p, mk, mv, new_h_s, new_conv_s, new_pool_s)
```

```python
import numpy as np
from contextlib import ExitStack
import concourse.bass as bass
import concourse.mybir as mybir
from concourse.bass_utils import run_bass_kernel_spmd

F32 = mybir.dt.float32
BF16 = mybir.dt.bfloat16
AF = mybir.ActivationFunctionType
ALU = mybir.AluOpType
AX = mybir.AxisListType

ENGS = ("pe", "act", "dve", "pool", "sp")


class Buf:
    __slots__ = ("w", "r", "name")

    def __init__(self, name=""):
        self.w = None
        self.r = []
        self.name = name


class Prog:
    def __init__(self, nc, n_dma_sems=32):
        self.nc = nc
        self.ops = {e: [] for e in ENGS}
        self.cnt = {e: 0 for e in ENGS}
        self.waited = {e: {} for e in ENGS}
        self.rings = {"sp": 24, "pool": 24, "act": 8}
        self.dma_val = {q: [0] * n for q, n in self.rings.items()}
        self.dma_next = {q: 0 for q in self.rings}
        self.sems = {}

    @staticmethod
    def _flat(x):
        out = []
        for b in x:
            if isinstance(b, (list, tuple)):
                out.extend(Prog._flat(b))
            else:
                out.append(b)
        return out

    def _collect(self, eng, reads, writes, after, safe):
        reads, writes, after = self._flat(reads), self._flat(writes), self._flat(after)
        deps = set()
        for b in reads:
            if b.w is not None:
                k, v = b.w
                if k == eng:
                    if eng == "pe" or safe or self.cnt[eng] - v >= 3:
                        continue
                deps.add(b.w)
        for b in list(writes) + list(after):
            if b.w is not None and b.w[0] != eng:
                deps.add(b.w)
            for c in b.r:
                if c[0] != eng:
                    deps.add(c)
        wd = self.waited[eng]
        best = {}
        for (k, v) in deps:
            if wd.get(k, 0) >= v:
                continue
            if best.get(k, 0) < v:
                best[k] = v
        out = []
        for k, v in best.items():
            wd[k] = v
            out.append((k, v))
        return out

    def _finish(self, comp, reads, writes):
        reads, writes = self._flat(reads), self._flat(writes)
        for b in reads:
            b.r.append(comp)
        for b in writes:
            b.w = comp
            b.r = []

    def op(self, eng, fn, reads=(), writes=(), after=(), safe=False):
        waits = self._collect(eng, reads, writes, after, safe)
        self.cnt[eng] += 1
        comp = (eng, self.cnt[eng])
        self.ops[eng].append((fn, waits, (eng, 1)))
        self._finish(comp, reads, writes)
        return comp

    def dma(self, eng, fn, reads=(), writes=(), after=()):
        i = self.dma_next[eng]
        self.dma_next[eng] = (i + 1) % self.rings[eng]
        key = "d%s%d" % (eng, i)
        waits = self._collect(eng, reads, writes, after, False)
        prev = self.dma_val[eng][i]
        if prev > 0 and self.waited[eng].get(key, 0) < prev:
            self.waited[eng][key] = prev
            waits.append((key, prev))
        self.dma_val[eng][i] += 16
        comp = (key, self.dma_val[eng][i])
        self.ops[eng].append((fn, waits, (key, 16)))
        self._finish(comp, reads, writes)
        return comp

    def wait_all(self, eng, bufs):
        waits = self._collect(eng, bufs, (), (), False)
        self.ops[eng].append((None, waits, None))

    def emit(self):
        nc = self.nc
        with ExitStack() as es:
            for e in ENGS:
                self.sems[e] = es.enter_context(nc.semaphore("s_" + e))
            for q, n in self.rings.items():
                for i in range(n):
                    self.sems["d%s%d" % (q, i)] = es.enter_context(nc.semaphore("s_d%s%d" % (q, i)))
            block = es.enter_context(nc.Block())

            def run(h, name):
                for (fn, waits, inc) in self.ops[name]:
                    for (k, v) in waits:
                        h.wait_ge(self.sems[k], v)
                    if fn is None:
                        continue
                    ins = fn(h)
                    if inc is not None:
                        ins.then_inc(self.sems[inc[0]], inc[1])

            @block.tensor
            def _(e):
                run(e, "pe")

            @block.scalar
            def _(e):
                run(e, "act")

            @block.vector
            def _(e):
                run(e, "dve")

            @block.gpsimd
            def _(e):
                run(e, "pool")

            @block.sync
            def _(e):
                run(e, "sp")


D = 2048
TP, TS, T = 1024, 16, 1040
TILES = [(0, 347), (347, 347), (694, 346)]
QT = [(0, 512), (512, 512)]
SW = 1056
NSLOT = 6
EPS = 1e-6
POOLW = (2, 4, 8, 16)


def build_program(dbg=False):
    nc = bass.Bass("TRN2", target_bir_lowering=False)
    dbg_t = nc.dram_tensor("dbg", [128, 12, 1040], F32, kind="ExternalOutput").ap() if dbg else None

    def din(name, shape):
        return nc.dram_tensor(name, shape, F32, kind="ExternalInput").ap()

    def dout(name, shape):
        return nc.dram_tensor(name, shape, F32, kind="ExternalOutput").ap()

    xp = din("xp", [TP, D]); xq = din("xq", [TP, D]); xs = din("xs", [TS, D]); mem = din("mem", [256, D])
    st_h = din("st_h", [TS, 1024]); st_conv = din("st_conv", [TS, 3, 1024]); st_pool = din("st_pool", [TS, 15, 1024])
    ck = din("ck", [TS, 256, 1024]); cv = din("cv", [TS, 256, 1024])
    g_pre = din("g_pre", [D]); w_in = din("w_in", [D, 12288]); conv_w = din("conv_w", [4, 1024]); conv_b = din("conv_b", [1024])
    w_rg_a = din("w_rg_a", [8, 128, 128]); b_rg_a = din("b_rg_a", [1024]); w_rg_x = din("w_rg_x", [8, 128, 128]); b_rg_x = din("b_rg_x", [1024])
    lam = din("lam", [1024]); w_pool = din("w_pool", [4, 256, 256]); pool_scale = din("pool_scale", [1024]); g_mem = din("g_mem", [D])
    w_kv = din("w_kv", [D, 2048]); w_branch = din("w_branch", [3072, D]); w_out = din("w_out", [D, D]); g_post = din("g_post", [D])
    flag = din("flag", [128, 1]); icnt = din("icnt", [128, 4, 16])

    y_p = dout("y_p", [TP, D]); y_s = dout("y_s", [TS, D]); h_p = dout("h_p", [1024]); conv_p = dout("conv_p", [3, 1024])
    pool_p = dout("pool_p", [15, 1024]); mk = dout("mk", [256, 1024]); mv = dout("mv", [256, 1024])
    h_s = dout("h_s", [TS, 1024]); conv_s = dout("conv_s", [TS, 3, 1024]); pool_s = dout("pool_s", [TS, 15, 1024])

    es = ExitStack()
    with es:
        NW = 51100
        arena = es.enter_context(nc.sbuf_tensor("arena", [128, NW], F32))
        off = [0]

        def carve(nwords):
            a = arena[:, off[0]:off[0] + nwords]
            off[0] += nwords
            assert off[0] <= NW, off[0]
            return a

        uT_w = carve(8320); uT = uT_w.bitcast(BF16).rearrange("p (c t) -> p c t", c=16)
        regB_w = carve(8320); regB = regB_w.bitcast(BF16)
        merged = regB.rearrange("p (c t) -> p c t", c=16)
        qT = regB[:, 0:8 * T].rearrange("p (c t) -> p c t", c=8)
        pT_all = regB[:, 8 * T:8 * T + 8192].rearrange("p (mc h t) -> p mc h t", mc=2, h=4)
        umT = regB[:, 0:4096].rearrange("p (c t) -> p c t", c=16)
        regC_w = carve(12480); regC = regC_w.bitcast(BF16)
        o_all = regC.rearrange("p (c t) -> p c t", c=24)
        uqT = regC[:, 8 * T:8 * T + 16384].rearrange("p (c t) -> p c t", c=16)
        arena_D = carve(8192)
        wsl_w = [arena_D[:, 0:4096], arena_D[:, 4096:8192]]
        wsl = [w.bitcast(BF16) for w in wsl_w]
        Fw = carve(NSLOT * SW)
        Fs = [Fw[:, i * SW:(i + 1) * SW] for i in range(NSLOT)]
        xstage = [Fw[:, 0:2048], Fw[:, 2 * SW:2 * SW + 2048]]
        kvf = [regB_w[:, 2048:4096], regB_w[:, 4096:6144]]
        kst = xstage[0].rearrange("p (m f) -> p m f", m=2)
        vst = xstage[1].rearrange("p (m f) -> p m f", m=2)
        hist = regC_w[:, 768:2688].rearrange("p (r k) -> p r k", r=15)
        osb = Fw[:, 4 * SW:4 * SW + 2048]
        KT_w = carve(1024)
        KT = KT_w.bitcast(BF16).rearrange("p (c m) -> p c m", c=8)
        Vb = carve(1024).bitcast(BF16).rearrange("p (m f) -> p m f", m=2)
        gpost_bc = regC_w[:, 0:2048]
        ident = carve(128)
        consts = carve(104)
        gpre_fm = consts[:, 0:16]; gmem_fm = consts[:, 16:32]
        convw_fm = consts[:, 32:64].rearrange("p (k c) -> p k c", k=4)
        convb_fm = consts[:, 64:72]; ba_fm = consts[:, 72:80]; bx_fm = consts[:, 80:88]; cch = consts[:, 88:96]; pscale_fm = consts[:, 96:104]
        cstage = regC_w[:, 640:768]
        dg = [carve(128) for _ in range(3)]
        ssq3 = carve(12)
        flag_t = carve(1); icnt_t = carve(64).rearrange("p (g t) -> p g t", g=4)
        wrga = carve(512).bitcast(BF16).rearrange("p (n e) -> p n e", n=8)
        wrgx = carve(512).bitcast(BF16).rearrange("p (n e) -> p n e", n=8)
        wpl = carve(1024).bitcast(BF16).rearrange("p (g dc e) -> p g dc e", g=4, dc=2)
        uqtail = carve(128).bitcast(BF16).rearrange("p (c t) -> p c t", c=16)
        sm = carve(16)
        ssq = carve(8)
        cst = regC_w[:, 0:384].rearrange("p (r k) -> p r k", r=3)
        cst_fm = carve(384).rearrange("p (r c b) -> p r c b", r=3, c=8)
        h0st = regC_w[:, 384:512]; h0_fm = carve(128).rearrange("p (c b) -> p c b", c=8)
        hsum = regC_w[:, 512:640]; hsum_fm = carve(128).rearrange("p (c b) -> p c b", c=8)
        h0p = carve(8); qtail = carve(24).rearrange("p (c r) -> p c r", c=8)
        tailx = carve(24).rearrange("p (c r) -> p c r", c=8); tailp = carve(120).rearrange("p (c r) -> p c r", c=8)
        hlast = carve(8)
        xsT_in = carve(128).rearrange("p (c b) -> p c b", c=8); xpsT_in = carve(128).rearrange("p (c b) -> p c b", c=8)
        hsT_in = carve(128).rearrange("p (c b) -> p c b", c=8)
        outT, outT2, outT3, outT4, outT5, outT6 = [regC_w[:, 2048 + 128 * i:2176 + 128 * i] for i in range(6)]
        qexp = regC_w[:, 8320:9344].bitcast(BF16).rearrange("p (h dc b j) -> p h dc b j", h=4, dc=2, b=16)
        epj = carve(512)
        ep = [epj[:, 0:256], epj[:, 256:512]]
        ps_s = regC_w[:, 9344:10368].rearrange("p (h m) -> p h m", h=4)
        pTs = carve(128).rearrange("p (mc h b) -> p mc h b", mc=2, h=4)
        os_fm = carve(128).rearrange("p (c b) -> p c b", c=8)
        junk = epj
        small15 = carve(16)

        psum = es.enter_context(nc.psum_tensor("psum", [128, 8, 512], F32))

        P = Prog(nc)
        B_uT = Buf("uT"); B_regB = Buf("regB"); B_qT = [Buf() for _ in range(8)]; B_pT = Buf("pT")
        B_o = [Buf("o%d" % i) for i in range(24)]
        B_uq = Buf("uqT")
        B_q = [Buf("q%d" % i) for i in range(4)]
        B_ws = [[B_q[0], B_q[1]], [B_q[2], B_q[3]]]
        B_F = [Buf("F%d" % i) for i in range(NSLOT)]
        B_ps = [Buf("ps%d" % i) for i in range(8)]
        B_c = {}

        def bc(name):
            if name not in B_c:
                B_c[name] = Buf(name)
            return B_c[name]

        B_out = Buf("outs")
        out_bufs = []
        rot = [0]

        bank_pool = [list(range(6))]

        def bank():
            p_ = bank_pool[0]
            b = p_[rot[0] % len(p_)]
            rot[0] += 1
            return b

        aux = [0]

        def auxbank():
            b = 6 + aux[0]
            aux[0] = 1 - aux[0]
            return b

        def out_dma(dst, src, reads):
            b = Buf()
            P.dma("sp", lambda e: e.dma_start(out=dst, in_=src), reads=reads, writes=[b])
            out_bufs.append(b)

        def dump(idx, src, bufs, ncol=1040):
            if dbg_t is not None:
                out_dma(dbg_t[:, idx, 0:ncol], src, bufs)

        def load(eng, dst, src, wb, slow=False, after=()):
            if slow:
                P.dma(eng, lambda e: e.dma_start(out=dst, in_=src, allow_slow_non_contiguous=True), writes=wb, after=after)
            else:
                P.dma(eng, lambda e: e.dma_start(out=dst, in_=src), writes=wb, after=after)

        wt_state = {"n": 0}

        def wtile(src_fn):
            s = wt_state["n"] % 2
            wt_state["n"] += 1
            for (dst, src) in src_fn(wsl[s]):
                P.dma("pool", lambda e, dst=dst, src=src: e.dma_start(out=dst, in_=src), writes=[B_ws[s]])
            return s

        def win_tile(t):
            def f(slot):
                v = slot.rearrange("p (kc n) -> p kc n", kc=16)
                return [(v, w_in[:, t * 512:(t + 1) * 512].rearrange("(kc p) n -> p kc n", p=128))]
            return wtile(f)

        def wview(s):
            return wsl[s].rearrange("p (kc n) -> p kc n", kc=16)

        P.op("dve", lambda e: e.memset(ident, 0.0), writes=[bc("ident")])
        P.op("pool", lambda e: e.affine_select(out=ident, in_=ident, pattern=[[-1, 128]], compare_op=ALU.not_equal, fill=1.0, base=0, channel_multiplier=1),
             reads=[bc("ident")], writes=[bc("ident")])
        crow = [(0, 16, g_pre.rearrange("(c p) -> c p", p=128)), (16, 16, g_mem.rearrange("(c p) -> c p", p=128)),
                (32, 32, conv_w.rearrange("k (c p) -> (k c) p", p=128)), (64, 8, conv_b.rearrange("(c p) -> c p", p=128)),
                (72, 8, b_rg_a.rearrange("(c p) -> c p", p=128)), (80, 8, b_rg_x.rearrange("(c p) -> c p", p=128)),
                (88, 8, lam.rearrange("(c p) -> c p", p=128)), (96, 8, pool_scale.rearrange("(c p) -> c p", p=128))]
        for (r0, nr, src) in crow:
            load("act", cstage[r0:r0 + nr, :], src, [bc("cstage")])
        bkc = auxbank()
        P.op("pe", lambda e: e.transpose(out=psum[:, bkc, 0:104], in_=cstage[0:104, :], identity=ident[0:104, 0:104]), reads=[bc("cstage"), bc("ident")], writes=[B_ps[bkc]])
        P.op("dve", lambda e: e.tensor_copy(out=consts, in_=psum[:, bkc, 0:104]), reads=[B_ps[bkc]],
             writes=[bc(nm) for nm in ("gpre", "gmem", "convw", "convb", "ba", "bx", "cch", "pscale")])
        load("act", flag_t, flag, [bc("flag")])
        load("act", icnt_t, icnt, [bc("icnt")])
        P.dma("pool", lambda e: e.dma_start(out=wrga, in_=w_rg_a.rearrange("n d e -> d n e")), writes=[bc("wrga")])
        P.dma("pool", lambda e: e.dma_start(out=wrgx, in_=w_rg_x.rearrange("n d e -> d n e")), writes=[bc("wrgx")])
        P.dma("pool", lambda e: e.dma_start(out=wpl, in_=w_pool.rearrange("g (dc p) e -> p g dc e", p=128)), writes=[bc("wpl")])

        B_hist = [bc("hist")]
        for c in range(8):
            load("pool", cst[16 * c:16 * c + 16, :, :], st_conv[:, :, c * 128:(c + 1) * 128], [bc("cst")])
            load("pool", h0st[16 * c:16 * c + 16, :], st_h[:, c * 128:(c + 1) * 128], [bc("h0st")])
            load("pool", hist[16 * c:16 * c + 16, :, :], st_pool[:, :, c * 128:(c + 1) * 128], B_hist)
        nb = [0]

        xst3 = [xstage[0], xstage[1], Fw[:, 4 * SW:4 * SW + 2048]]
        BXS3 = [[B_F[0], B_F[1]], [B_F[2], B_F[3]], [B_F[4], B_F[5]]]

        def norm_parts(src_rows, n, dstT_fn, g_fm, gname, dstbuf, extra_after=()):
            i = nb[0] % 3
            nb[0] += 1
            xst = xst3[i]
            bx = BXS3[i]
            rstd = sm[:, i:i + 1]
            rb = bc("rstd%d" % i)
            sq = ssq3[:, 4 * i:4 * i + 4]
            sqb = bc("ssq%d" % i)
            dgi = dg[i]
            dgb = bc("dg%d" % i)

            def front():
                load("sp", xst[0:n, :], src_rows, bx)
                junk_bf = junk.bitcast(BF16)
                for j in range(2):
                    P.op("act", lambda e, j=j: e.activation(out=junk_bf[0:n, :], in_=xst[0:n, j * 1024:(j + 1) * 1024], func=AF.Square, accum_out=sq[0:n, j:j + 1]),
                         reads=bx, writes=[bc("junk"), sqb])
                P.op("dve", lambda e: e.tensor_reduce(out=rstd[0:n], in_=sq[0:n, 0:2], axis=AX.X, op=ALU.add), reads=[sqb], writes=[rb])
                P.op("dve", lambda e: e.tensor_scalar(out=rstd[0:n], in0=rstd[0:n], scalar1=1.0 / D, scalar2=EPS, op0=ALU.mult, op1=ALU.add), reads=[rb], writes=[rb])
                P.op("act", lambda e: e.activation(out=rstd[0:n], in_=rstd[0:n], func=AF.Sqrt), reads=[rb], writes=[rb])
                P.op("dve", lambda e: e.reciprocal(out=rstd[0:n], in_=rstd[0:n]), reads=[rb], writes=[rb])
                P.op("act", lambda e: e.activation(out=xst[0:n, :], in_=xst[0:n, :], func=AF.Copy, scale=rstd[0:n]), reads=bx + [rb], writes=bx)

            def back():
                for grp in range(4):
                    bk = auxbank()

                    def tr(e, grp=grp, bk=bk):
                        ins = None
                        for j in range(4):
                            c = grp * 4 + j
                            ins = e.transpose(out=psum[:, bk, j * 128:j * 128 + n], in_=xst[0:n, c * 128:(c + 1) * 128], identity=ident[0:n, 0:n])
                        return ins
                    P.op("pe", tr, reads=bx + [bc("ident")], writes=[B_ps[bk]])
                    c0 = grp * 4
                    P.op("dve", lambda e, c0=c0, bk=bk: e.tensor_tensor(out=dstT_fn(slice(c0, c0 + 4)), in0=psum[:, bk, :].rearrange("p (j m) -> p j m", j=4)[:, :, 0:n],
                                                                     in1=g_fm[:, c0:c0 + 4].unsqueeze(2).to_broadcast([128, 4, n]), op=ALU.mult),
                         reads=[B_ps[bk], bc(gname)], writes=[dstbuf], after=extra_after)
            return front, back

        def norm_many(arglist):
            parts = [norm_parts(*a) for a in arglist]
            for k in range(len(parts) + 1):
                if k < len(parts):
                    parts[k][0]()
                if k > 0:
                    parts[k - 1][1]()

        def norm_block(*a, **kw):
            norm_many([a])

        norm_many([(mem[mb * 128:(mb + 1) * 128, :], 128, (lambda c, mb=mb: umT[:, c, mb * 128:(mb + 1) * 128]), gmem_fm, "gmem", B_regB) for mb in range(2)])
        nl = [(xq[tb * 128:(tb + 1) * 128, :], 128, (lambda c, tb=tb: uqT[:, c, tb * 128:(tb + 1) * 128]), gpre_fm, "gpre", B_uq) for tb in range(8)]
        nl += [(xp[tb * 128:(tb + 1) * 128, :], 128, (lambda c, tb=tb: uT[:, c, tb * 128:(tb + 1) * 128]), gpre_fm, "gpre", B_uT) for tb in range(8)]
        nl += [(xs, 16, (lambda c: uT[:, c, 1024:1040]), gpre_fm, "gpre", B_uT)]
        norm_many(nl)
        P.op("dve", lambda e: e.tensor_copy(out=uqtail, in_=uqT[:, :, 1008:1024]), reads=[B_uq], writes=[bc("uqtail")])

        bk = auxbank()

        def tr_c(e, bk=bk):
            ins = None
            for r in range(3):
                ins = e.transpose(out=psum[:, bk, r * 128:(r + 1) * 128], in_=cst[:, r, :], identity=ident)
            return ins
        P.op("pe", tr_c, reads=[bc("cst"), bc("ident")], writes=[B_ps[bk]])
        P.op("dve", lambda e, bk=bk: e.tensor_copy(out=cst_fm.rearrange("p r c b -> p (r c b)"), in_=psum[:, bk, 0:384]), reads=[B_ps[bk]], writes=[bc("cst_fm")])
        bk = auxbank()
        P.op("pe", lambda e, bk=bk: e.transpose(out=psum[:, bk, 0:128], in_=h0st, identity=ident), reads=[bc("h0st"), bc("ident")], writes=[B_ps[bk]])
        P.op("dve", lambda e, bk=bk: e.tensor_copy(out=h0_fm.rearrange("p c b -> p (c b)"), in_=psum[:, bk, 0:128]), reads=[B_ps[bk]], writes=[bc("h0_fm")])
        for g in range(4):
            w = POOLW[g]
            if w == 2:
                P.op("dve", lambda e, g=g: e.tensor_copy(out=hsum[32 * g:32 * g + 32, :], in_=hist[32 * g:32 * g + 32, 14, :]), reads=B_hist, writes=[bc("hsum")])
            else:
                P.op("dve", lambda e, g=g, w=w: e.tensor_reduce(out=hsum[32 * g:32 * g + 32, :], in_=hist[32 * g:32 * g + 32, 16 - w:15, :].rearrange("p r k -> p k r"), axis=AX.X, op=ALU.add),
                     reads=B_hist, writes=[bc("hsum")])
        bk = auxbank()
        P.op("pe", lambda e, bk=bk: e.transpose(out=psum[:, bk, 0:128], in_=hsum, identity=ident), reads=[bc("hsum"), bc("ident")], writes=[B_ps[bk]])
        P.op("dve", lambda e, bk=bk: e.tensor_copy(out=hsum_fm.rearrange("p c b -> p (c b)"), in_=psum[:, bk, 0:128]), reads=[B_ps[bk]], writes=[bc("hsum_fm")])

        B_kvf = [Buf("kvf0"), Buf("kvf1")]
        for ct in range(4):
            s = wtile(lambda slot, ct=ct: [(slot.rearrange("p (kc n) -> p kc n", kc=16), w_kv[:, ct * 512:(ct + 1) * 512].rearrange("(kc p) n -> p kc n", p=128))])
            wv = wview(s)
            for mb in range(2):
                bk = bank()

                def mm(e, mb=mb, bk=bk, wv=wv):
                    ins = None
                    for kc in range(16):
                        ins = e.matmul(psum[:, bk, :], lhsT=umT[:, kc, mb * 128:(mb + 1) * 128], rhs=wv[:, kc, :], start=(kc == 0), stop=(kc == 15))
                    return ins
                P.op("pe", mm, reads=[B_regB, B_ws[s]], writes=[B_ps[bk]])
                P.op("act", lambda e, mb=mb, bk=bk, ct=ct: e.activation(out=kvf[mb][:, ct * 512:(ct + 1) * 512], in_=psum[:, bk, :], func=AF.Copy),
                     reads=[B_ps[bk]], writes=B_kvf)
        for mb in range(2):
            out_dma(mk[mb * 128:(mb + 1) * 128, :], kvf[mb][:, 0:1024], B_kvf)
            out_dma(mv[mb * 128:(mb + 1) * 128, :], kvf[mb][:, 1024:2048], B_kvf)
            P.op("dve", lambda e, mb=mb: e.tensor_copy(out=Vb[:, mb, :], in_=kvf[mb][:, 1024:2048]), reads=B_kvf, writes=[bc("Vb")])
            for grp in range(2):
                bk = auxbank()

                def tr(e, mb=mb, grp=grp, bk=bk):
                    ins = None
                    for j in range(4):
                        c = grp * 4 + j
                        ins = e.transpose(out=psum[:, bk, j * 128:(j + 1) * 128], in_=kvf[mb][:, c * 128:(c + 1) * 128], identity=ident)
                    return ins
                P.op("pe", tr, reads=B_kvf + [bc("ident")], writes=[B_ps[bk]])
                P.op("act", lambda e, mb=mb, grp=grp, bk=bk: e.activation(out=KT[:, grp * 4:(grp + 1) * 4, mb * 128:(mb + 1) * 128],
                                                                            in_=psum[:, bk, :].rearrange("p (j m) -> p j m", j=4), func=AF.Copy),
                     reads=[B_ps[bk]], writes=[bc("KT")])

        Fs2 = [regB_w[:, i * SW:(i + 1) * SW] for i in range(NSLOT)]
        B_F2 = [Buf("G%d" % i) for i in range(NSLOT)]
        SETS = [(Fs, B_F), (Fs2, B_F2)]
        Dq_w = [arena_D[:, k * 2048:(k + 1) * 2048] for k in range(4)]
        Dq = [w.bitcast(BF16) for w in Dq_w]
        qfree = [0, 1, 2, 3]

        def qalloc():
            return qfree.pop(0)

        def qrelease(k):
            qfree.append(k)

        P.op("act", lambda e: e.activation(out=cch, in_=cch, func=AF.Exp, scale=-1.0), reads=[bc("cch")], writes=[bc("cch")])
        P.op("act", lambda e: e.activation(out=cch, in_=cch, func=AF.Ln, bias=1.0), reads=[bc("cch")], writes=[bc("cch")])
        P.op("dve", lambda e: e.tensor_scalar(out=cch, in0=cch, scalar1=-8.0, scalar2=None, op0=ALU.mult), reads=[bc("cch")], writes=[bc("cch")])
        hba = carve(8); hbx = carve(8); hcch = carve(8)
        P.op("dve", lambda e: e.tensor_scalar(out=hba, in0=ba_fm, scalar1=0.5, scalar2=None, op0=ALU.mult), reads=[bc("ba")], writes=[bc("hba")])
        P.op("dve", lambda e: e.tensor_scalar(out=hbx, in0=bx_fm, scalar1=0.5, scalar2=None, op0=ALU.mult), reads=[bc("bx")], writes=[bc("hbx")])
        P.op("dve", lambda e: e.tensor_scalar(out=hcch, in0=cch, scalar1=0.5, scalar2=None, op0=ALU.mult), reads=[bc("cch")], writes=[bc("hcch")])

        def ucols(kc, st, sz):
            return uT[:, kc, st:st + sz]

        pend_banks = set()

        def zmm(wv, wbufs, ucols_fn, tiles, ubuf, pend=None):
            res = []
            for (st, sz) in tiles:
                bk = bank()
                if pend is not None:
                    assert bk not in pend, "PSUM bank %d re-claimed before its evacuation was queued" % bk
                    pend.add(bk)

                def mm(e, bk=bk, st=st, sz=sz):
                    ins = None
                    for kc in range(16):
                        ins = e.matmul(psum[:, bk, 0:sz], lhsT=wv[:, kc, :], rhs=ucols_fn(kc, st, sz), start=(kc == 0), stop=(kc == 15))
                    return ins
                P.op("pe", mm, reads=wbufs + [ubuf], writes=[B_ps[bk]])
                res.append((bk, st, sz))
            return res

        def z_mm(s, j, ucols_fn, tiles, ubuf):
            wv = wview(s)
            return zmm(wv[:, :, j * 128:(j + 1) * 128], [B_ws[s]], ucols_fn, tiles, ubuf)

        BPQ = [[[Buf() for _ in range(2)] for _ in range(NSLOT)] for _ in range(2)]
        BPP = [[[Buf() for _ in range(3)] for _ in range(NSLOT)] for _ in range(2)]

        def make_rg(n, is_q, ci):
            si = ci % 2
            Sl, Bl = SETS[si]
            first_after = ([B_regB] + B_kvf) if si == 1 else []
            BP = BPQ[si] if is_q else BPP[si]
            pre = list(Bl) + first_after + ([] if is_q else [b for sl in BPQ[si] for b in sl])
            st8 = {}

            def load():
                if len(qfree) < 1:
                    return False
                k = qalloc()
                st8["k"] = k
                v = Dq[k].rearrange("p (a kc n) -> p a kc n", a=2, kc=16)
                P.dma("pool", lambda e: e.dma_start(out=v[:, 0], in_=w_in[:, n * 128:(n + 1) * 128].rearrange("(kc p) n -> p kc n", p=128)), writes=[B_q[k]])
                if not is_q:
                    P.dma("pool", lambda e: e.dma_start(out=v[:, 1], in_=w_in[:, 1024 + n * 128:1024 + (n + 1) * 128].rearrange("(kc p) n -> p kc n", p=128)), writes=[B_q[k]])
                return True

            def gen():
                k = st8["k"]
                v = Dq[k].rearrange("p (a kc n) -> p a kc n", a=2, kc=16)
                X, C, CB, R, I, M = Sl[0], Sl[1], Sl[2].bitcast(BF16), Sl[3], Sl[4], Sl[5]
                BX, BC, BCB, BR, BI, BM = BP
                parts = QT if is_q else TILES
                NP = len(parts)
                LP = 1024
                seq = [(st, min(st + sz, LP)) for (st, sz) in parts]
                if is_q:
                    xb = zmm(v[:, 0], [B_q[k]], lambda kc, st, sz: uqT[:, kc, st:st + sz], QT, B_uq, pend_banks)
                else:
                    xb = zmm(v[:, 0], [B_q[k]], ucols, TILES, B_uT, pend_banks)
                yield
                gb = [] if is_q else zmm(v[:, 1], [B_q[k]], ucols, TILES, B_uT, pend_banks)
                qrelease(k)
                yield
                if is_q:
                    P.op("dve", lambda e: e.memset(X[:, 0:3], 0.0), writes=[BX[0]], after=pre)
                else:
                    P.op("dve", lambda e: e.tensor_copy(out=X[:, 0:3], in_=qtail[:, n, :]), reads=[bc("qtail")], writes=[BX[0]], after=pre)
                for p, (bk, st, sz) in enumerate(xb):
                    P.op("act", lambda e, bk=bk, st=st, sz=sz: e.activation(out=X[:, 3 + st:3 + st + sz], in_=psum[:, bk, 0:sz], func=AF.Copy), reads=[B_ps[bk]], writes=[BX[p]], after=pre)
                    pend_banks.discard(bk)
                yield
                for p, (st, sz) in enumerate(parts):
                    P.op("dve", lambda e, st=st, sz=sz: e.tensor_scalar(out=C[:, st:st + sz], in0=X[:, 3 + st:3 + st + sz], scalar1=convw_fm[:, 3, n:n + 1], scalar2=convb_fm[:, n:n + 1], op0=ALU.mult, op1=ALU.add),
                         reads=[BX[p], bc("convw"), bc("convb")], writes=[BC[p]], after=pre)
                yield
                for kk in range(3):
                    for p, (st, en) in enumerate(seq):
                        rd = [BX[p], BC[p], bc("convw")] + ([BX[p - 1]] if p > 0 else [])
                        P.op("dve", lambda e, kk=kk, st=st, en=en: e.scalar_tensor_tensor(out=C[:, st:en], in0=X[:, st + kk:en + kk], scalar=convw_fm[:, kk, n:n + 1], in1=C[:, st:en], op0=ALU.mult, op1=ALU.add),
                             reads=rd, writes=[BC[p]], safe=True)
                    yield
                lastp = NP - 1
                if is_q:
                    P.op("dve", lambda e: e.tensor_copy(out=qtail[:, n, :], in_=X[:, 3 + 1021:3 + 1024]), reads=[BX[lastp]], writes=[bc("qtail")])
                else:
                    P.op("dve", lambda e: e.tensor_copy(out=tailx[:, n, :], in_=X[:, 3 + 1021:3 + 1024]), reads=[BX[lastp]], writes=[bc("tailx")])
                    P.op("dve", lambda e: e.tensor_copy(out=xsT_in[:, n, :], in_=X[:, 3 + 1024:3 + 1040]), reads=[BX[lastp]], writes=[bc("xsT_in")])
                    for kk in range(3):
                        P.op("dve", lambda e, kk=kk: e.scalar_tensor_tensor(out=C[:, 1024:1040], in0=cst_fm[:, kk, n, :], scalar=convw_fm[:, kk, n:n + 1], in1=C[:, 1024:1040], op0=ALU.mult, op1=ALU.add),
                             reads=[bc("cst_fm"), BC[lastp], bc("convw")], writes=[BC[lastp]])
                yield
                for p, (st, sz) in enumerate(parts):
                    P.op("act", lambda e, st=st, sz=sz: e.activation(out=CB[:, st:st + sz], in_=C[:, st:st + sz], func=AF.Copy), reads=[BC[p]], writes=[BCB[p]], after=pre)
                for p, (bk, st, sz) in enumerate(gb):
                    ex = [BX[p + 1]] if p + 1 < NP else []
                    P.op("act", lambda e, bk=bk, st=st, sz=sz: e.activation(out=X[:, 3 + st:3 + st + sz], in_=psum[:, bk, 0:sz], func=AF.Silu), reads=[B_ps[bk]], writes=[BX[p]], after=ex)
                    pend_banks.discard(bk)
                yield
                for (wg, hb, bname, dst, bdst, wname) in ((wrga, hba, "hba", R, BR, "wrga"), (wrgx, hbx, "hbx", I, BI, "wrgx")):
                    for p, (st, sz) in enumerate(parts):
                        bk = auxbank()
                        P.op("pe", lambda e, bk=bk, st=st, sz=sz, wg=wg: e.matmul(psum[:, bk, 0:sz], lhsT=wg[:, n, :], rhs=CB[:, st:st + sz], start=True, stop=True),
                             reads=[BCB[p], bc(wname)], writes=[B_ps[bk]])
                        P.op("act", lambda e, bk=bk, st=st, sz=sz, dst=dst, hb=hb: e.activation(out=dst[:, st:st + sz], in_=psum[:, bk, 0:sz], func=AF.Tanh, bias=hb[:, n:n + 1], scale=0.5),
                             reads=[B_ps[bk], bc(bname)], writes=[bdst[p]], after=pre)
                yield
                for p, (st, sz) in enumerate(parts):
                    P.op("act", lambda e, st=st, sz=sz: e.activation(out=M[:, st:st + sz], in_=R[:, st:st + sz], func=AF.Exp, scale=cch[:, n:n + 1], bias=cch[:, n:n + 1]), reads=[BR[p], bc("cch")], writes=[BM[p]], after=pre)
                for p, (st, sz) in enumerate(parts):
                    P.op("act", lambda e, st=st, sz=sz: e.activation(out=R[:, st:st + sz], in_=R[:, st:st + sz], func=AF.Exp, scale=hcch[:, n:n + 1], bias=hcch[:, n:n + 1]), reads=[BR[p], bc("hcch")], writes=[BR[p]])
                yield
                for p, (st, sz) in enumerate(parts):
                    P.op("act", lambda e, st=st, sz=sz: e.activation(out=M[:, st:st + sz], in_=M[:, st:st + sz], func=AF.Sqrt, scale=-1.0, bias=1.0), reads=[BM[p]], writes=[BM[p]])
                yield
                for p, (st, sz) in enumerate(parts):
                    P.op("dve", lambda e, st=st, sz=sz: e.scalar_tensor_tensor(out=M[:, st:st + sz], in0=I[:, st:st + sz], scalar=1.0, in1=M[:, st:st + sz], op0=ALU.add, op1=ALU.mult), reads=[BM[p], BI[p]], writes=[BM[p]])
                yield
                for p, (st, sz) in enumerate(parts):
                    P.op("dve", lambda e, st=st, sz=sz: e.scalar_tensor_tensor(out=M[:, st:st + sz], in0=M[:, st:st + sz], scalar=0.5, in1=C[:, st:st + sz], op0=ALU.mult, op1=ALU.mult), reads=[BM[p], BC[p]], writes=[BM[p]])
                yield
                for p, (st, en) in enumerate(seq):
                    if p == 0:
                        init = 0.0 if is_q else h0p[:, n:n + 1]
                        rd = [BR[p], BM[p]] + ([] if is_q else [bc("h0p")])
                    else:
                        init = C[:, st - 1:st]
                        rd = [BR[p], BM[p], BC[p - 1]]
                    P.op("dve", lambda e, st=st, en=en, init=init: e.tensor_tensor_scan(out=C[:, st:en], data0=R[:, st:en], data1=M[:, st:en], initial=init, op0=ALU.mult, op1=ALU.add),
                         reads=rd, writes=[BC[p]])
                if is_q:
                    P.op("dve", lambda e: e.tensor_scalar(out=h0p[:, n:n + 1], in0=C[:, LP - 1:LP], scalar1=flag_t[:, 0:1], scalar2=None, op0=ALU.mult), reads=[BC[lastp], bc("flag")], writes=[bc("h0p")])
                else:
                    P.op("dve", lambda e: e.tensor_tensor(out=C[:, 1024:1040], in0=R[:, 1024:1040], in1=h0_fm[:, n, :], op=ALU.mult), reads=[BR[lastp], bc("h0_fm")], writes=[BC[lastp]])
                    P.op("dve", lambda e: e.tensor_tensor(out=C[:, 1024:1040], in0=C[:, 1024:1040], in1=M[:, 1024:1040], op=ALU.add), reads=[BC[lastp], BM[lastp]], writes=[BC[lastp]])
                    P.op("dve", lambda e: e.tensor_copy(out=hlast[:, n:n + 1], in_=C[:, 1023:1024]), reads=[BC[lastp]], writes=[bc("hlast")])
                    P.op("dve", lambda e: e.tensor_copy(out=hsT_in[:, n, :], in_=C[:, 1024:1040]), reads=[BC[lastp]], writes=[bc("hsT_in")])
                    yield
                    if n == 0:
                        dump(0, C[:, 0:1040], list(BC)); dump(1, X[:, 3:3 + 1040], list(BX)); dump(2, R[:, 0:1040], list(BR)); dump(3, M[:, 0:1040], list(BM))
                    for p, (st, sz) in enumerate(parts):
                        P.op("dve", lambda e, st=st, sz=sz: e.tensor_tensor(out=o_all[:, n, st:st + sz], in0=C[:, st:st + sz], in1=X[:, 3 + st:3 + st + sz], op=ALU.mult), reads=[BX[p], BC[p]], writes=[B_o[n]],
                             after=[bc("cst"), bc("h0st"), bc("hsum"), bc("cstage"), bc("hist")])
            return (load, gen)

        def rg_fence(si):
            allp = [b for sl in BPQ[si] for b in sl] + [b for sl in BPP[si] for b in sl]
            P.op("dve", lambda e: e.memset(small15[:, 1:2], 0.0), reads=allp, writes=[bc("small15")] + list(SETS[si][1]))

        def make_pool(g, ci):
            Sl, Bl = SETS[ci % 2]
            first_after = ([B_regB] + B_kvf) if ci % 2 == 1 else []
            w = POOLW[g]
            st8 = {}

            def load():
                if len(qfree) < 2:
                    return False
                kx = qalloc(); kg = qalloc()
                st8["kx"], st8["kg"] = kx, kg
                vx = Dq[kx].rearrange("p (kc n) -> p kc n", kc=16)
                vg = Dq[kg].rearrange("p (kc n) -> p kc n", kc=16)
                P.dma("pool", lambda e: e.dma_start(out=vx, in_=w_in[:, 2048 + g * 256:2048 + (g + 1) * 256].rearrange("(kc p) n -> p kc n", p=128)), writes=[B_q[kx]])
                P.dma("pool", lambda e: e.dma_start(out=vg, in_=w_in[:, 3072 + g * 256:3072 + (g + 1) * 256].rearrange("(kc p) n -> p kc n", p=128)), writes=[B_q[kg]])
                return True

            def gen():
                kx, kg = st8["kx"], st8["kg"]
                vx = Dq[kx].rearrange("p (kc n) -> p kc n", kc=16)
                vg = Dq[kg].rearrange("p (kc n) -> p kc n", kc=16)
                X, SA, SBt, G = Sl[0], Sl[1], Sl[2], Sl[4]
                DB = Sl[3].bitcast(BF16)
                BX, BSA, BSB, BDB, BG = Bl[0], Bl[1], Bl[2], Bl[3], Bl[4]
                E = 15 + 1024
                for eo in range(2):
                    c = 2 * g + eo
                    wvx = vx[:, :, eo * 128:(eo + 1) * 128]
                    banks = zmm(wvx, [B_q[kx]], ucols, TILES, B_uT)
                    bkh = auxbank()

                    def mmh(e, bkh=bkh, wvx=wvx):
                        ins = None
                        for kc in range(16):
                            ins = e.matmul(psum[:, bkh, 0:16], lhsT=wvx[:, kc, :], rhs=uqtail[:, kc, :], start=(kc == 0), stop=(kc == 15))
                        return ins
                    P.op("pe", mmh, reads=[B_q[kx], bc("uqtail")], writes=[B_ps[bkh]])
                    if eo == 1:
                        qrelease(kx)
                    yield
                    P.op("act", lambda e, bkh=bkh: e.activation(out=X[:, 0:15], in_=psum[:, bkh, 1:16], func=AF.Copy), reads=[B_ps[bkh]], writes=[BX], after=first_after)
                    for (bk, st, sz) in banks:
                        P.op("act", lambda e, bk=bk, st=st, sz=sz: e.activation(out=X[:, 15 + st:15 + st + sz], in_=psum[:, bk, 0:sz], func=AF.Copy), reads=[B_ps[bk]], writes=[BX])
                    yield
                    P.op("dve", lambda e, c=c: e.tensor_copy(out=tailp[:, c, :], in_=X[:, 15 + 1009:15 + 1024]), reads=[BX], writes=[bc("tailp")])
                    P.op("dve", lambda e, c=c: e.tensor_copy(out=xpsT_in[:, c, :], in_=X[:, 15 + 1024:15 + 1040]), reads=[BX], writes=[bc("xpsT_in")])
                    P.op("dve", lambda e: e.tensor_tensor(out=SA[:, 1:E], in0=X[:, 1:E], in1=X[:, 0:E - 1], op=ALU.add), reads=[BX], writes=[BSA], after=first_after)
                    yield
                    cur, curb, oth, othb = SA, BSA, SBt, BSB
                    sh = 2
                    lo = 1
                    while sh < w:
                        lo2 = lo + sh
                        P.op("dve", lambda e, cur=cur, oth=oth, lo2=lo2, sh=sh: e.tensor_tensor(out=oth[:, lo2:E], in0=cur[:, lo2:E], in1=cur[:, lo2 - sh:E - sh], op=ALU.add),
                             reads=[curb], writes=[othb], after=first_after)
                        cur, curb, oth, othb = oth, othb, cur, curb
                        lo = lo2
                        sh *= 2
                        yield
                    dcol = eo * SW
                    P.op("dve", lambda e, cur=cur, dcol=dcol: e.scalar_tensor_tensor(out=DB[:, dcol:dcol + 1024], in0=cur[:, 15:E], scalar=1.0 / w, in1=X[:, 15:E], op0=ALU.mult, op1=ALU.subtract),
                         reads=[curb, BX], writes=[BDB], after=first_after)
                    P.op("dve", lambda e, cur=cur: e.tensor_tensor(out=small15[:, 0:15], in0=cur[:, 15:30], in1=icnt_t[:, g, 0:15], op=ALU.mult), reads=[curb, bc("icnt")], writes=[bc("small15")])
                    P.op("dve", lambda e, dcol=dcol: e.tensor_tensor(out=DB[:, dcol:dcol + 15], in0=small15[:, 0:15], in1=X[:, 15:30], op=ALU.subtract), reads=[bc("small15"), BX], writes=[BDB])
                    P.op("dve", lambda e, c=c: e.tensor_tensor(out=small15[:, 0:16], in0=hsum_fm[:, c, :], in1=X[:, E:E + 16], op=ALU.add), reads=[bc("hsum_fm"), BX], writes=[bc("small15")])
                    P.op("dve", lambda e, dcol=dcol: e.scalar_tensor_tensor(out=DB[:, dcol + 1024:dcol + 1040], in0=small15[:, 0:16], scalar=1.0 / w, in1=X[:, E:E + 16], op0=ALU.mult, op1=ALU.subtract),
                         reads=[bc("small15"), BX], writes=[BDB])
                    yield
                for eo in range(2):
                    c = 2 * g + eo
                    gbanks = zmm(vg[:, :, eo * 128:(eo + 1) * 128], [B_q[kg]], ucols, TILES, B_uT)
                    if eo == 1:
                        qrelease(kg)
                    yield
                    for (bk, st, sz) in gbanks:
                        P.op("act", lambda e, bk=bk, st=st, sz=sz: e.activation(out=G[:, st:st + sz], in_=psum[:, bk, 0:sz], func=AF.Silu), reads=[B_ps[bk]], writes=[BG], after=first_after)
                    yield
                    for (st, sz) in TILES:
                        bk = bank()

                        def mmp(e, bk=bk, st=st, sz=sz, eo=eo):
                            ins = None
                            for dc in range(2):
                                ins = e.matmul(psum[:, bk, 0:sz], lhsT=wpl[:, g, dc, eo * 128:(eo + 1) * 128], rhs=DB[:, dc * SW + st:dc * SW + st + sz], start=(dc == 0), stop=(dc == 1))
                            return ins
                        P.op("pe", mmp, reads=[BDB, bc("wpl")], writes=[B_ps[bk]])
                        P.op("dve", lambda e, bk=bk, st=st, sz=sz, c=c: e.scalar_tensor_tensor(out=o_all[:, 8 + c, st:st + sz], in0=psum[:, bk, 0:sz], scalar=pscale_fm[:, c:c + 1], in1=G[:, st:st + sz], op0=ALU.mult, op1=ALU.mult),
                             reads=[B_ps[bk], BG, bc("pscale")], writes=[B_o[8 + c]], after=[B_uq])
                    yield
            return (load, gen)

        def run_pipeline(chains, depth=2, skew=9, lookahead=2):
            loaded = 0
            active = []
            nxt = 0
            while nxt < len(chains) or active:
                while loaded < len(chains) and loaded <= nxt + lookahead:
                    if not chains[loaded][0]():
                        break
                    loaded += 1
                if nxt < len(chains) and nxt < loaded and len(active) < depth and (not active or active[-1][1] >= skew):
                    active.append([chains[nxt][1](), 0])
                    nxt += 1
                assert active, "pipeline stalled"
                for a_ in list(active):
                    try:
                        next(a_[0])
                        a_[1] += 1
                    except StopIteration:
                        active.remove(a_)

        chains = []
        ci = 0
        for n in range(8):
            chains.append(make_rg(n, True, ci)); ci += 1
        for n in range(8):
            chains.append(make_rg(n, False, ci)); ci += 1
        def run_seq(chains, skew):
            loaded = 0
            nxt = 0
            cur = None
            pre = None
            steps = 0
            while True:
                while loaded < len(chains) and loaded <= nxt + 1:
                    if not chains[loaded][0]():
                        break
                    loaded += 1
                if cur is None:
                    if pre is not None:
                        cur, pre, steps = pre, None, 1
                    elif nxt < len(chains):
                        assert nxt < loaded
                        cur = chains[nxt][1]()
                        nxt += 1
                        next(cur)
                        steps = 1
                    else:
                        break
                try:
                    next(cur)
                    steps += 1
                except StopIteration:
                    cur = None
                    continue
                if steps == skew and pre is None and nxt < len(chains) and nxt < loaded:
                    pre = chains[nxt][1]()
                    nxt += 1
                    next(pre)

        def run_rg_sched(chains):
            N = len(chains)
            gens = [None] * N
            loaded = [0]

            def ensure(j, must):
                while loaded[0] <= min(j, N - 1):
                    if not chains[loaded[0]][0]():
                        break
                    loaded[0] += 1
                if must:
                    assert loaded[0] > j, "weights for chain %d could not be queued" % j

            def G(i):
                if i < 0 or i >= N:
                    return None
                if gens[i] is None:
                    ensure(i, True)
                    gens[i] = chains[i][1]()
                return gens[i]

            def st(g, n=1):
                if g is None:
                    return
                for _ in range(n):
                    try:
                        next(g)
                    except StopIteration:
                        return

            st(G(0), 2)
            for i in range(N + 1):
                A = gens[i - 1] if i >= 1 else None
                B = G(i) if i < N else None
                C = G(i + 1) if i + 1 < N else None
                ensure(i + 2, False)
                st(B, 1)
                st(C, 1)
                st(A, 2)
                st(B, 5)
                st(A, 2)
                st(B, 2)
                st(A, 2)
                st(C, 1)

        run_rg_sched(chains)
        rg_fence(0)
        rg_fence(1)
        chains = []
        for g in range(4):
            chains.append(make_pool(g, ci)); ci += 1
        run_pipeline(chains)

        s_qt = [win_tile(8), win_tile(9)]

        def q_chunk(c):
            banks = z_mm(s_qt[c // 4], c % 4, ucols, TILES, B_uT)
            for (bk, st, sz) in banks:
                P.op("act", lambda e, bk=bk, st=st, sz=sz: e.activation(out=qT[:, c, st:st + sz], in_=psum[:, bk, 0:sz], func=AF.Copy), reads=[B_ps[bk]], writes=[B_qT[c]], after=[B_regB] + B_F2 + B_kvf)

        bank_pool[0] = [0, 1, 2, 3]
        q_chunk(0)
        q_chunk(1)
        SC = 1.0 / 16.0

        ep3 = [ep[0], ep[1], carve(256)]
        smx = carve(8)
        ab8 = [0]

        def abank8():
            b_ = 4 + ab8[0]
            ab8[0] = (ab8[0] + 1) % 4
            return b_

        def attn_stages(it, h, tb):
            i3 = it % 3
            e_t = ep3[i3]
            be = bc("ep3_%d" % i3)
            mx = smx[:, i3:i3 + 1]
            sume = smx[:, 3 + i3:4 + i3]
            bm = bc("mx3_%d" % i3)
            bs_ = bc("sume3_%d" % i3)
            stt = {}

            def s1():
                bk = abank8()
                stt["bk"] = bk

                def mms(e):
                    ins = None
                    for dc in range(2):
                        ins = e.matmul(psum[:, bk, 0:256], lhsT=qT[:, 2 * h + dc, tb * 128:(tb + 1) * 128], rhs=KT[:, 2 * h + dc, :], start=(dc == 0), stop=(dc == 1))
                    return ins
                P.op("pe", mms, reads=[B_qT[2 * h], B_qT[2 * h + 1], bc("KT")], writes=[B_ps[bk]])

            def s2():
                bk = stt["bk"]
                P.op("dve", lambda e: e.reduce_max(out=mx, in_=psum[:, bk, 0:256], axis=AX.X), reads=[B_ps[bk]], writes=[bm])
                P.op("dve", lambda e: e.tensor_scalar(out=mx, in0=mx, scalar1=-SC, scalar2=None, op0=ALU.mult), reads=[bm], writes=[bm])
                P.op("act", lambda e: e.activation(out=e_t, in_=psum[:, bk, 0:256], func=AF.Exp, scale=SC, bias=mx, accum_out=sume),
                     reads=[B_ps[bk], bm], writes=[be, bs_])

            def s3():
                P.op("dve", lambda e: e.reciprocal(out=sume, in_=sume), reads=[bs_], writes=[bs_])
                P.op("dve", lambda e: e.tensor_scalar(out=e_t, in0=e_t, scalar1=sume, scalar2=None, op0=ALU.mult), reads=[be, bs_], writes=[be])
                bk2 = abank8()
                stt["bk2"] = bk2

                def trp(e):
                    ins = None
                    for mc in range(2):
                        ins = e.transpose(out=psum[:, bk2, mc * 128:(mc + 1) * 128], in_=e_t[:, mc * 128:(mc + 1) * 128], identity=ident)
                    return ins
                P.op("pe", trp, reads=[be, bc("ident")], writes=[B_ps[bk2]])

            def s4():
                bk2 = stt["bk2"]
                P.op("act", lambda e: e.activation(out=pT_all[:, :, h, tb * 128:(tb + 1) * 128], in_=psum[:, bk2, 0:256].rearrange("p (mc t) -> p mc t", mc=2), func=AF.Copy),
                     reads=[B_ps[bk2]], writes=[B_pT], after=[B_regB] + B_F2 + B_kvf)
            return (s1, s2, s3, s4)

        astg = [attn_stages(h * 8 + tb, h, tb) for h in range(4) for tb in range(8)]
        NA = len(astg)
        for r in range(NA + 3):
            for d in range(4):
                k = r - d
                if 0 <= k < NA:
                    astg[k][d]()
            if r % 8 in (1, 4) and r < 24:
                q_chunk(2 * (r // 8) + 2 + (0 if r % 8 == 1 else 1))
        bank_pool[0] = list(range(6))
        s_gx = [win_tile(10), win_tile(11)]

        P.op("dve", lambda e: e.memset(qexp.rearrange("p h dc b j -> p (h dc b j)"), 0.0), writes=[bc("qexp")], after=[B_uq])
        for h in range(4):
            for dc in range(2):
                P.op("dve", lambda e, h=h, dc=dc: e.tensor_copy(out=qexp[:, h, dc, :, :].rearrange("p b j -> p (b j)")[:, 0:256:17], in_=qT[:, 2 * h + dc, 1024:1040]),
                     reads=[B_qT[2 * h + dc], bc("qexp")], writes=[bc("qexp")])
        kstb = [xstage[0][:, i * 1024:(i + 1) * 1024].bitcast(BF16).rearrange("p (m f) -> p m f", m=2) for i in range(2)]
        vstb = [xstage[1][:, i * 1024:(i + 1) * 1024].bitcast(BF16).rearrange("p (m f) -> p m f", m=2) for i in range(2)]
        B_kb = [Buf("kb0"), Buf("kb1")]
        B_vb = [Buf("vb0"), Buf("vb1")]
        ident_bf = carve(64).bitcast(BF16)
        P.op("dve", lambda e: e.tensor_copy(out=ident_bf, in_=ident), reads=[bc("ident")], writes=[bc("ident_bf")])
        pTs_bf = carve(64).bitcast(BF16).rearrange("p (mc h b) -> p mc h b", mc=2, h=4)
        KTs = Fs[4].bitcast(BF16)[:, 0:2048].rearrange("p (c m) -> p c m", c=8)
        psum_bf = [psum[:, 6, :].bitcast(BF16), psum[:, 7, :].bitcast(BF16)]
        KTs2 = [KTs, Fs[5].bitcast(BF16)[:, 0:2048].rearrange("p (c m) -> p c m", c=8)]
        BK2 = [B_F[4], B_F[5]]
        pbf = {bk_: psum[:, bk_, :].bitcast(BF16) for bk_ in (4, 5, 6, 7)}

        def s_T(b):
            kb = kstb[b % 2]
            kt = KTs2[b % 2]
            P.dma("pool", lambda e: e.dma_start(out=kb, in_=ck[b].rearrange("(mc p) f -> p mc f", p=128)), writes=[B_kb[b % 2]],
                  after=([B_F[0], B_F[1]] if b < 2 else []))
            for mc in range(2):
                bk = (4 + mc) if b % 2 == 0 else (6 + mc)
                pb = pbf[bk]

                def trk(e, mc=mc, pb=pb):
                    ins = None
                    for c in range(8):
                        ins = e.transpose(out=pb[:, c * 128:(c + 1) * 128], in_=kb[:, mc, c * 128:(c + 1) * 128], identity=ident_bf)
                    return ins
                P.op("pe", trk, reads=[B_kb[b % 2], bc("ident_bf")], writes=[B_ps[bk]])
                if mc == 0:
                    P.op("act", lambda e, mc=mc, pb=pb: e.activation(out=kt[:, :, mc * 128:(mc + 1) * 128], in_=pb.rearrange("p (j m) -> p j m", j=8), func=AF.Copy),
                         reads=[B_ps[bk]], writes=[BK2[b % 2]])
                else:
                    P.op("dve", lambda e, mc=mc, pb=pb: e.tensor_copy(out=kt[:, :, mc * 128:(mc + 1) * 128], in_=pb.rearrange("p (j m) -> p j m", j=8)),
                         reads=[B_ps[bk]], writes=[BK2[b % 2]])

        def s_M(b):
            kt = KTs2[b % 2]
            for h in range(4):
                def mmq(e, h=h):
                    ins = None
                    for dc in range(2):
                        ins = e.matmul(psum[0:16, h, 0:256], lhsT=qexp[:, h, dc, b, :], rhs=kt[:, 2 * h + dc, :], start=(b == 0 and dc == 0), stop=(b == TS - 1 and dc == 1))
                    return ins
                P.op("pe", mmq, reads=[BK2[b % 2], bc("qexp")], writes=[B_ps[h]])

        s_T(0)
        for b in range(TS):
            if b + 1 < TS:
                s_T(b + 1)
            s_M(b)
        for h in range(4):
            mx = sm[0:16, 8:9]
            sume = sm[0:16, 9:10]
            P.op("dve", lambda e, h=h: e.reduce_max(out=mx, in_=psum[0:16, h, 0:256], axis=AX.X), reads=[B_ps[h]], writes=[bc("mxs")])
            P.op("dve", lambda e: e.tensor_scalar(out=mx, in0=mx, scalar1=-SC, scalar2=None, op0=ALU.mult), reads=[bc("mxs")], writes=[bc("mxs")])
            P.op("act", lambda e, h=h: e.activation(out=ps_s[0:16, h, :], in_=psum[0:16, h, 0:256], func=AF.Exp, scale=SC, bias=mx, accum_out=sume), reads=[B_ps[h], bc("mxs")], writes=[bc("ps_s"), bc("sumes")], after=[B_uq])
            P.op("dve", lambda e: e.reciprocal(out=sume, in_=sume), reads=[bc("sumes")], writes=[bc("sumes")])
            P.op("dve", lambda e, h=h: e.tensor_scalar(out=ps_s[0:16, h, :], in0=ps_s[0:16, h, :], scalar1=sume, scalar2=None, op0=ALU.mult), reads=[bc("ps_s"), bc("sumes")], writes=[bc("ps_s")])
        bk = auxbank()

        def trps(e, bk=bk):
            ins = None
            for mc in range(2):
                for h in range(4):
                    ins = e.transpose(out=psum[:, bk, (mc * 4 + h) * 16:(mc * 4 + h) * 16 + 16], in_=ps_s[0:16, h, mc * 128:(mc + 1) * 128], identity=ident[0:16, 0:16])
            return ins
        P.op("pe", trps, reads=[bc("ps_s"), bc("ident")], writes=[B_ps[bk]])
        P.op("dve", lambda e, bk=bk: e.tensor_copy(out=pTs.rearrange("p mc h b -> p (mc h b)"), in_=psum[:, bk, 0:128]), reads=[B_ps[bk]], writes=[bc("pTs")])
        P.op("dve", lambda e: e.tensor_copy(out=pTs_bf.rearrange("p mc h b -> p (mc h b)"), in_=pTs.rearrange("p mc h b -> p (mc h b)")), reads=[bc("pTs")], writes=[bc("pTs_bf")])
        G_s = dg[0].rearrange("p (c b) -> p c b", c=8)

        def gx_chunk(c):
            t, j = c // 4, c % 4
            h = c // 2
            gbanks = z_mm(s_gx[t], j, ucols, TILES, B_uT)
            G = Fs[5]
            for (bk, st, sz) in gbanks:
                P.op("act", lambda e, bk=bk, st=st, sz=sz: e.activation(out=G[:, st:st + sz], in_=psum[:, bk, 0:sz], func=AF.Silu), reads=[B_ps[bk]], writes=[B_F[5]])
            P.op("dve", lambda e: e.tensor_copy(out=G_s[:, c, :], in_=G[:, 1024:1040]), reads=[B_F[5]], writes=[bc("G_s")])
            for tt in range(2):
                bk = bank()

                def mmpv(e, bk=bk, tt=tt):
                    ins = None
                    for mc in range(2):
                        ins = e.matmul(psum[:, bk, :], lhsT=Vb[:, mc, c * 128:(c + 1) * 128], rhs=pT_all[:, mc, h, tt * 512:(tt + 1) * 512], start=(mc == 0), stop=(mc == 1))
                    return ins
                P.op("pe", mmpv, reads=[bc("Vb"), B_pT], writes=[B_ps[bk]])
                P.op("dve", lambda e, bk=bk, tt=tt: e.tensor_tensor(out=o_all[:, 16 + c, tt * 512:(tt + 1) * 512], in0=psum[:, bk, :], in1=G[:, tt * 512:(tt + 1) * 512], op=ALU.mult),
                     reads=[B_ps[bk], B_F[5]], writes=[B_o[16 + c]], after=[B_uq, bc("qexp"), bc("ps_s")])

        bko = auxbank()
        vst4 = [vstb[0], vstb[1], kstb[0], kstb[1]]
        B_v4 = [B_vb[0], B_vb[1], B_kb[0], B_kb[1]]
        for b in range(TS):
            vb = vst4[b % 4]
            P.dma("pool", lambda e, vb=vb, b=b: e.dma_start(out=vb, in_=cv[b].rearrange("(mc p) f -> p mc f", p=128)), writes=[B_v4[b % 4]],
                  after=([B_F[2], B_F[3]] if b < 2 else []))

            def mmv(e, b=b, bko=bko, vb=vb):
                ins = None
                for c in range(8):
                    for mc in range(2):
                        ins = e.matmul(psum[:, bko, c * 16 + b:c * 16 + b + 1], lhsT=vb[:, mc, c * 128:(c + 1) * 128], rhs=pTs_bf[:, mc, c // 2, b:b + 1], start=(mc == 0), stop=(mc == 1))
                return ins
            P.op("pe", mmv, reads=[B_v4[b % 4], bc("pTs_bf")], writes=[B_ps[bko]])
            if b % 2 == 1:
                gx_chunk(b // 2)
        P.op("dve", lambda e: e.memset(small15[:, 0:1], 0.0), reads=B_kb + B_vb, writes=[bc("small15")] + B_F[0:4])
        P.op("dve", lambda e, bko=bko: e.tensor_copy(out=os_fm.rearrange("p c b -> p (c b)"), in_=psum[:, bko, 0:128]), reads=[B_ps[bko]], writes=[bc("os_fm")])
        P.op("dve", lambda e: e.tensor_tensor(out=o_all[:, 16:24, 1024:1040], in0=os_fm, in1=G_s, op=ALU.mult), reads=[bc("os_fm"), bc("G_s")], writes=B_o[16:24],
             after=[B_uq, bc("qexp"), bc("ps_s")])

        outT, outT2, outT3, outT4, outT5, outT6 = [KT_w[:, 128 * i:128 * (i + 1)] for i in range(6)]

        def fm_out(src2d, ncols, stage, emit_dmas, rbufs):
            bk = auxbank()
            P.op("pe", lambda e, bk=bk: e.transpose(out=psum[0:ncols, bk, 0:128], in_=src2d, identity=ident), reads=rbufs + [bc("ident")], writes=[B_ps[bk]])
            sb_ = Buf()
            P.op("dve", lambda e, bk=bk: e.tensor_copy(out=stage[0:ncols, :], in_=psum[0:ncols, bk, 0:128]), reads=[B_ps[bk]], writes=[sb_], after=[bc("KT")])
            emit_dmas(sb_)

        out_dma(conv_s[:, 0:2, :], st_conv[:, 1:3, :], [])
        out_dma(pool_s[:, 0:14, :], st_pool[:, 1:15, :], [])
        fm_out(hlast, 8, outT, lambda sb_: out_dma(h_p.rearrange("(c k) -> c k", k=128), outT[0:8, :], [sb_]), [bc("hlast")])
        fm_out(tailx.rearrange("p c r -> p (c r)"), 24, outT2,
               lambda sb_: [out_dma(conv_p[:, c * 128:(c + 1) * 128], outT2[3 * c:3 * c + 3, :], [sb_]) for c in range(8)], [bc("tailx")])
        fm_out(tailp.rearrange("p c r -> p (c r)"), 120, outT3,
               lambda sb_: [out_dma(pool_p[:, c * 128:(c + 1) * 128], outT3[15 * c:15 * c + 15, :], [sb_]) for c in range(8)], [bc("tailp")])
        fm_out(hsT_in.rearrange("p c b -> p (c b)"), 128, outT4,
               lambda sb_: [out_dma(h_s[:, c * 128:(c + 1) * 128], outT4[16 * c:16 * c + 16, :], [sb_]) for c in range(8)], [bc("hsT_in")])
        fm_out(xsT_in.rearrange("p c b -> p (c b)"), 128, outT5,
               lambda sb_: [out_dma(conv_s[:, 2, c * 128:(c + 1) * 128], outT5[16 * c:16 * c + 16, :], [sb_]) for c in range(8)], [bc("xsT_in")])
        fm_out(xpsT_in.rearrange("p c b -> p (c b)"), 128, outT6,
               lambda sb_: [out_dma(pool_s[:, 14, c * 128:(c + 1) * 128], outT6[16 * c:16 * c + 16, :], [sb_]) for c in range(8)], [bc("xpsT_in")])

        B_merged = Buf("merged")
        Wsl = [Fw[:, 3 * SW + k * 1536:3 * SW + (k + 1) * 1536].bitcast(BF16).rearrange("p (kc n) -> p kc n", kc=24) for k in range(2)]
        B_W2 = [Buf("W2a"), Buf("W2b")]

        def load_phaseB(f):
            s_ = f % 2
            for i in range(3):
                dst = wsl[s_][:, i * 2048:(i + 1) * 2048].rearrange("p (kc n) -> p kc n", kc=16)
                src = w_in[:, 6144 + i * 2048 + f * 128:6144 + i * 2048 + (f + 1) * 128].rearrange("(kc p) n -> p kc n", p=128)
                P.dma("pool", lambda e, dst=dst, src=src: e.dma_start(out=dst, in_=src), writes=[B_ws[s_]])
            srcw = w_branch[:, f * 128:(f + 1) * 128].rearrange("(kc p) n -> p kc n", p=128)
            P.dma("pool", lambda e, s_=s_, srcw=srcw: e.dma_start(out=Wsl[s_], in_=srcw), writes=[B_W2[s_]], after=[B_F[3], B_F[4], B_F[5]])

        load_phaseB(0)
        for f in range(16):
            if f + 1 < 16:
                load_phaseB(f + 1)
            s_ = f % 2
            gv = wsl[s_][:, 0:6144].rearrange("p (i kc n) -> p i kc n", i=3, kc=16)
            bv = Wsl[s_]
            MACC, BMACC = Fs[2], B_F[2]
            for i in range(3):
                GS, BGS = (Fs[0], B_F[0]) if i != 1 else (Fs[1], B_F[1])
                for (st, sz) in TILES:
                    bk = bank()

                    def mmg(e, bk=bk, st=st, sz=sz, i=i, gv=gv):
                        ins = None
                        for kc in range(16):
                            ins = e.matmul(psum[:, bk, 0:sz], lhsT=gv[:, i, kc, :], rhs=uT[:, kc, st:st + sz], start=(kc == 0), stop=(kc == 15))
                        return ins
                    P.op("pe", mmg, reads=[B_ws[s_], B_uT], writes=[B_ps[bk]])
                    P.op("act", lambda e, bk=bk, st=st, sz=sz, GS=GS: e.activation(out=GS[:, st:st + sz], in_=psum[:, bk, 0:sz], func=AF.Sigmoid), reads=[B_ps[bk]], writes=[BGS])
                for (st, sz) in TILES:
                    bk = bank()

                    def mmy(e, bk=bk, st=st, sz=sz, i=i, bv=bv):
                        ins = None
                        for kc in range(8):
                            ins = e.matmul(psum[:, bk, 0:sz], lhsT=bv[:, i * 8 + kc, :], rhs=o_all[:, i * 8 + kc, st:st + sz], start=(kc == 0), stop=(kc == 7))
                        return ins
                    P.op("pe", mmy, reads=[B_W2[s_]] + B_o[i * 8:(i + 1) * 8], writes=[B_ps[bk]])
                    if i == 0:
                        P.op("dve", lambda e, bk=bk, st=st, sz=sz, GS=GS: e.tensor_tensor(out=MACC[:, st:st + sz], in0=psum[:, bk, 0:sz], in1=GS[:, st:st + sz], op=ALU.mult), reads=[B_ps[bk], BGS], writes=[BMACC])
                    elif i == 1:
                        P.op("dve", lambda e, bk=bk, st=st, sz=sz, GS=GS: e.tensor_tensor(out=GS[:, st:st + sz], in0=psum[:, bk, 0:sz], in1=GS[:, st:st + sz], op=ALU.mult), reads=[B_ps[bk], BGS], writes=[BGS])
                        P.op("dve", lambda e, st=st, sz=sz, GS=GS: e.tensor_tensor(out=MACC[:, st:st + sz], in0=MACC[:, st:st + sz], in1=GS[:, st:st + sz], op=ALU.add), reads=[BMACC, BGS], writes=[BMACC], safe=True)
                    else:
                        P.op("dve", lambda e, bk=bk, st=st, sz=sz, GS=GS: e.tensor_tensor(out=GS[:, st:st + sz], in0=psum[:, bk, 0:sz], in1=GS[:, st:st + sz], op=ALU.mult), reads=[B_ps[bk], BGS], writes=[BGS])
                        P.op("dve", lambda e, st=st, sz=sz, f=f, GS=GS: e.tensor_tensor(out=merged[:, f, st:st + sz], in0=MACC[:, st:st + sz], in1=GS[:, st:st + sz], op=ALU.add), reads=[BMACC, BGS], writes=[B_merged],
                             after=[B_pT] + B_qT, safe=True)

        wo = []
        for ct in range(4):
            src = w_out[:, ct * 512:(ct + 1) * 512].rearrange("(kc p) n -> p kc n", p=128)
            if ct < 2:
                s = wtile(lambda slot, src=src: [(slot.rearrange("p (kc n) -> p kc n", kc=16), src)])
                wo.append((wview(s), B_ws[s]))
            else:
                v = uT_w.bitcast(BF16)[:, (ct - 2) * 8192:(ct - 1) * 8192].rearrange("p (kc n) -> p kc n", kc=16)
                bwo = Buf("wo%d" % ct)
                P.dma("pool", lambda e, v=v, src=src: e.dma_start(out=v, in_=src), writes=[bwo], after=[B_uT])
                wo.append((v, bwo))
        load("sp", gpost_bc, g_post.partition_broadcast(128), [bc("gpost")], after=B_o)
        blocks = [(tb * 128, 128, xp[tb * 128:(tb + 1) * 128, :], y_p[tb * 128:(tb + 1) * 128, :]) for tb in range(8)] + [(1024, 16, xs, y_s)]
        osb_l = [osb, regC_w[:, 3072:5120]]
        B_osb_l = [[B_F[4], B_F[5]], [Buf("osb2")]]
        for bi, (t0, n, xsrc, ydst) in enumerate(blocks):
            osb = osb_l[bi % 2]
            B_osb = B_osb_l[bi % 2]
            ssqc = ssq[:, 4 * (bi % 2):4 * (bi % 2) + 4]
            ssqb = bc("ssqC%d" % (bi % 2))
            xi = bi % 2
            xst = xstage[xi]
            bxs = [B_F[2 * xi], B_F[2 * xi + 1]]
            load("sp", xst[0:n, :], xsrc, bxs, after=B_W2)
            for ct in range(4):
                bk = bank()
                wv, wb = wo[ct]

                def mmo(e, bk=bk, wv=wv, t0=t0, n=n):
                    ins = None
                    for kc in range(16):
                        ins = e.matmul(psum[0:n, bk, :], lhsT=merged[:, kc, t0:t0 + n], rhs=wv[:, kc, :], start=(kc == 0), stop=(kc == 15))
                    return ins
                P.op("pe", mmo, reads=[B_merged, wb], writes=[B_ps[bk]])
                P.op("act", lambda e, bk=bk, ct=ct, n=n, osb=osb: e.activation(out=osb[0:n, ct * 512:(ct + 1) * 512], in_=psum[0:n, bk, :], func=AF.Copy), reads=[B_ps[bk]], writes=B_osb, after=B_W2 + B_o)
                P.op("act", lambda e, bk=bk, ct=ct, n=n, ssqc=ssqc: e.activation(out=junk[0:n, :], in_=psum[0:n, bk, :], func=AF.Square, accum_out=ssqc[0:n, ct:ct + 1]), reads=[B_ps[bk]], writes=[bc("junk"), ssqb])
            rstd = sm[:, 10 + (bi % 2):11 + (bi % 2)]
            rb = bc("rstdC%d" % (bi % 2))
            P.op("dve", lambda e, n=n, rstd=rstd, ssqc=ssqc: e.tensor_reduce(out=rstd[0:n], in_=ssqc[0:n, 0:4], axis=AX.X, op=ALU.add), reads=[ssqb], writes=[rb])
            P.op("dve", lambda e, n=n, rstd=rstd: e.tensor_scalar(out=rstd[0:n], in0=rstd[0:n], scalar1=1.0 / D, scalar2=EPS, op0=ALU.mult, op1=ALU.add), reads=[rb], writes=[rb])
            P.op("act", lambda e, n=n, rstd=rstd: e.activation(out=rstd[0:n], in_=rstd[0:n], func=AF.Sqrt), reads=[rb], writes=[rb])
            P.op("dve", lambda e, n=n, rstd=rstd: e.reciprocal(out=rstd[0:n], in_=rstd[0:n]), reads=[rb], writes=[rb])
            P.op("dve", lambda e, n=n, osb=osb, rstd=rstd: e.scalar_tensor_tensor(out=osb[0:n, :], in0=osb[0:n, :], scalar=rstd[0:n], in1=gpost_bc[0:n, :], op0=ALU.mult, op1=ALU.mult), reads=B_osb + [rb, bc("gpost")], writes=B_osb)
            P.op("dve", lambda e, n=n, xst=xst, osb=osb: e.tensor_tensor(out=osb[0:n, :], in0=osb[0:n, :], in1=xst[0:n, :], op=ALU.add), reads=B_osb + bxs, writes=B_osb, safe=True)
            out_dma(ydst, osb[0:n, :], B_osb)

        dump(8, xsT_in.rearrange("p c b -> p (c b)"), [bc("xsT_in")], 128)
        dump(9, xpsT_in.rearrange("p c b -> p (c b)"), [bc("xpsT_in")], 128)
        dump(10, tailp.rearrange("p c r -> p (c r)"), [bc("tailp")], 120)
        P.wait_all("sp", out_bufs)
        print('arena used', off[0], 'of', NW)
        P.emit()
    return nc


_NC_CACHE = {}


def kernel(x_prompt, x_sample, mem_prompt, state_rglru_h, state_conv, state_pool, cache_mem_k, cache_mem_v,
           g_pre, w_in, conv_w, conv_b, w_rg_a, b_rg_a, w_rg_x, b_rg_x, lru_lambda, w_pool, pool_scale, g_mem,
           w_kv, w_branch, w_out, g_post):
    f = lambda a: np.ascontiguousarray(np.asarray(a, dtype=np.float32))
    x_prompt = f(x_prompt); x_sample = f(x_sample); mem_prompt = f(mem_prompt)
    state_rglru_h = f(state_rglru_h); state_conv = f(state_conv); state_pool = f(state_pool)
    cache_mem_k = f(cache_mem_k); cache_mem_v = f(cache_mem_v)
    shared = {
        "g_pre": f(g_pre)[0], "w_in": f(w_in)[0], "conv_w": f(conv_w)[0], "conv_b": f(conv_b)[0],
        "w_rg_a": f(w_rg_a)[0], "b_rg_a": f(b_rg_a)[0], "w_rg_x": f(w_rg_x)[0], "b_rg_x": f(b_rg_x)[0],
        "lam": f(lru_lambda)[0], "w_pool": f(w_pool)[0], "pool_scale": f(pool_scale)[0], "g_mem": f(g_mem)[0],
        "w_kv": f(w_kv)[0], "w_branch": f(w_branch)[0], "w_out": f(w_out)[0], "g_post": f(g_post)[0],
    }
    in_maps = []
    for c in range(8):
        b, half = c // 2, c % 2
        m = dict(shared)
        m["xp"] = x_prompt[b, half * 1024:(half + 1) * 1024]
        m["xq"] = x_prompt[b, 0:1024] if half == 1 else np.zeros((1024, 2048), np.float32)
        m["xs"] = x_sample[16 * c:16 * c + 16, 0]
        m["mem"] = mem_prompt[b]
        m["st_h"] = state_rglru_h[0, 16 * c:16 * c + 16]
        m["st_conv"] = state_conv[0, 16 * c:16 * c + 16]
        m["st_pool"] = state_pool[0, 16 * c:16 * c + 16]
        m["ck"] = cache_mem_k[0, 16 * c:16 * c + 16].reshape(16, 256, 1024)
        m["cv"] = cache_mem_v[0, 16 * c:16 * c + 16].reshape(16, 256, 1024)
        m["flag"] = np.full((128, 1), float(half), np.float32)
        ic = np.zeros((128, 4, 16), np.float32)
        for g, w in enumerate(POOLW):
            for t in range(16):
                pos = half * 1024 + t
                ic[:, g, t] = 1.0 / min(pos + 1, w)
        m["icnt"] = ic
        in_maps.append({k: np.ascontiguousarray(v) for k, v in m.items()})
    if "nc" not in _NC_CACHE:
        _NC_CACHE["nc"] = build_program()
    nc = _NC_CACHE["nc"]
    res = run_bass_kernel_spmd(nc, in_maps, core_ids=list(range(8)))
    R = res.results
    y_prompt = np.zeros((4, 2048, 2048), np.float32)
    y_sample = np.zeros((128, 1, 2048), np.float32)
    new_h_p = np.zeros((1, 4, 1024), np.float32); new_conv_p = np.zeros((1, 4, 3, 1024), np.float32)
    new_pool_p = np.zeros((1, 4, 15, 1024), np.float32)
    mk = np.zeros((1, 4, 256, 4, 256), np.float32); mv = np.zeros((1, 4, 256, 4, 256), np.float32)
    new_h_s = np.zeros((1, 128, 1024), np.float32); new_conv_s = np.zeros((1, 128, 3, 1024), np.float32)
    new_pool_s = np.zeros((1, 128, 15, 1024), np.float32)
    for c in range(8):
        b, half = c // 2, c % 2
        r = R[c]
        y_prompt[b, half * 1024:(half + 1) * 1024] = r["y_p"]
        y_sample[16 * c:16 * c + 16, 0] = r["y_s"]
        new_h_s[0, 16 * c:16 * c + 16] = r["h_s"]
        new_conv_s[0, 16 * c:16 * c + 16] = r["conv_s"]
        new_pool_s[0, 16 * c:16 * c + 16] = r["pool_s"]
        if half == 1:
            new_h_p[0, b] = r["h_p"]
            new_conv_p[0, b] = r["conv_p"]
            new_pool_p[0, b] = r["pool_p"]
        else:
            mk[0, b] = r["mk"].reshape(256, 4, 256)
            mv[0, b] = r["mv"].reshape(256, 4, 256)
    return (y_prompt, y_sample, new_h_p, new_conv_p, new_pool_p, mk, mv, new_h_s, new_conv_s, new_pool_s)
```
